# Optimizing a Trainium2 kernel written in Bass

```python
import math
import jax, jax.numpy as jnp
from jax import lax
import numpy as np

D_MODEL = 1024
BATCH = 1
SEQ = 16384
DEPTH = 2
DEC_BATCH = 32
DEC_SEQ = 64
PAST_LEN = 1024

CHUNK = 64
Q_BLOCK = 128
N_MEM = 256
EPS = 1e-6
ROPE_THETA = 10000.0
NEG = -1e30
F32 = jnp.float32

MLA_HEADS = 8
MLA_NOPE = 128
MLA_ROPE = 64
MLA_V = 128
MLA_Q_RANK = 384
MLA_KV_RANK = 256
MLA_IN = MLA_Q_RANK + MLA_KV_RANK + MLA_ROPE
MLA_OUT = MLA_HEADS * MLA_V

HG_HEADS = 8
HG_DIM = 128
HG_WIDTH = HG_HEADS * HG_DIM

X_HEADS = 4
X_DIM = 128
X_WIDTH = X_HEADS * X_DIM

D_FF = 4 * D_MODEL

N_MLA_LAYERS = (DEPTH + 1) // 2
N_HGRN_LAYERS = DEPTH // 2
MIX_OUT = MLA_OUT + X_WIDTH

kernel_name = 'hybrid_mla_hgrn2_stream_step'


def rmsnorm(x, g):
    xf = x.astype(F32)
    y = xf * lax.rsqrt(jnp.mean(xf * xf, axis=-1, keepdims=True) + EPS)
    return (y * g.astype(F32)).astype(x.dtype)


def rope(x, pos):
    half = x.shape[-1] // 2
    inv = jnp.power(ROPE_THETA, -jnp.arange(half, dtype=F32) / half)
    ang = pos.astype(F32)[:, None] * inv[None, :]
    ang = ang.reshape((1, ang.shape[0]) + (1,) * (x.ndim - 3) + (half,))
    cos, sin = jnp.cos(ang), jnp.sin(ang)
    xf = x.astype(F32)
    x1, x2 = xf[..., :half], xf[..., half:]
    return jnp.concatenate([x1 * cos - x2 * sin, x2 * cos + x1 * sin], axis=-1).astype(x.dtype)


def mla_attention(q_lat, q_rope, lat_all, krope_all, q_pos):
    B, T, H, C = q_lat.shape
    L = lat_all.shape[1]
    k_chunk = jnp.arange(L) // CHUNK
    scale = (MLA_NOPE + MLA_ROPE) ** -0.5

    def block(args):
        ql, qr, qp = args
        s = (jnp.einsum('bqhc,bkc->bhqk', ql, lat_all)
             + jnp.einsum('bqhr,bkr->bhqk', qr, krope_all)).astype(F32) * scale
        mask = k_chunk[None, :] <= (qp // CHUNK)[:, None]
        p = jax.nn.softmax(jnp.where(mask, s, NEG), axis=-1).astype(lat_all.dtype)
        return jnp.einsum('bhqk,bkc->bqhc', p, lat_all)

    qb = Q_BLOCK if T % Q_BLOCK == 0 else T
    nb = T // qb
    if nb == 1:
        return block((q_lat, q_rope, q_pos))
    split = lambda a: jnp.moveaxis(a.reshape((B, nb, qb) + a.shape[2:]), 1, 0)
    out = lax.map(block, (split(q_lat), split(q_rope), q_pos.reshape(nb, qb)))
    return jnp.moveaxis(out, 0, 1).reshape(B, T, H, C)


def memory_attention(xq, mem_k, mem_v):
    s = jnp.einsum('bthd,bmhd->bhtm', xq, mem_k).astype(F32) * (X_DIM ** -0.5)
    p = jax.nn.softmax(s, axis=-1).astype(mem_v.dtype)
    return jnp.einsum('bhtm,bmhd->bthd', p, mem_v)


def gated_linear_recurrence(q, k, v, logf, s0, chunk):
    B, T, H, K = q.shape
    V = v.shape[-1]
    n = T // chunk
    blk = lambda a: a.astype(F32).reshape((B, n, chunk) + a.shape[2:])
    q, k, v, logf = blk(q), blk(k), blk(v), blk(logf)
    b = jnp.cumsum(logf, axis=2)
    ref = b[:, :, chunk // 2][:, :, None]
    b_last = b[:, :, -1]
    causal = jnp.tril(jnp.ones((chunk, chunk), dtype=bool))
    a = jnp.einsum('bnthk,bnshk->bnhts', q * jnp.exp(b - ref), k * jnp.exp(ref - b))
    a = jnp.where(causal, a, 0.0)
    o_intra = jnp.einsum('bnhts,bnshv->bnthv', a, v)
    u = jnp.einsum('bnshk,bnshv->bnhkv', k * jnp.exp(b_last[:, :, None] - b), v)
    decay = jnp.exp(b_last)

    def step(s, xs):
        d, uc = xs
        return d[..., None] * s + uc, s

    s_fin, s_in = lax.scan(step, s0.astype(F32),
                           (jnp.moveaxis(decay, 1, 0), jnp.moveaxis(u, 1, 0)))
    o_inter = jnp.einsum('bnthk,nbhkv->bnthv', q * jnp.exp(b), s_in)
    o = (o_intra + o_inter).reshape(B, T, H, V)
    return o, s_fin


def hgrn2_mixer(proj, lb, s0, o_norm_g):
    B, T, _ = proj.shape
    q, f, i, g = jnp.split(proj, 4, axis=-1)
    heads = lambda a: a.reshape(B, T, HG_HEADS, HG_DIM)
    ff = f.astype(F32)
    lbf = lb.astype(F32)
    forget = lbf + (1.0 - lbf) * jax.nn.sigmoid(ff)
    logf = jnp.log(forget)
    k = (1.0 - lbf) * jax.nn.sigmoid(-ff)
    chunk = CHUNK if T % CHUNK == 0 else T
    o, s_fin = gated_linear_recurrence(heads(jax.nn.silu(q)), heads(k), heads(i), heads(logf), s0, chunk)
    o = rmsnorm(o.astype(proj.dtype), o_norm_g) * heads(jax.nn.silu(g))
    return o.reshape(B, T, HG_WIDTH), s_fin.astype(s0.dtype)


def trunk(x, pos, past_lat, past_krope, hgrn_s0, mem_k, mem_v,
          ln_mix_pre, ln_mix_post, ln_ffn_pre, ln_ffn_post,
          mla_w_in, mla_q_norm, mla_kv_norm, mla_w_uq, mla_w_uk, mla_w_uv, mla_w_out,
          hgrn_w_in, hgrn_lb, hgrn_o_norm, hgrn_w_out, w_ffn_up, w_ffn_down):
    B, T, _ = x.shape
    lb_soft = jax.nn.softmax(hgrn_lb.astype(F32), axis=0)
    lb_all = jnp.cumsum(lb_soft, axis=0) - lb_soft[0]
    new_lat, new_krope, new_state = [], [], []
    for l in range(DEPTH):
        j = l // 2
        h = rmsnorm(x, ln_mix_pre[l])
        if l % 2 == 0:
            proj = h @ mla_w_in[j]
            c_q = proj[..., :MLA_Q_RANK]
            c_kv = proj[..., MLA_Q_RANK:MLA_Q_RANK + MLA_KV_RANK]
            k_r = proj[..., MLA_Q_RANK + MLA_KV_RANK:MLA_IN]
            xq = proj[..., MLA_IN:]
            q = (rmsnorm(c_q, mla_q_norm[j]) @ mla_w_uq[j]).reshape(B, T, MLA_HEADS, MLA_NOPE + MLA_ROPE)
            q_nope = q[..., :MLA_NOPE]
            q_rope = rope(q[..., MLA_NOPE:], pos)
            lat = rmsnorm(c_kv, mla_kv_norm[j])
            krope = rope(k_r, pos)
            q_lat = jnp.einsum('bthd,chd->bthc', q_nope, mla_w_uk[j])
            if past_lat is None:
                lat_all, krope_all = lat, krope
            else:
                lat_all = jnp.concatenate([past_lat[j], lat], axis=1)
                krope_all = jnp.concatenate([past_krope[j], krope], axis=1)
            o_lat = mla_attention(q_lat, q_rope, lat_all, krope_all, pos)
            mix = jnp.einsum('bthc,chv->bthv', o_lat, mla_w_uv[j]).reshape(B, T, MLA_OUT)
            w_out = mla_w_out[j]
            new_lat.append(lat)
            new_krope.append(krope)
        else:
            proj = h @ hgrn_w_in[j]
            mix, s_fin = hgrn2_mixer(proj[..., :4 * HG_WIDTH], lb_all[l], hgrn_s0[j], hgrn_o_norm[j])
            xq = proj[..., 4 * HG_WIDTH:]
            w_out = hgrn_w_out[j]
            new_state.append(s_fin)
        cross = memory_attention(xq.reshape(B, T, X_HEADS, X_DIM), mem_k[l], mem_v[l]).reshape(B, T, X_WIDTH)
        x = x + rmsnorm(jnp.concatenate([mix, cross], axis=-1) @ w_out, ln_mix_post[l])
        h = rmsnorm(x, ln_ffn_pre[l])
        x = x + rmsnorm(jnp.square(jax.nn.relu(h @ w_ffn_up[l])) @ w_ffn_down[l], ln_ffn_post[l])
    return x, jnp.stack(new_lat), jnp.stack(new_krope), jnp.stack(new_state)


def setup_inputs(seed: int = 0) -> dict:
    key = jax.random.key(seed)
    ks = jax.random.split(key, 32)
    nrm = lambda k, shape, scale: jax.random.normal(k, shape, F32) * scale
    gain = lambda k, shape: 1.0 + 0.05 * jax.random.normal(k, shape, F32)
    return {
        'x_prompt': nrm(ks[0], (BATCH, SEQ, D_MODEL), 1.0),
        'x_sample': nrm(ks[1], (DEC_BATCH, DEC_SEQ, D_MODEL), 1.0),
        'cache_mla_latent': nrm(ks[2], (N_MLA_LAYERS, DEC_BATCH, PAST_LEN, MLA_KV_RANK), 1.0),
        'cache_mla_krope': nrm(ks[3], (N_MLA_LAYERS, DEC_BATCH, PAST_LEN, MLA_ROPE), 1.0),
        'cache_hgrn_state': nrm(ks[4], (N_HGRN_LAYERS, DEC_BATCH, HG_HEADS, HG_DIM, HG_DIM), 0.3),
        'cache_mem_k': nrm(ks[5], (DEPTH, DEC_BATCH, N_MEM, X_HEADS, X_DIM), 1.0),
        'cache_mem_v': nrm(ks[6], (DEPTH, DEC_BATCH, N_MEM, X_HEADS, X_DIM), 1.0),
        'mem_prompt': nrm(ks[7], (BATCH, N_MEM, D_MODEL), 1.0),
        'ln_mix_pre': gain(ks[8], (DEPTH, D_MODEL)),
        'ln_mix_post': gain(ks[9], (DEPTH, D_MODEL)),
        'ln_ffn_pre': gain(ks[10], (DEPTH, D_MODEL)),
        'ln_ffn_post': gain(ks[11], (DEPTH, D_MODEL)),
        'mem_norm': gain(ks[12], (DEPTH, D_MODEL)),
        'w_mem_kv': nrm(ks[13], (DEPTH, D_MODEL, 2 * X_WIDTH), D_MODEL ** -0.5),
        'mla_w_in': nrm(ks[14], (N_MLA_LAYERS, D_MODEL, MLA_IN + X_WIDTH), D_MODEL ** -0.5),
        'mla_q_norm': gain(ks[15], (N_MLA_LAYERS, MLA_Q_RANK)),
        'mla_kv_norm': gain(ks[16], (N_MLA_LAYERS, MLA_KV_RANK)),
        'mla_w_uq': nrm(ks[17], (N_MLA_LAYERS, MLA_Q_RANK, MLA_HEADS * (MLA_NOPE + MLA_ROPE)), MLA_Q_RANK ** -0.5),
        'mla_w_uk': nrm(ks[18], (N_MLA_LAYERS, MLA_KV_RANK, MLA_HEADS, MLA_NOPE), MLA_KV_RANK ** -0.5),
        'mla_w_uv': nrm(ks[19], (N_MLA_LAYERS, MLA_KV_RANK, MLA_HEADS, MLA_V), MLA_KV_RANK ** -0.5),
        'mla_w_out': nrm(ks[20], (N_MLA_LAYERS, MIX_OUT, D_MODEL), MIX_OUT ** -0.5),
        'hgrn_w_in': nrm(ks[21], (N_HGRN_LAYERS, D_MODEL, 4 * HG_WIDTH + X_WIDTH), D_MODEL ** -0.5),
        'hgrn_lb': nrm(ks[22], (DEPTH, HG_WIDTH), 0.1),
        'hgrn_o_norm': gain(ks[23], (N_HGRN_LAYERS, HG_DIM)),
        'hgrn_w_out': nrm(ks[24], (N_HGRN_LAYERS, MIX_OUT, D_MODEL), MIX_OUT ** -0.5),
        'w_ffn_up': nrm(ks[25], (DEPTH, D_MODEL, D_FF), D_MODEL ** -0.5),
        'w_ffn_down': nrm(ks[26], (DEPTH, D_FF, D_MODEL), D_FF ** -0.5),
    }


def reference(x_prompt, x_sample, cache_mla_latent, cache_mla_krope, cache_hgrn_state,
              cache_mem_k, cache_mem_v, mem_prompt,
              ln_mix_pre, ln_mix_post, ln_ffn_pre, ln_ffn_post, mem_norm, w_mem_kv,
              mla_w_in, mla_q_norm, mla_kv_norm, mla_w_uq, mla_w_uk, mla_w_uv, mla_w_out,
              hgrn_w_in, hgrn_lb, hgrn_o_norm, hgrn_w_out, w_ffn_up, w_ffn_down):
    Bp, Tp, _ = x_prompt.shape
    Bs, Ts, _ = x_sample.shape
    past = cache_mla_latent.shape[2]
    mem_n = rmsnorm(mem_prompt[None], mem_norm[:, None, None, :])
    kv = jnp.einsum('lbmd,lde->lbme', mem_n, w_mem_kv).reshape(DEPTH, Bp, N_MEM, 2, X_HEADS, X_DIM)
    mem_k_p, mem_v_p = kv[:, :, :, 0], kv[:, :, :, 1]
    weights = (ln_mix_pre, ln_mix_post, ln_ffn_pre, ln_ffn_post,
               mla_w_in, mla_q_norm, mla_kv_norm, mla_w_uq, mla_w_uk, mla_w_uv, mla_w_out,
               hgrn_w_in, hgrn_lb, hgrn_o_norm, hgrn_w_out, w_ffn_up, w_ffn_down)
    s0_p = jnp.zeros((N_HGRN_LAYERS, Bp, HG_HEADS, HG_DIM, HG_DIM), x_prompt.dtype)
    y_p, lat_p, kr_p, st_p = trunk(x_prompt, jnp.arange(Tp), None, None, s0_p,
                                   mem_k_p, mem_v_p, *weights)
    y_s, lat_s, kr_s, st_s = trunk(x_sample, past + jnp.arange(Ts), cache_mla_latent, cache_mla_krope,
                                   cache_hgrn_state, cache_mem_k, cache_mem_v, *weights)
    return (y_p, y_s, lat_p, kr_p, st_p, mem_k_p, mem_v_p, lat_s, kr_s, st_s)
```

```python
import numpy as np
import concourse.bass as bass
import concourse.mybir as mybir
from concourse.bass_utils import run_bass_kernel_spmd

F32 = mybir.dt.float32
BF16 = mybir.dt.bfloat16
AF = mybir.ActivationFunctionType
ALU = mybir.AluOpType
AX = mybir.AxisListType
NCORE = 8
D = 1024
NTP = 16
NTS = 2
NT = NTP + NTS
EPS = 1e-6
NEGB = -30000.0
G4 = [[0, 1, 2, 3], [4, 5, 6, 7]]
G2 = [[0, 4], [1, 5], [2, 6], [3, 7]]


class Buf:
    def __init__(self, ap, sem=None):
        self.ap = ap
        self.sem = sem
        self.qs = {}
        self.lastw = {}
        self.reads = {}


class QSem:
    def __init__(self, sem):
        self.sem = sem
        self.tot = 0

    def __getitem__(self, key):
        return self.ap[key]


def _merge(d, evs):
    for key, ev in evs.items():
        if key not in d or d[key][1] < ev[1]:
            d[key] = ev


class KB:
    def __init__(self, nc):
        self.nc = nc
        self.E = dict(pe=nc.tensor, act=nc.scalar, dve=nc.vector, pool=nc.gpsimd, sp=nc.sync)
        self.psem = {e: nc.alloc_semaphore("prog_" + e) for e in ("pe", "act", "dve", "pool")}
        self.cnt = {e: 0 for e in self.psem}
        self.known = {e: {} for e in self.E}
        self.n = 0
        self.ccsem = nc.alloc_semaphore("ccsem")
        self.cccnt = 0
        self.outs = {}

    def _name(self, p):
        self.n += 1
        return "%s%d" % (p, self.n)

    def sb(self, shape, dt=F32, dma=False):
        t = self.nc.alloc_sbuf_tensor(self._name("sb"), list(shape), dt)
        return Buf(t.ap(), True if dma else None)

    def ps(self, shape, dt=F32):
        t = self.nc.alloc_psum_tensor(self._name("ps"), list(shape), dt)
        return Buf(t.ap())

    def dram(self, shape, dt=F32, dma=False):
        t = self.nc.dram_tensor(self._name("dr"), list(shape), dt)
        return Buf(t.ap(), True if dma else None)

    def _wait(self, eng, evs, skip=None):
        for key, ev in evs.items():
            sem, val = ev[0], ev[1]
            if skip is not None and key == skip:
                continue
            if len(ev) > 2 and ev[2] is not None:
                val = max(val, ev[2].tot)
            if self.known[eng].get(key, 0) >= val:
                continue
            self.E[eng].wait_ge(sem, val)
            self.known[eng][key] = val

    def op(self, eng, fn, r=(), w=()):
        deps = {}
        for b in r:
            _merge(deps, b.lastw)
        for b in w:
            _merge(deps, b.lastw)
            _merge(deps, b.reads)
        self._wait(eng, deps, skip=("p", "pe") if eng == "pe" else None)
        ins = fn(self.E[eng])
        self.cnt[eng] += 1
        ins.then_inc(self.psem[eng], 1)
        ev = {("p", eng): (self.psem[eng], self.cnt[eng])}
        for b in w:
            b.lastw = dict(ev)
            b.reads = {}
        for b in r:
            _merge(b.reads, ev)

    def dma(self, q, out, in_, w=None, r=None, final=False):
        deps = {}
        if r is not None:
            _merge(deps, r.lastw)
        if w is not None:
            _merge(deps, {kk: vv for kk, vv in w.lastw.items() if kk[0] != "d"})
            _merge(deps, w.reads)
        self._wait(q, deps)
        sb = w if (w is not None and w.sem is not None) else r
        assert sb is not None and sb.sem is not None
        if q not in sb.qs:
            sb.qs[q] = QSem(self.nc.alloc_semaphore(self._name("dq")))
        qs = sb.qs[q]
        self.E[q].dma_start(out=out, in_=in_).then_inc(qs.sem, 16)
        qs.tot += 16
        ev = {("d", id(qs)): (qs.sem, qs.tot, qs)}
        if w is not None:
            _merge(w.lastw, ev)
            w.reads = {}
        if r is not None:
            _merge(r.reads, ev)
        if final:
            _merge(self.outs, ev)

    def allgather(self, groups, src, dst):
        deps = {}
        _merge(deps, src.lastw)
        _merge(deps, dst.lastw)
        _merge(deps, dst.reads)
        _merge(deps, getattr(self, "lastcc", {}))
        self._wait("pool", deps)
        sem = self.nc.alloc_semaphore(self._name("cc"))
        self.nc.gpsimd.collective_compute("AllGather", ALU.bypass, replica_groups=groups,
                                          ins=[src.ap], outs=[dst.ap]).then_inc(sem)
        ev = {("cc", id(sem)): (sem, 1)}
        self.lastcc = dict(ev)
        dst.lastw = dict(ev)
        dst.reads = {}
        _merge(src.reads, ev)

    def finish(self):
        self._wait("pool", self.outs)


def build():
    nc = bass.Bass("TRN2", target_bir_lowering=False)
    k = KB(nc)

    def din(name, shape, dt=F32):
        return nc.dram_tensor(name, list(shape), dt, kind="ExternalInput").ap()

    def dout(name, shape):
        return nc.dram_tensor(name, list(shape), F32, kind="ExternalOutput").ap()

    xin = din("xin", [NT * 128, D])
    c_lat = din("c_lat", [4, 1024, 256])
    c_kr = din("c_kr", [4, 1024, 64])
    c_st = din("c_st", [4, 8, 128, 128])
    c_mk = din("c_mk", [2, 4, 256, 512])
    c_mv = din("c_mv", [2, 4, 256, 512])
    memp = din("memp", [256, D])
    gains = din("gains", [10, D])
    qn_g = din("qn_g", [384])
    kvn_g = din("kvn_g", [256])
    on_g = din("on_g", [1024])
    hlb = din("hlb", [2, 1024])
    w_memkv = din("w_memkv", [2, 1024, 1024])
    w_in0 = din("w_in0", [1024, 1216])
    w_uq = din("w_uq", [384, 1536])
    w_ukT = din("w_ukT", [8, 128, 256])
    w_uv = din("w_uv", [256, 1024])
    w_out0 = din("w_out0", [1536, 1024])
    w_in1 = din("w_in1", [1024, 4608])
    w_out1 = din("w_out1", [1536, 1024])
    w_up = din("w_up", [2, 1024, 4096])
    w_dn = din("w_dn", [2, 4096, 1024])
    cos_t = din("cos_t", [NT * 128, 256])
    sin_t = din("sin_t", [NT * 128, 256])
    maskb = din("maskb", [32, 128, 128])
    scanm = din("scanm", [128, 8])
    hconst = din("hconst", [128, 128 + 128 + 4])

    y_o = dout("y_o", [NT * 128, D])
    lat_o = dout("lat_o", [NT * 128, 256])
    kr_o = dout("kr_o", [NT * 128, 64])
    stp_o = dout("stp_o", [8, 128, 128])
    mk_o = dout("mk_o", [2, 256, 512])
    mv_o = dout("mv_o", [2, 256, 512])
    sts_o = dout("sts_o", [4, 8, 128, 128])

    def wscr(shape):
        return k.dram(shape, BF16)
    s_memkv = wscr([2, 1024, 1024]); s_in0 = wscr([1024, 1216]); s_uq = wscr([384, 1536])
    s_ukT = wscr([8, 128, 256]); s_uv = wscr([256, 1024]); s_out0 = wscr([1536, 1024])
    s_in1 = wscr([1024, 4608]); s_out1 = wscr([1536, 1024]); s_up = [wscr([1024, 4096]) for _ in range(2)]
    s_dn = [wscr([4096, 1024]) for _ in range(2)]
    kT_rows = [128, 128, 64]
    kT_own = [k.dram([r_, 2048], BF16) for r_ in kT_rows]
    kT_g4 = [k.dram([4 * r_, 2048], BF16) for r_ in kT_rows]
    kT_all = [k.dram([8 * r_, 2048], BF16) for r_ in kT_rows]
    v_own = [k.dram([1024, 256], BF16) for _ in range(2)]
    v_g4 = [k.dram([4 * 1024, 256], BF16) for _ in range(2)]
    v_all = [k.dram([8 * 1024, 256], BF16) for _ in range(2)]
    kT_s = [k.dram([320, 1088], BF16) for _ in range(4)]
    v_s = [k.dram([1088, 256], BF16, dma=True) for _ in range(4)]
    x1_s = k.dram([NT * 128, D], F32)
    hT_s = k.dram([NT * 128, D], BF16)
    memkv_s = k.dram([2, 256, 1024], F32)
    exU_own = [k.dram([128, 1024], F32) for _ in range(4)]
    exU_g4 = [k.dram([4 * 128, 1024], F32) for _ in range(4)]
    exU_all = [k.dram([8 * 128, 1024], F32) for _ in range(4)]
    exD_own = k.dram([128, 32], F32); exD_g4 = k.dram([4 * 128, 32], F32); exD_all = k.dram([8 * 128, 32], F32)
    snapd = [k.dram([128, 1024], F32) for _ in range(4)]
    st_o = k.dram([NTP * 128, 1024], F32); st_g = k.dram([NTP * 128, 1024], F32)
    st_q = k.dram([NTP * 128, 1024], BF16); st_x = k.dram([NTP * 128, 512], BF16)

    ident = k.sb([128, 128], BF16)
    epsb = k.sb([128, 1], F32)
    ones_f = k.sb([128, 128], F32)
    gb = [k.sb([128, D], F32, dma=True) for _ in range(4)]
    qn_b = k.sb([128, 384], F32, dma=True)
    kvn_b = k.sb([128, 256], F32, dma=True)
    on_b = k.sb([128, 1024], F32, dma=True)
    lbb = k.sb([128, 1024], F32, dma=True)
    oml = k.sb([128, 1024], F32, dma=True)
    hc = k.sb([128, 260], F32, dma=True)
    scm = k.sb([128, 8], F32, dma=True)
    xr = [k.sb([128, D], F32, dma=True) for _ in range(4)]
    WS = 4096
    wr = [k.sb([128, WS], BF16, dma=True) for _ in range(6)]
    wri = [0]
    ktr = [k.sb([128, 3, 512], BF16, dma=True) for _ in range(3)]
    vtr = [k.sb([128, 4, 258], BF16, dma=True) for _ in range(3)]
    kvi = [0]
    mkr = [k.sb([128, 128], F32, dma=True) for _ in range(2)]
    cosr = [k.sb([128, 256], F32, dma=True) for _ in range(2)]
    sinr = [k.sb([128, 256], F32, dma=True) for _ in range(2)]
    Ft = [k.sb([128, 1024], F32, dma=True) for _ in range(7)]
    Ht = [k.sb([128, 1024], BF16, dma=True) for _ in range(14)]
    qsb = k.sb([128, 1536], F32)
    small = [k.sb([128, 16], F32) for _ in range(8)]
    exb = [k.sb([128, 1032], F32, dma=True) for _ in range(2)]
    Sst = k.sb([128, 1024], F32, dma=True)
    Dacc = k.sb([128, 8], F32)
    exD_sb = k.sb([128, 32], F32, dma=True)
    exDa_sb = k.sb([128, 8, 32], F32, dma=True)
    stsb = k.sb([128, 32], F32)
    mkT = [k.sb([128, 4, 256], BF16) for _ in range(2)]
    mv = [k.sb([128, 2, 4, 130], BF16, dma=True) for _ in range(2)]
    mstage = k.sb([128, 2, 512], BF16, dma=True)
    P = [k.ps([128, 512], F32) for _ in range(6)]
    PT = [k.ps([128, 1024], BF16) for _ in range(2)]
    pti = [0]

    E = k.op

    E("pool", lambda e: e.memset(ident.ap, 0.0), w=[ident])
    E("pool", lambda e: e.affine_select(out=ident.ap, in_=ident.ap, pattern=[[-1, 128]], compare_op=ALU.not_equal,
                                        fill=1.0, base=0, channel_multiplier=1), r=[ident], w=[ident])
    E("pool", lambda e: e.memset(epsb.ap, EPS), w=[epsb])
    E("pool", lambda e: e.memset(ones_f.ap, 1.0), w=[ones_f])
    for vb in vtr:
        E("pool", lambda e, vb=vb: e.memset(vb.ap, 1.0), w=[vb])
    for kb_ in ktr:
        E("pool", lambda e, kb_=kb_: e.memset(kb_.ap[64:128, 2, :], 0.0), w=[kb_])
    for m in mv:
        E("pool", lambda e, m=m: e.memset(m.ap, 1.0), w=[m])
    k.dma("sp", qn_b.ap, qn_g.partition_broadcast(128), w=qn_b)
    k.dma("sp", kvn_b.ap, kvn_g.partition_broadcast(128), w=kvn_b)
    k.dma("sp", on_b.ap, on_g.partition_broadcast(128), w=on_b)
    k.dma("sp", hc.ap, hconst, w=hc)
    k.dma("sp", scm.ap, scanm, w=scm)
    k.dma("sp", lbb.ap, hlb[1].partition_broadcast(128), w=lbb)
    k.dma("sp", oml.ap, hlb[0].partition_broadcast(128), w=oml)
    E("dve", lambda e: e.tensor_tensor(out=lbb.ap, in0=lbb.ap, in1=oml.ap, op=ALU.subtract), r=[oml, lbb], w=[lbb])
    E("act", lambda e: e.activation(out=lbb.ap, in_=lbb.ap, func=AF.Sigmoid), r=[lbb], w=[lbb])
    E("dve", lambda e: e.tensor_scalar(out=oml.ap, in0=lbb.ap, scalar1=-1.0, scalar2=1.0, op0=ALU.mult, op1=ALU.add),
      r=[lbb], w=[oml])

    wsem = Buf(None, True)
    wsem0 = Buf(None, True)

    bg = []

    def bg_run(n):
        for _ in range(n):
            if bg:
                bg.pop(0)()

    def castw(dst, src2d, dst2d, rows, ws, defer=False):
        step = 128
        for r0 in range(0, rows, step):
            fn = lambda r0=r0: k.dma("pool", dst2d[r0:r0 + step], src2d[r0:r0 + step], w=dst, r=ws)
            if defer:
                bg.append(fn)
            else:
                fn()

    def cast_fin(ws, scs):
        _q = ws.qs["pool"]
        _fin = {("d", id(_q)): (_q.sem, _q.tot, _q)}
        for sc in scs:
            sc.lastw = dict(_fin)
    castw(s_memkv, w_memkv.rearrange("l k n -> (l k) n"), s_memkv.ap.rearrange("l k n -> (l k) n"), 2048, wsem0)
    castw(s_in0, w_in0, s_in0.ap, 1024, wsem0)
    cast_fin(wsem0, (s_memkv, s_in0))

    wsemB = Buf(None, True)

    def cast_rest():
        castw(s_uq, w_uq, s_uq.ap, 384, wsem, defer=True)
        castw(s_ukT, w_ukT.rearrange("h d c -> (h d) c"), s_ukT.ap.rearrange("h d c -> (h d) c"), 1024, wsem, defer=True)
        castw(s_uv, w_uv, s_uv.ap, 256, wsem, defer=True)
        castw(s_out0, w_out0, s_out0.ap, 1536, wsem, defer=True)
        castw(s_up[0], w_up[0], s_up[0].ap, 1024, wsem, defer=True)
        castw(s_dn[0], w_dn[0], s_dn[0].ap, 4096, wsem, defer=True)

    def cast_layer1():
        castw(s_in1, w_in1, s_in1.ap, 1024, wsemB, defer=True)
        castw(s_out1, w_out1, s_out1.ap, 1536, wsemB, defer=True)
        castw(s_up[1], w_up[1], s_up[1].ap, 1024, wsemB, defer=True)
        castw(s_dn[1], w_dn[1], s_dn[1].ap, 4096, wsemB, defer=True)

    def wload(src_buf, src_ap, shape):
        b = wr[wri[0] % 6]
        wri[0] += 1
        n = shape[1] * shape[2]
        assert n <= WS
        view = b.ap[:, 0:n].rearrange("p (a b) -> p a b", a=shape[1])
        k.dma("sp", view, src_ap, w=b, r=src_buf)
        return b, view

    def rstd_of(src_ap, src_buf, n, junk, out_small, col=0):
        o = out_small.ap[:, col:col + 1]
        E("act", lambda e: e.activation(out=junk.ap[:, 0:n], in_=src_ap, func=AF.Square, accum_out=o),
          r=[src_buf], w=[junk, out_small])
        E("act", lambda e: e.activation(out=o, in_=o, func=AF.Ln, bias=epsb.ap, scale=1.0 / n), r=[out_small, epsb], w=[out_small])
        E("act", lambda e: e.activation(out=o, in_=o, func=AF.Exp, scale=-0.5), r=[out_small], w=[out_small])
        return o

    def transposes(dst_buf, dst_fn, src_buf, src_fn, n, rows=128, cols=128, whole=None, ptview=None):
        pt = PT[pti[0] % 2]
        pti[0] += 1
        for i in range(n):
            E("pe", lambda e, i=i: e.transpose(out=pt.ap[0:cols, i * 128:i * 128 + rows], in_=src_fn(i), identity=ident.ap[0:rows, 0:rows]),
              r=[src_buf, ident], w=[pt])
        if whole is not None and rows == 128:
            src = pt.ap[0:cols, 0:n * 128]
            if ptview is not None:
                src = ptview(src)
            E("dve", lambda e: e.tensor_copy(out=whole, in_=src), r=[pt], w=[dst_buf])
            return
        for i in range(n):
            E("dve", lambda e, i=i: e.tensor_copy(out=dst_fn(i), in_=pt.ap[0:cols, i * 128:i * 128 + rows]), r=[pt], w=[dst_buf])

    def load_gains(l):
        for j in range(4):
            k.dma("sp", gb[j].ap, gains[2 * j + l].partition_broadcast(128), w=gb[j])

    xissued = {}
    nexttile = [None]

    xseq = [0]
    hpre = [k.sb([128, D], BF16, dma=True) for _ in range(2)]
    hseq = [0]
    hissued = {}
    want_hT = [False]

    def xprefetch(t, l):
        if (t, l) in xissued:
            return
        slot = xseq[0] % 4
        xseq[0] += 1
        xissued[(t, l)] = slot
        xb = xr[slot]
        if l == 0:
            k.dma("sp", xb.ap, xin[t * 128:(t + 1) * 128, :], w=xb)
            if want_hT[0]:
                hs = hseq[0] % 2
                hseq[0] += 1
                hissued[t] = hs
                k.dma("sp", hpre[hs].ap, hT_s.ap[t * 128:(t + 1) * 128, :], w=hpre[hs], r=hT_s)
        else:
            k.dma("sp", xb.ap, x1_s.ap[t * 128:(t + 1) * 128, :], w=xb, r=x1_s)

    def prefetch_next():
        if nexttile[0] is not None:
            xprefetch(*nexttile[0])

    def front(t, l, gi):
        xprefetch(t, l)
        return xr[xissued.pop((t, l))]

    def norm_T(xb, gi, hb, hTb, sm=None):
        sm = small[0] if sm is None else sm
        rs = rstd_of(xb.ap, xb, D, Ht[13], sm)
        E("dve", lambda e: e.scalar_tensor_tensor(out=hb.ap, in0=xb.ap, scalar=rs, in1=gb[gi].ap, op0=ALU.mult, op1=ALU.mult),
          r=[xb, sm, gb[gi]], w=[hb])
        transposes(hTb, lambda i: hTb.ap[:, i * 128:(i + 1) * 128], hb, lambda i: hb.ap[:, i * 128:(i + 1) * 128], 8, whole=hTb.ap[:, 0:8 * 128])

    def proj(hTb, nk, wsrc_buf, wsrc_fn, ncols, pbank, kstep=4):
        for k0 in range(0, nk, kstep):
            kn = min(kstep, nk - k0)
            wb, wv = wload(wsrc_buf, wsrc_fn(k0, kn), [128, kn, ncols])
            for kk in range(kn):
                kc = k0 + kk
                E("pe", lambda e, kc=kc, kk=kk, wv=wv: e.matmul(pbank.ap[:, 0:ncols], lhsT=hTb.ap[:, kc * 128:(kc + 1) * 128],
                                                           rhs=wv[:, kk, :], start=(kc == 0), stop=(kc == nk - 1)),
                  r=[hTb, wb], w=[pbank])

    def wcols(sbuf2d, c0, ncols):
        return lambda k0, kn: sbuf2d[k0 * 128:(k0 + kn) * 128, c0:c0 + ncols].rearrange("(k p) n -> p k n", p=128)

    def phase0():
        for l in range(2):
            k.dma("sp", gb[0].ap, gains[8 + l].partition_broadcast(128), w=gb[0])
            for mb in range(2):
                xb = xr[mb % 2]
                k.dma("sp", xb.ap, memp[mb * 128:(mb + 1) * 128, :], w=xb)
                norm_T(xb, 0, Ht[0], Ht[1])
                for g in range(2):
                    proj(Ht[1], 8, s_memkv, wcols(s_memkv.ap[l], g * 512, 512), 512, P[g])
                    E("dve", lambda e, g=g: e.tensor_copy(out=Ft[0].ap[:, g * 512:(g + 1) * 512], in_=P[g].ap), r=[P[g]], w=[Ft[0]])
                k.dma("pool", mk_o[l, mb * 128:(mb + 1) * 128, :], Ft[0].ap[:, 0:512], r=Ft[0], final=True)
                k.dma("pool", mv_o[l, mb * 128:(mb + 1) * 128, :], Ft[0].ap[:, 512:1024], r=Ft[0], final=True)
                k.dma("pool", memkv_s.ap[l, mb * 128:(mb + 1) * 128, :], Ft[0].ap, w=memkv_s, r=Ft[0])

    def build_memset(slot, ksrc_ap, vsrc_ap, src_buf):
        k.dma("pool", mstage.ap, ksrc_ap.rearrange("(b p) n -> p b n", p=128), w=mstage, r=src_buf)
        for blk in range(2):
            transposes(mkT[slot], lambda h, blk=blk: mkT[slot].ap[:, h, blk * 128:(blk + 1) * 128],
                       mstage, lambda h, blk=blk: mstage.ap[:, blk, h * 128:(h + 1) * 128], 4,
                       whole=mkT[slot].ap[:, :, blk * 128:(blk + 1) * 128], ptview=lambda a: a.rearrange("p (h m) -> p h m", h=4))
        for blk in range(2):
            k.dma("pool", mv[slot].ap[:, blk, :, 0:128], vsrc_ap[blk * 128:(blk + 1) * 128, :].rearrange("p (h d) -> p h d", h=4), w=mv[slot], r=src_buf)

    def cross_attn(xqT, segs, crossT):
        pTx = Ht[11]
        csb = Ht[12]
        for (t0, ntk, slot) in segs:
            for h in range(4):
                for blk in range(2):
                    bank = P[4 + (h // 2)]
                    off = ((h % 2) * 2 + blk) * 128
                    E("pe", lambda e, h=h, blk=blk, bank=bank, off=off: e.matmul(
                        bank.ap[:, off + t0:off + t0 + ntk], lhsT=mkT[slot].ap[:, h, blk * 128:(blk + 1) * 128],
                        rhs=xqT.ap[:, h * 128 + t0:h * 128 + t0 + ntk], start=True, stop=True), r=[mkT[slot], xqT], w=[bank])
        for hb in range(2):
            E("act", lambda e, hb=hb: e.activation(out=pTx.ap[:, hb * 512:(hb + 1) * 512], in_=P[4 + hb].ap, func=AF.Exp,
                                                   scale=128.0 ** -0.5), r=[P[4 + hb]], w=[pTx])
        for (t0, ntk, slot) in segs:
            for h in range(4):
                bank = P[h // 2]
                for blk in range(2):
                    off = (h * 2 + blk) * 128
                    E("pe", lambda e, h=h, blk=blk, bank=bank, off=off: e.matmul(
                        bank.ap[t0:t0 + ntk, (h % 2) * 130:(h % 2) * 130 + 129], lhsT=pTx.ap[:, off + t0:off + t0 + ntk],
                        rhs=mv[slot].ap[:, blk, h, 0:129], start=(blk == 0), stop=(blk == 1)), r=[pTx, mv[slot]], w=[bank])
        for h in range(4):
            bank = P[h // 2]
            c0 = (h % 2) * 130
            E("dve", lambda e, h=h, bank=bank, c0=c0: e.reciprocal(out=small[1].ap[:, h:h + 1], in_=bank.ap[:, c0 + 128:c0 + 129]),
              r=[bank], w=[small[1]])
            E("dve", lambda e, h=h, bank=bank, c0=c0: e.tensor_scalar(out=csb.ap[:, h * 128:(h + 1) * 128], in0=bank.ap[:, c0:c0 + 128],
                                                                   scalar1=small[1].ap[:, h:h + 1], scalar2=None, op0=ALU.mult),
              r=[bank, small[1]], w=[csb])
        transposes(crossT, lambda h: crossT.ap[:, h * 128:(h + 1) * 128], csb, lambda h: csb.ap[:, h * 128:(h + 1) * 128], 4, whole=crossT.ap[:, 0:512])

    def post_add(xb, gi, src=None):
        if src is None:
            z = Ft[6]
            for g in range(2):
                E("dve", lambda e, g=g: e.tensor_copy(out=z.ap[:, g * 512:(g + 1) * 512], in_=P[g].ap), r=[P[g]], w=[z])
        else:
            z = src
        rs = rstd_of(z.ap, z, D, Ht[13], small[2])
        E("dve", lambda e: e.scalar_tensor_tensor(out=z.ap, in0=z.ap, scalar=rs, in1=gb[gi].ap, op0=ALU.mult, op1=ALU.mult),
          r=[z, small[2], gb[gi]], w=[z])
        E("dve", lambda e: e.tensor_tensor(out=xb.ap, in0=xb.ap, in1=z.ap, op=ALU.add), r=[xb, z], w=[xb])

    pending = []

    def ffn_pair(items):
        l = items[0][0]
        hbs = [(Ht[0], Ht[1]), (Ht[2], Ht[3])]
        accs = [Ft[0], Ft[1]]
        rls = [Ft[5], Ft[4]]
        hids = [Ht[9], Ht[7]]
        hidTs = [Ht[10], Ht[8]]
        for j, (l_, t, xb, last) in enumerate(items):
            norm_T(xb, 2, hbs[j][0], hbs[j][1], sm=small[0] if j == 0 else small[1])
        for pc in range(4):
            ups = [wload(s_up[l], s_up[l].ap[:, pc * 1024 + g * 512:pc * 1024 + (g + 1) * 512].rearrange("(k p) n -> p k n", p=128), [128, 8, 512])
                   for g in range(2)]
            for j in range(len(items)):
                hTb = hbs[j][1]
                for g in range(2):
                    bank = P[2 + 2 * j + g]
                    wb, wv = ups[g]
                    for kc in range(8):
                        E("pe", lambda e, kc=kc, bank=bank, wv=wv, hTb=hTb: e.matmul(bank.ap, lhsT=hTb.ap[:, kc * 128:(kc + 1) * 128],
                                                                                rhs=wv[:, kc, :], start=(kc == 0), stop=(kc == 7)),
                          r=[hTb, wb], w=[bank])
            dns = [wload(s_dn[l], s_dn[l].ap[pc * 1024 + h_ * 512:pc * 1024 + (h_ + 1) * 512, :].rearrange("(k p) n -> p k n", p=128), [128, 4, 1024])
                   for h_ in range(2)]
            for j in range(len(items)):
                rl, hid, hidT, acc = rls[j], hids[j], hidTs[j], accs[j]
                for g in range(2):
                    bank = P[2 + 2 * j + g]
                    E("act", lambda e, g=g, bank=bank, rl=rl: e.activation(out=rl.ap[:, g * 512:(g + 1) * 512], in_=bank.ap, func=AF.Relu), r=[bank], w=[rl])
                E("pool", lambda e, rl=rl, hid=hid: e.tensor_tensor(out=hid.ap, in0=rl.ap, in1=rl.ap, op=ALU.mult), r=[rl], w=[hid])
                transposes(hidT, lambda i, hidT=hidT: hidT.ap[:, i * 128:(i + 1) * 128], hid, lambda i, hid=hid: hid.ap[:, i * 128:(i + 1) * 128], 8,
                           whole=hidT.ap[:, 0:1024])
                for kc in range(8):
                    wb2, wv2 = dns[kc // 4]
                    for g in range(2):
                        E("pe", lambda e, kc=kc, g=g, wv2=wv2, hidT=hidT: e.matmul(P[g].ap, lhsT=hidT.ap[:, kc * 128:(kc + 1) * 128],
                                                                              rhs=wv2[:, kc % 4, g * 512:(g + 1) * 512], start=(kc == 0), stop=(kc == 7)),
                          r=[hidT, wb2], w=[P[g]])
                for g in range(2):
                    if pc == 0:
                        E("dve", lambda e, g=g, acc=acc: e.tensor_copy(out=acc.ap[:, g * 512:(g + 1) * 512], in_=P[g].ap), r=[P[g]], w=[acc])
                    else:
                        E("dve", lambda e, g=g, acc=acc: e.tensor_tensor(out=acc.ap[:, g * 512:(g + 1) * 512], in0=P[g].ap,
                                                                         in1=acc.ap[:, g * 512:(g + 1) * 512], op=ALU.add), r=[P[g], acc], w=[acc])
        for j, (l_, t, xb, last) in enumerate(items):
            post_add(xb, 3, src=accs[j])
            if last:
                k.dma("pool", y_o[t * 128:(t + 1) * 128, :], xb.ap, r=xb, final=True)
            else:
                k.dma("pool", x1_s.ap[t * 128:(t + 1) * 128, :], xb.ap, w=x1_s, r=xb)

    def flush_ffn():
        if pending:
            ffn_pair(list(pending))
            del pending[:]

    def out_and_ffn(l, t, xb, mixT, crossT, s_out, last):
        for half in range(3):
            wb, wv = wload(s_out, s_out.ap[half * 512:(half + 1) * 512, :].rearrange("(k p) n -> p k n", p=128), [128, 4, 1024])
            for kk in range(4):
                kc = half * 4 + kk
                src, so = (mixT, kc * 128) if kc < 8 else (crossT, (kc - 8) * 128)
                for g in range(2):
                    E("pe", lambda e, kc=kc, kk=kk, g=g, src=src, so=so, wv=wv: e.matmul(
                        P[g].ap, lhsT=src.ap[:, so:so + 128], rhs=wv[:, kk, g * 512:(g + 1) * 512],
                        start=(kc == 0), stop=(kc == 11)), r=[src, wb], w=[P[g]])
        post_add(xb, 1)
        prefetch_next()
        pending.append((l, t, xb, last))
        if len(pending) == 2:
            flush_ffn()

    def l0_kpart(t):
        xb = front(t, 0, 0)
        od = t % 2
        hb, hTb = (Ht[0], Ht[1]) if od == 0 else (Ht[4], Ht[5])
        norm_T(xb, 0, hb, hTb, sm=small[0] if od == 0 else small[1])
        k.dma("pool", hT_s.ap[t * 128:(t + 1) * 128, :], hTb.ap, w=hT_s, r=hTb)
        pb = P[od]
        proj(hTb, 8, s_in0, wcols(s_in0.ap, 384, 320), 320, pb, kstep=8)
        ckv = Ft[0] if od == 0 else Ft[3]
        E("dve", lambda e: e.tensor_copy(out=ckv.ap[:, 0:320], in_=pb.ap[:, 0:320]), r=[pb], w=[ckv])
        sm3 = small[3] if od == 0 else small[4]
        rs = rstd_of(ckv.ap[:, 0:256], ckv, 256, Ht[13], sm3)
        lat = Ft[1] if od == 0 else Ft[4]
        E("dve", lambda e: e.scalar_tensor_tensor(out=lat.ap[:, 0:256], in0=ckv.ap[:, 0:256], scalar=rs, in1=kvn_b.ap, op0=ALU.mult, op1=ALU.mult),
          r=[ckv, sm3, kvn_b], w=[lat])
        cb, sbf = cosr[t % 2], sinr[t % 2]
        k.dma("sp", cb.ap, cos_t[t * 128:(t + 1) * 128, :], w=cb)
        k.dma("sp", sbf.ap, sin_t[t * 128:(t + 1) * 128, :], w=sbf)
        kr = lat.ap[:, 256:320]
        tmp = Ft[2] if od == 0 else Ft[5]
        x1, x2 = ckv.ap[:, 256:288], ckv.ap[:, 288:320]
        c, s = cb.ap[:, 0:32], sbf.ap[:, 0:32]
        E("dve", lambda e: e.tensor_tensor(out=tmp.ap[:, 0:32], in0=x1, in1=c, op=ALU.mult), r=[ckv, cb], w=[tmp])
        E("dve", lambda e: e.tensor_tensor(out=tmp.ap[:, 32:64], in0=x2, in1=s, op=ALU.mult), r=[ckv, sbf], w=[tmp])
        E("dve", lambda e: e.tensor_tensor(out=tmp.ap[:, 64:96], in0=x2, in1=c, op=ALU.mult), r=[ckv, cb], w=[tmp])
        E("dve", lambda e: e.tensor_tensor(out=tmp.ap[:, 96:128], in0=x1, in1=s, op=ALU.mult), r=[ckv, sbf], w=[tmp])
        E("dve", lambda e: e.tensor_tensor(out=lat.ap[:, 256:288], in0=tmp.ap[:, 0:32], in1=tmp.ap[:, 32:64], op=ALU.subtract), r=[tmp], w=[lat])
        E("dve", lambda e: e.tensor_tensor(out=lat.ap[:, 288:320], in0=tmp.ap[:, 64:96], in1=tmp.ap[:, 96:128], op=ALU.add), r=[tmp], w=[lat])
        k.dma("pool", lat_o[t * 128:(t + 1) * 128, :], lat.ap[:, 0:256], r=lat, final=True)
        k.dma("pool", kr_o[t * 128:(t + 1) * 128, :], lat.ap[:, 256:320], r=lat, final=True)
        kb = Ht[2] if od == 0 else Ht[6]
        E("dve", lambda e: e.tensor_copy(out=kb.ap[:, 0:320], in_=lat.ap[:, 0:320]), r=[lat], w=[kb])
        kTb = Ht[3] if od == 0 else Ht[7]
        transposes(kTb, lambda i: kTb.ap[:, i * 128:(i + 1) * 128], kb, lambda i: kb.ap[:, i * 128:(i + 1) * 128], 2, whole=kTb.ap[:, 0:2 * 128])
        transposes(kTb, lambda i: kTb.ap[0:64, 256:384], kb, lambda i: kb.ap[:, 256:320], 1, rows=128, cols=64, whole=kTb.ap[0:64, 256:384])
        if t < NTP:
            k.dma("pool", v_own[t // 8].ap[(t % 8) * 128:(t % 8 + 1) * 128, :], kb.ap[:, 0:256], w=v_own[t // 8], r=kb)
            k.dma("pool", kT_own[0].ap[:, t * 128:(t + 1) * 128], kTb.ap[:, 0:128], w=kT_own[0], r=kTb)
            k.dma("pool", kT_own[1].ap[:, t * 128:(t + 1) * 128], kTb.ap[:, 128:256], w=kT_own[1], r=kTb)
            k.dma("pool", kT_own[2].ap[:, t * 128:(t + 1) * 128], kTb.ap[0:64, 256:384], w=kT_own[2], r=kTb)
        else:
            for hs in range(2):
                sq = (t - NTP) * 2 + hs
                k.dma("pool", v_s[sq].ap[1024:1088, :], kb.ap[hs * 64:(hs + 1) * 64, 0:256], w=v_s[sq], r=kb)
                k.dma("pool", kT_s[sq].ap[0:256, 1024:1088].rearrange("(j p) n -> p j n", p=128),
                      kTb.ap[:, 0:256].rearrange("p (j n) -> p j n", j=2)[:, :, hs * 64:(hs + 1) * 64], w=kT_s[sq], r=kTb)
                k.dma("pool", kT_s[sq].ap[256:320, 1024:1088], kTb.ap[0:64, 256 + hs * 64:256 + (hs + 1) * 64], w=kT_s[sq], r=kTb)

    def sample_cache_prep_v():
        for sq in range(4):
            k.dma("pool", v_s[sq].ap[0:1024, :], c_lat[sq], w=v_s[sq], r=wsem0)

    def sample_cache_iters():
        its = []
        for sq in range(4):
            for tt in range(8):
                def it(sq=sq, tt=tt):
                    kb = Ht[8] if tt % 2 == 0 else Ht[10]
                    k.dma("pool", kb.ap[:, 0:256], c_lat[sq, tt * 128:(tt + 1) * 128, :], w=kb)
                    k.dma("pool", kb.ap[:, 256:320], c_kr[sq, tt * 128:(tt + 1) * 128, :], w=kb)
                    kTb = Ht[9] if tt % 2 == 0 else Ht[11]
                    transposes(kTb, lambda i: kTb.ap[:, i * 128:(i + 1) * 128], kb, lambda i: kb.ap[:, i * 128:(i + 1) * 128], 2, whole=kTb.ap[:, 0:2 * 128])
                    transposes(kTb, lambda i: kTb.ap[0:64, 256:384], kb, lambda i: kb.ap[:, 256:320], 1, rows=128, cols=64, whole=kTb.ap[0:64, 256:384])
                    k.dma("pool", kT_s[sq].ap[0:256, tt * 128:(tt + 1) * 128].rearrange("(j p) n -> p j n", p=128),
                          kTb.ap[:, 0:256].rearrange("p (j n) -> p j n", j=2), w=kT_s[sq], r=kTb)
                    k.dma("pool", kT_s[sq].ap[256:320, tt * 128:(tt + 1) * 128], kTb.ap[0:64, 256:384], w=kT_s[sq], r=kTb)
                its.append(it)
        return its

    def attend(qlT, qrT, n, kT_src, kfn, vfn, L, mask_qc, osb, odst):
        nblk = (L + 127) // 128
        if mask_qc is not None:
            mb = mkr[mask_qc % 2]
            k.dma("sp", mb.ap, maskb[mask_qc], w=mb)
        pS = [P[2], P[3], P[4], P[5]]
        pTs = [Ht[7], Ht[8], Ht[9], Ht[10]]
        acc = Ft[1]
        blocks = []

        def load_group(g0):
            key0 = g0 * 128
            nkeys = min(512, L - key0)
            kt = ktr[kvi[0] % 3]
            vt = vtr[kvi[0] % 3]
            kvi[0] += 1
            for j in range(3):
                kr_ = 128 if j < 2 else 64
                k.dma("sp", kt.ap[0:kr_, j, 0:nkeys], kfn(j, key0, nkeys), w=kt, r=kT_src[j])
            nfull = nkeys // 128
            vb_, vap = vfn(key0, nkeys)
            if nfull > 0:
                k.dma("sp", vt.ap[:, 0:nfull, 0:256], vap[0:nfull * 128, :].rearrange("(b p) c -> p b c", p=128), w=vt, r=vb_)
            if nkeys % 128:
                rem = nkeys % 128
                k.dma("sp", vt.ap[0:rem, nfull, 0:256], vap[nfull * 128:nkeys, :], w=vt, r=vb_)
            return kt, vt

        cur = [None, None]

        def emit_st(blk):
            if blk % 4 == 0:
                cur[0], cur[1] = load_group(blk)
            kt, vt = cur
            b = blk % 4
            kn = min(128, L - blk * 128)
            ps = pS[blk % 4]
            for j in range(3):
                kr_ = 128
                rhs = qlT.ap[:, (j * 2 + n) * 512:(j * 2 + n + 1) * 512] if j < 2 else qrT.ap[:, n * 512:(n + 1) * 512]
                E("pe", lambda e, j=j, kr_=kr_, rhs=rhs, ps=ps, kt=kt, b=b, kn=kn: e.matmul(
                    ps.ap[0:kn, :], lhsT=kt.ap[0:kr_, j, b * 128:b * 128 + kn], rhs=rhs, start=(j == 0), stop=(j == 2)),
                  r=[kt, qlT, qrT], w=[ps])
            return (vt, b, kn)

        def emit_exp(blk, info):
            vt, b, kn = info
            ps = pS[blk % 4]
            pt = pTs[blk % 4]
            if mask_qc is not None:
                E("act", lambda e, ps=ps, pt=pt, kn=kn, blk=blk: e.activation(
                    out=pt.ap[0:kn, 0:512], in_=ps.ap[0:kn, :], func=AF.Exp, scale=192.0 ** -0.5, bias=mb.ap[0:kn, blk:blk + 1]),
                  r=[ps, mb], w=[pt])
            else:
                E("act", lambda e, ps=ps, pt=pt, kn=kn: e.activation(
                    out=pt.ap[0:kn, 0:512], in_=ps.ap[0:kn, :], func=AF.Exp, scale=192.0 ** -0.5), r=[ps], w=[pt])

        def emit_pv(blk, info):
            vt, b, kn = info
            pt = pTs[blk % 4]
            for cc in range(2):
                E("pe", lambda e, cc=cc, pt=pt, vt=vt, b=b, kn=kn, blk=blk: e.matmul(
                    P[cc].ap, lhsT=vt.ap[0:kn, b, cc * 128:(cc + 1) * 128], rhs=pt.ap[0:kn, 0:512],
                    start=(blk == 0), stop=(blk == nblk - 1)), r=[pt, vt], w=[P[cc]])
            if blk == 0:
                E("dve", lambda e, pt=pt: e.tensor_copy(out=acc.ap[:, 0:512], in_=pt.ap[:, 0:512]), r=[pt], w=[acc])
            else:
                E("dve", lambda e, pt=pt, kn=kn: e.tensor_tensor(out=acc.ap[0:kn, 0:512], in0=acc.ap[0:kn, 0:512], in1=pt.ap[0:kn, 0:512], op=ALU.add),
                  r=[pt, acc], w=[acc])

        units = [list(range(u, min(u + 2, nblk))) for u in range(0, nblk, 2)]
        infos = {b_: emit_st(b_) for b_ in units[0]}
        for ui, un in enumerate(units):
            if ui + 1 < len(units):
                for b_ in units[ui + 1]:
                    infos[b_] = emit_st(b_)
            for b_ in un:
                emit_exp(b_, infos[b_])
            for b_ in un:
                emit_pv(b_, infos.pop(b_))
        E("pe", lambda e: e.matmul(P[2].ap, lhsT=ones_f.ap, rhs=acc.ap[:, 0:512], start=True, stop=True), r=[ones_f, acc], w=[P[2]])
        E("dve", lambda e: e.reciprocal(out=acc.ap[:, 512:1024], in_=P[2].ap), r=[P[2]], w=[acc])
        for cc in range(2):
            E("dve", lambda e, cc=cc: e.tensor_tensor(out=odst[:, (cc * 2 + n) * 512:(cc * 2 + n + 1) * 512], in0=P[cc].ap,
                                                      in1=acc.ap[:, 512:1024], op=ALU.mult), r=[P[cc], acc], w=[osb])

    def l0_tile(t):
        xb = front(t, 0, 0)
        hTb = hpre[hissued.pop(t)]
        proj(hTb, 8, s_in0, wcols(s_in0.ap, 0, 384), 384, P[0], kstep=8)
        cq = Ft[0]
        E("dve", lambda e: e.tensor_copy(out=cq.ap[:, 0:384], in_=P[0].ap[:, 0:384]), r=[P[0]], w=[cq])
        rs = rstd_of(cq.ap[:, 0:384], cq, 384, Ht[13], small[3])
        cqb, cqT = Ht[2], Ht[3]
        E("dve", lambda e: e.scalar_tensor_tensor(out=cqb.ap[:, 0:384], in0=cq.ap[:, 0:384], scalar=rs, in1=qn_b.ap, op0=ALU.mult, op1=ALU.mult),
          r=[cq, small[3], qn_b], w=[cqb])
        transposes(cqT, lambda i: cqT.ap[:, i * 128:(i + 1) * 128], cqb, lambda i: cqb.ap[:, i * 128:(i + 1) * 128], 3, whole=cqT.ap[:, 0:3 * 128])
        wb, wv = wload(s_uq, s_uq.ap.rearrange("(k p) n -> p k n", p=128)[:, :, 0:768], [128, 3, 768])
        wb2, wv2 = wload(s_uq, s_uq.ap.rearrange("(k p) n -> p k n", p=128)[:, :, 768:1536], [128, 3, 768])
        for g in range(3):
            wbb, wvv, c0 = (wb, wv, g * 512) if g == 0 else ((wb2, wv2, g * 512 - 768) if g == 2 else (None, None, None))
            for kc in range(3):
                if g == 1:
                    E("pe", lambda e, kc=kc: e.matmul(P[2].ap[:, 0:256], lhsT=cqT.ap[:, kc * 128:(kc + 1) * 128], rhs=wv[:, kc, 512:768],
                                                     start=(kc == 0), stop=(kc == 2)), r=[cqT, wb], w=[P[2]])
                    E("pe", lambda e, kc=kc: e.matmul(P[3].ap[:, 0:256], lhsT=cqT.ap[:, kc * 128:(kc + 1) * 128], rhs=wv2[:, kc, 0:256],
                                                     start=(kc == 0), stop=(kc == 2)), r=[cqT, wb2], w=[P[3]])
                else:
                    bank = P[0] if g == 0 else P[1]
                    E("pe", lambda e, kc=kc, bank=bank, wvv=wvv, c0=c0: e.matmul(bank.ap, lhsT=cqT.ap[:, kc * 128:(kc + 1) * 128],
                                                                            rhs=wvv[:, kc, c0:c0 + 512], start=(kc == 0), stop=(kc == 2)),
                      r=[cqT, wbb], w=[bank])
        E("dve", lambda e: e.tensor_copy(out=qsb.ap[:, 0:512], in_=P[0].ap), r=[P[0]], w=[qsb])
        E("dve", lambda e: e.tensor_copy(out=qsb.ap[:, 512:768], in_=P[2].ap[:, 0:256]), r=[P[2]], w=[qsb])
        E("dve", lambda e: e.tensor_copy(out=qsb.ap[:, 768:1024], in_=P[3].ap[:, 0:256]), r=[P[3]], w=[qsb])
        E("dve", lambda e: e.tensor_copy(out=qsb.ap[:, 1024:1536], in_=P[1].ap), r=[P[1]], w=[qsb])
        q3 = qsb.ap.rearrange("p (h d) -> p h d", h=8)
        qn, qr = Ht[2], Ht[6]
        E("dve", lambda e: e.tensor_copy(out=qn.ap.rearrange("p (h d) -> p h d", h=8), in_=q3[:, :, 0:128]), r=[qsb], w=[qn])
        cb, sbf = cosr[t % 2], sinr[t % 2]
        k.dma("sp", cb.ap, cos_t[t * 128:(t + 1) * 128, :], w=cb)
        k.dma("sp", sbf.ap, sin_t[t * 128:(t + 1) * 128, :], w=sbf)
        c3 = cb.ap.rearrange("p (h d) -> p h d", h=8)
        s3 = sbf.ap.rearrange("p (h d) -> p h d", h=8)
        tmp = Ft[2]
        t4 = tmp.ap.rearrange("p (a h d) -> p a h d", a=4, h=8)
        x1, x2 = q3[:, :, 128:160], q3[:, :, 160:192]
        E("dve", lambda e: e.tensor_tensor(out=t4[:, 0], in0=x1, in1=c3, op=ALU.mult), r=[qsb, cb], w=[tmp])
        E("dve", lambda e: e.tensor_tensor(out=t4[:, 1], in0=x2, in1=s3, op=ALU.mult), r=[qsb, sbf], w=[tmp])
        E("dve", lambda e: e.tensor_tensor(out=t4[:, 2], in0=x2, in1=c3, op=ALU.mult), r=[qsb, cb], w=[tmp])
        E("dve", lambda e: e.tensor_tensor(out=t4[:, 3], in0=x1, in1=s3, op=ALU.mult), r=[qsb, sbf], w=[tmp])
        qr3 = qr.ap[:, 0:512].rearrange("p (h d) -> p h d", h=8)
        E("dve", lambda e: e.tensor_tensor(out=qr3[:, :, 0:32], in0=t4[:, 0], in1=t4[:, 1], op=ALU.subtract), r=[tmp], w=[qr])
        E("dve", lambda e: e.tensor_tensor(out=qr3[:, :, 32:64], in0=t4[:, 2], in1=t4[:, 3], op=ALU.add), r=[tmp], w=[qr])
        qnT, qrT0 = Ht[3], Ht[0]
        transposes(qnT, lambda i: qnT.ap[:, i * 128:(i + 1) * 128], qn, lambda i: qn.ap[:, i * 128:(i + 1) * 128], 8, whole=qnT.ap[:, 0:8 * 128])
        transposes(qrT0, lambda i: qrT0.ap[0:64, i * 128:(i + 1) * 128], qr, lambda i: qr.ap[:, i * 64:(i + 1) * 64], 8, rows=128, cols=64, whole=qrT0.ap[0:64, 0:1024])
        qrT = Ht[6]
        E("dve", lambda e: e.tensor_copy(out=qrT.ap[0:64, :].rearrange("p (n h q) -> p n h q", n=2, h=8),
                                         in_=qrT0.ap[0:64, :].rearrange("p (h n q) -> p n h q", h=8, n=2)), r=[qrT0], w=[qrT])
        E("dve", lambda e: e.memset(qrT.ap[64:128, :], 0.0), w=[qrT])
        wbk, wvk = wload(s_ukT, s_ukT.ap.rearrange("h d c -> d h c"), [128, 8, 256])
        qlT = Ft[3]
        qlT_b = Buf(qlT.ap.bitcast(BF16), None)
        qlT_b.lastw, qlT_b.reads = qlT.lastw, qlT.reads
        for cc in range(2):
            for h in range(8):
                bank = P[cc * 2 + h // 4]
                E("pe", lambda e, cc=cc, h=h, bank=bank: e.matmul(bank.ap[:, (h % 4) * 128:(h % 4 + 1) * 128],
                                                                   lhsT=wvk[:, h, cc * 128:(cc + 1) * 128], rhs=qnT.ap[:, h * 128:(h + 1) * 128],
                                                                   start=True, stop=True), r=[wbk, qnT], w=[bank])
            for hh in range(2):
                bank = P[cc * 2 + hh]
                dst = qlT.ap.bitcast(BF16)[:, cc * 1024:(cc + 1) * 1024].rearrange("p (n h q) -> p n h q", n=2, h=8)[:, :, hh * 4:(hh + 1) * 4, :]
                src = bank.ap.rearrange("p (h n q) -> p n h q", h=4, n=2)
                E("dve", lambda e, dst=dst, src=src: e.tensor_copy(out=dst, in_=src), r=[bank], w=[qlT])
        proj(hTb, 8, s_in0, wcols(s_in0.ap, 704, 512), 512, P[1], kstep=8)
        xqb, xqT = Ht[4], Ht[5]
        E("dve", lambda e: e.tensor_copy(out=xqb.ap[:, 0:512], in_=P[1].ap), r=[P[1]], w=[xqb])
        transposes(xqT, lambda i: xqT.ap[:, i * 128:(i + 1) * 128], xqb, lambda i: xqb.ap[:, i * 128:(i + 1) * 128], 4, whole=xqT.ap[:, 0:4 * 128])
        olT = Ft[4]
        olT_bf = olT.ap.bitcast(BF16)
        qlT_v = Buf(qlT.ap.bitcast(BF16))
        for n in range(2):
            osb = olT
            qlT_v.lastw, qlT_v.reads = qlT.lastw, qlT.reads
            if t < NTP:
                def kfn_p(j, k0, nk):
                    r_ = kT_rows[j]
                    gidx = k0 // 512
                    ip, rk = gidx // 8, gidx % 8
                    off = ip * 512 + (k0 % 512)
                    return kT_all[j].ap[rk * r_:(rk + 1) * r_, off:off + nk]

                def vfn_p(k0, nk):
                    gidx = k0 // 512
                    ip, rk = gidx // 8, gidx % 8
                    loc = ip * 512 + (k0 % 512)
                    p_, i_ = loc // 1024, loc % 1024
                    return v_all[p_], v_all[p_].ap[rk * 1024 + i_:rk * 1024 + i_ + nk, :]
                attend(qlT_v, qrT, n, kT_all, kfn_p, vfn_p, 4096 * (t // 4 + 1), t * 2 + n, osb, olT_bf)
            else:
                sq = (t - NTP) * 2 + n
                attend(qlT_v, qrT, n, [kT_s[sq]] * 3,
                       lambda j, k0, nk, sq=sq: kT_s[sq].ap[j * 128:j * 128 + (128 if j < 2 else 64), k0:k0 + nk],
                       lambda k0, nk, sq=sq: (v_s[sq], v_s[sq].ap[k0:k0 + nk, :]), 1088, None, osb, olT_bf)
            qlT.reads = qlT_v.reads
        wbv, wvv = wload(s_uv, s_uv.ap.rearrange("(k p) n -> p k n", p=128), [128, 2, 1024])
        mixT = Ht[2]
        for h in range(8):
            bank = P[h // 4]
            for cc in range(2):
                rhs = olT_bf[:, cc * 1024:(cc + 1) * 1024].rearrange("p (n h q) -> p n h q", n=2, h=8)[:, :, h, :]
                E("pe", lambda e, h=h, cc=cc, bank=bank, rhs=rhs: e.matmul(
                    bank.ap[:, (h % 4) * 128:(h % 4 + 1) * 128].rearrange("p (n q) -> p n q", n=2),
                    lhsT=wvv[:, cc, h * 128:(h + 1) * 128], rhs=rhs, start=(cc == 0), stop=(cc == 1)), r=[wbv, olT], w=[bank])
        for hh in range(2):
            E("dve", lambda e, hh=hh: e.tensor_copy(out=mixT.ap[:, hh * 512:(hh + 1) * 512], in_=P[hh].ap), r=[P[hh]], w=[mixT])
        crossT = Ht[3]
        if t < NTP:
            segs = [(0, 128, 0)]
        else:
            for hs in range(2):
                sq = (t - NTP) * 2 + hs
                build_memset(hs, c_mk[0, sq], c_mv[0, sq], wsem)
            segs = [(0, 64, 0), (64, 64, 1)]
        cross_attn(xqT, segs, crossT)
        out_and_ffn(0, t, xb, mixT, crossT, s_out0, last=False)

    Dm = hc.ap[:, 0:128]
    maskT = hc.ap[:, 128:256]
    Ind = hc.ap[:, 256:260]

    def l1_tail(t, xb, osb_, gate, xqb):
        osq = Ft[1]
        E("dve", lambda e: e.tensor_tensor(out=osq.ap, in0=osb_.ap, in1=osb_.ap, op=ALU.mult), r=[osb_], w=[osq])
        r8 = small[3]
        E("dve", lambda e: e.tensor_reduce(out=r8.ap[:, 0:8], in_=osq.ap.rearrange("p (h v) -> p h v", h=8), axis=AX.X, op=ALU.add), r=[osq], w=[r8])
        E("act", lambda e: e.activation(out=r8.ap[:, 0:8], in_=r8.ap[:, 0:8], func=AF.Ln, bias=epsb.ap, scale=1.0 / 128), r=[r8, epsb], w=[r8])
        E("act", lambda e: e.activation(out=r8.ap[:, 0:8], in_=r8.ap[:, 0:8], func=AF.Exp, scale=-0.5), r=[r8], w=[r8])
        E("dve", lambda e: e.tensor_tensor(out=osb_.ap.rearrange("p (h v) -> p h v", h=8), in0=osb_.ap.rearrange("p (h v) -> p h v", h=8),
                                           in1=r8.ap[:, 0:8].unsqueeze(2).to_broadcast([128, 8, 128]), op=ALU.mult), r=[osb_, r8], w=[osb_])
        E("dve", lambda e: e.tensor_tensor(out=osb_.ap, in0=osb_.ap, in1=on_b.ap, op=ALU.mult), r=[osb_, on_b], w=[osb_])
        mixb, mixT = Ht[5], Ht[2]
        E("dve", lambda e: e.tensor_tensor(out=mixb.ap, in0=osb_.ap, in1=gate.ap, op=ALU.mult), r=[osb_, gate], w=[mixb])
        transposes(mixT, lambda i: mixT.ap[:, i * 128:(i + 1) * 128], mixb, lambda i: mixb.ap[:, i * 128:(i + 1) * 128], 8, whole=mixT.ap[:, 0:8 * 128])
        xqT, crossT = Ht[6], Ht[3]
        transposes(xqT, lambda i: xqT.ap[:, i * 128:(i + 1) * 128], xqb, lambda i: xqb.ap[:, i * 128:(i + 1) * 128], 4, whole=xqT.ap[:, 0:4 * 128])
        if t < NTP:
            segs = [(0, 128, 0)]
        else:
            for hs in range(2):
                sq = (t - NTP) * 2 + hs
                build_memset(hs, c_mk[1, sq], c_mv[1, sq], wsem)
            segs = [(0, 64, 0), (64, 64, 1)]
        cross_attn(xqT, segs, crossT)
        out_and_ffn(1, t, xb, mixT, crossT, s_out1, last=True)

    def l1_back(t):
        xb = front(t, 1, 0)
        osb_, gate, qcT, xqb, Sr = Ft[0], Ft[2], Ht[7], Ht[4], Ht[8]
        rr = slice(t * 128, (t + 1) * 128)
        k.dma("sp", osb_.ap, st_o.ap[rr, :], w=osb_, r=st_o)
        k.dma("sp", gate.ap, st_g.ap[rr, :], w=gate, r=st_g)
        k.dma("sp", qcT.ap, st_q.ap[rr, :], w=qcT, r=st_q)
        k.dma("sp", xqb.ap[:, 0:512], st_x.ap[rr, :], w=xqb, r=st_x)
        Sin = Ft[3]
        if t % 4 == 0:
            k.dma("sp", Sin.ap, snapd[t // 4].ap, w=Sin, r=snapd[t // 4])
        E("dve", lambda e: e.tensor_copy(out=Sr.ap, in_=Sin.ap), r=[Sin], w=[Sr])
        for h in range(8):
            bank = P[h // 4]
            E("pe", lambda e, h=h, bank=bank: e.matmul(bank.ap[:, (h % 4) * 128:(h % 4 + 1) * 128], lhsT=qcT.ap[:, h * 128:(h + 1) * 128],
                                                       rhs=Sr.ap[:, h * 128:(h + 1) * 128], start=True, stop=True), r=[qcT, Sr], w=[bank])
        for hh in range(2):
            E("dve", lambda e, hh=hh: e.tensor_tensor(out=osb_.ap[:, hh * 512:(hh + 1) * 512], in0=P[hh].ap, in1=osb_.ap[:, hh * 512:(hh + 1) * 512],
                                                      op=ALU.add), r=[P[hh], osb_], w=[osb_])
        l1_tail(t, xb, osb_, gate, xqb)

    def l1_tile(t, mode):
        passA = False
        frontm = (mode == "front")
        xb = front(t, 1, 0)
        if frontm:
            prefetch_next()
        hb, hTb = Ht[0], Ht[1]
        norm_T(xb, 0, hb, hTb)
        qs, sg, gate, logf, kk = Ft[0], Ft[1], Ft[2], Ft[3], Ft[4]
        vb, xqb = Ht[2], Ht[4]
        def do_groups(groups):
          for gi in groups:
              bank = P[gi % 4]
              proj(hTb, 8, s_in1, wcols(s_in1.ap, gi * 512, 512), 512, bank, kstep=8)
              c0 = (gi % 2) * 512
              if gi < 2:
                  E("act", lambda e, bank=bank, c0=c0: e.activation(out=qs.ap[:, c0:c0 + 512], in_=bank.ap, func=AF.Silu), r=[bank], w=[qs])
              elif gi < 4:
                  E("act", lambda e, bank=bank, c0=c0: e.activation(out=sg.ap[:, c0:c0 + 512], in_=bank.ap, func=AF.Sigmoid), r=[bank], w=[sg])
              elif gi < 6:
                  E("dve", lambda e, bank=bank, c0=c0: e.tensor_copy(out=vb.ap[:, c0:c0 + 512], in_=bank.ap), r=[bank], w=[vb])
              elif gi < 8:
                  E("act", lambda e, bank=bank, c0=c0: e.activation(out=gate.ap[:, c0:c0 + 512], in_=bank.ap, func=AF.Silu), r=[bank], w=[gate])
              else:
                  E("dve", lambda e, bank=bank: e.tensor_copy(out=xqb.ap[:, 0:512], in_=bank.ap), r=[bank], w=[xqb])
        do_groups([2, 3, 4, 5, 0, 1])
        E("dve", lambda e: e.tensor_tensor(out=sg.ap, in0=sg.ap, in1=oml.ap, op=ALU.mult), r=[sg, oml], w=[sg])
        E("dve", lambda e: e.tensor_tensor(out=logf.ap, in0=sg.ap, in1=lbb.ap, op=ALU.add), r=[sg, lbb], w=[logf])
        E("act", lambda e: e.activation(out=logf.ap, in_=logf.ap, func=AF.Ln), r=[logf], w=[logf])
        E("dve", lambda e: e.tensor_tensor(out=kk.ap, in0=oml.ap, in1=sg.ap, op=ALU.subtract), r=[sg, oml], w=[kk])
        E1 = Ft[5]
        ke, qe = Ht[5], Ht[6]
        for g in range(2):
            E("pe", lambda e, g=g: e.matmul(P[4 + g].ap, lhsT=Dm, rhs=logf.ap[:, g * 512:(g + 1) * 512], start=True, stop=True),
              r=[hc, logf], w=[P[4 + g]])
        for g in range(2):
            E("act", lambda e, g=g: e.activation(out=E1.ap[:, g * 512:(g + 1) * 512], in_=P[4 + g].ap, func=AF.Exp, scale=-1.0), r=[P[4 + g]], w=[E1])
        E("dve", lambda e: e.tensor_tensor(out=ke.ap, in0=kk.ap, in1=E1.ap, op=ALU.mult), r=[kk, E1], w=[ke])
        if not passA:
            for g in range(2):
                E("act", lambda e, g=g: e.activation(out=E1.ap[:, g * 512:(g + 1) * 512], in_=P[4 + g].ap, func=AF.Exp, scale=1.0), r=[P[4 + g]], w=[E1])
            E("dve", lambda e: e.tensor_tensor(out=qe.ap, in0=qs.ap, in1=E1.ap, op=ALU.mult), r=[qs, E1], w=[qe])
        st = P[4]
        for h in range(8):
            E("pe", lambda e, h=h: e.matmul(st.ap[:, h * 4:(h + 1) * 4], lhsT=logf.ap[:, h * 128:(h + 1) * 128], rhs=Ind, start=True, stop=True),
              r=[logf, hc], w=[st])
        cs = small[5]
        er = small[6]
        el = small[7]
        E("dve", lambda e: e.tensor_copy(out=stsb.ap, in_=st.ap[:, 0:32]), r=[st], w=[stsb])
        st3 = stsb.ap.rearrange("p (h f) -> p h f", h=8)
        E("act", lambda e: e.activation(out=cs.ap.rearrange("p (h n) -> p h n", h=8), in_=st3[:, :, 0:2], func=AF.Exp), r=[stsb], w=[cs])
        E("act", lambda e: e.activation(out=er.ap.rearrange("p (h n) -> p h n", h=8), in_=st3[:, :, 2:4], func=AF.Exp), r=[stsb], w=[er])
        E("dve", lambda e: e.tensor_tensor(out=el.ap.rearrange("p (h n) -> p h n", h=8), in0=st3[:, :, 0:2], in1=st3[:, :, 2:4], op=ALU.subtract),
          r=[stsb], w=[el])
        E("act", lambda e: e.activation(out=el.ap, in_=el.ap, func=AF.Exp), r=[el], w=[el])
        do_groups([6, 7, 8])
        if not passA:
            qeT, keT = Ht[7], Ht[8]
            transposes(qeT, lambda i: qeT.ap[:, i * 128:(i + 1) * 128], qe, lambda i: qe.ap[:, i * 128:(i + 1) * 128], 8, whole=qeT.ap[:, 0:8 * 128])
            transposes(keT, lambda i: keT.ap[:, i * 128:(i + 1) * 128], ke, lambda i: ke.ap[:, i * 128:(i + 1) * 128], 8, whole=keT.ap[:, 0:8 * 128])
            aTm = Ht[9]
            for h in range(8):
                bank = P[4 + h // 4]
                E("pe", lambda e, h=h, bank=bank: e.matmul(bank.ap[:, (h % 4) * 128:(h % 4 + 1) * 128], lhsT=keT.ap[:, h * 128:(h + 1) * 128],
                                                           rhs=qeT.ap[:, h * 128:(h + 1) * 128], start=True, stop=True), r=[keT, qeT], w=[bank])
            for hh in range(2):
                bank = P[4 + hh]
                E("dve", lambda e, hh=hh, bank=bank: e.tensor_tensor(out=aTm.ap[:, hh * 512:(hh + 1) * 512].rearrange("p (h t) -> p h t", h=4),
                                                                     in0=bank.ap.rearrange("p (h t) -> p h t", h=4),
                                                                     in1=maskT.unsqueeze(1).to_broadcast([128, 4, 128]), op=ALU.mult), r=[bank, hc], w=[aTm])
        S3 = Sst.ap.rearrange("p (h v) -> p h v", h=8)
        coef = small[4]
        if frontm and t % 4 == 0:
            E("dve", lambda e: e.memset(Sst.ap, 0.0), w=[Sst])
            E("dve", lambda e: e.memset(Dacc.ap, 1.0), w=[Dacc])
        for n in range(2):
            sq = (t - NTP) * 2 + n
            if t >= NTP:
                k.dma("sp", S3, c_st[sq].rearrange("h k v -> k h v"), w=Sst)
            r0 = n * 64
            for h in range(8):
                bank = P[2 + h // 4]
                E("pe", lambda e, h=h, bank=bank, r0=r0: e.matmul(bank.ap[:, (h % 4) * 128:(h % 4 + 1) * 128], lhsT=ke.ap[r0:r0 + 64, h * 128:(h + 1) * 128],
                                                                  rhs=vb.ap[r0:r0 + 64, h * 128:(h + 1) * 128], start=True, stop=True), r=[ke, vb], w=[bank])
            if not passA:
                Sr = Ht[10 + n]
                E("dve", lambda e, n=n, Sr=Sr: e.tensor_tensor(out=Sr.ap.rearrange("p (h v) -> p h v", h=8), in0=S3,
                                                                in1=er.ap.rearrange("p (h n) -> p h n", h=8)[:, :, n:n + 1].to_broadcast([128, 8, 128]),
                                                                op=ALU.mult), r=[Sst, er], w=[Sr])
                for h in range(8):
                    bank = P[h // 4]
                    oo = bank.ap[r0:r0 + 64, (h % 4) * 128:(h % 4 + 1) * 128]
                    E("pe", lambda e, h=h, oo=oo, r0=r0: e.matmul(oo, lhsT=aTm.ap[:, h * 128 + r0:h * 128 + r0 + 64], rhs=vb.ap[:, h * 128:(h + 1) * 128],
                                                                  start=True, stop=False), r=[aTm, vb], w=[bank])
                    E("pe", lambda e, h=h, oo=oo, r0=r0, Sr=Sr: e.matmul(oo, lhsT=qeT.ap[:, h * 128 + r0:h * 128 + r0 + 64], rhs=Sr.ap[:, h * 128:(h + 1) * 128],
                                                                         start=False, stop=True), r=[qeT, Sr], w=[bank])
            if frontm:
                E("dve", lambda e, n=n: e.tensor_tensor(out=coef.ap.rearrange("p (h n) -> p h n", h=8)[:, :, n],
                                                        in0=er.ap.rearrange("p (h n) -> p h n", h=8)[:, :, n], in1=Dacc.ap, op=ALU.mult),
                  r=[er, Dacc], w=[coef])
            t1 = Ft[5]
            for hh in range(2):
                E("dve", lambda e, hh=hh, n=n: e.tensor_tensor(
                    out=t1.ap[:, hh * 512:(hh + 1) * 512].rearrange("p (h v) -> p h v", h=4), in0=P[2 + hh].ap.rearrange("p (h v) -> p h v", h=4),
                    in1=el.ap.rearrange("p (h n) -> p h n", h=8)[:, hh * 4:(hh + 1) * 4, n:n + 1].to_broadcast([128, 4, 128]), op=ALU.mult),
                  r=[P[2 + hh], el], w=[t1])
            E("dve", lambda e, n=n: e.tensor_tensor(out=S3, in0=S3, in1=cs.ap.rearrange("p (h n) -> p h n", h=8)[:, :, n:n + 1].to_broadcast([128, 8, 128]),
                                                    op=ALU.mult), r=[Sst, cs], w=[Sst])
            E("dve", lambda e: e.tensor_tensor(out=Sst.ap, in0=Sst.ap, in1=t1.ap, op=ALU.add), r=[Sst, t1], w=[Sst])
            if frontm:
                E("dve", lambda e, n=n: e.tensor_tensor(out=Dacc.ap, in0=Dacc.ap, in1=cs.ap.rearrange("p (h n) -> p h n", h=8)[:, :, n], op=ALU.mult),
                  r=[Dacc, cs], w=[Dacc])
            if t >= NTP:
                k.dma("pool", sts_o[sq].rearrange("h k v -> k h v"), S3, r=Sst, final=True)
        osb_ = Ft[0]
        for hh in range(2):
            E("dve", lambda e, hh=hh: e.tensor_copy(out=osb_.ap[:, hh * 512:(hh + 1) * 512], in_=P[hh].ap), r=[P[hh]], w=[osb_])
        if frontm:
            if t % 4 == 3:
                sg_ = t // 4
                E("dve", lambda e: e.tensor_copy(out=exD_sb.ap[:, sg_ * 8:(sg_ + 1) * 8], in_=Dacc.ap), r=[Dacc], w=[exD_sb])
                k.dma("pool", exU_own[sg_].ap, Sst.ap, w=exU_own[sg_], r=Sst)
            qcT = Ht[3]
            E("dve", lambda e: e.tensor_tensor(out=qcT.ap.rearrange("p (h n q) -> p h n q", h=8, n=2),
                                               in0=qeT.ap.rearrange("p (h n q) -> p h n q", h=8, n=2),
                                               in1=coef.ap.rearrange("p (h n) -> p h n", h=8).unsqueeze(3).to_broadcast([128, 8, 2, 64]), op=ALU.mult),
              r=[qeT, coef], w=[qcT])
            rr = slice(t * 128, (t + 1) * 128)
            k.dma("pool", st_q.ap[rr, :], qcT.ap, w=st_q, r=qcT)
            k.dma("pool", st_o.ap[rr, :], osb_.ap, w=st_o, r=osb_)
            k.dma("pool", st_g.ap[rr, :], gate.ap, w=st_g, r=gate)
            k.dma("pool", st_x.ap[rr, :], xqb.ap[:, 0:512], w=st_x, r=xqb)
            return
        l1_tail(t, xb, osb_, gate, xqb)

    sample_cache_prep_v()
    cast_rest()
    phase0()
    load_gains(0)
    cits = sample_cache_iters()
    for t in range(NT):
        bg_run(5)
        l0_kpart(t)
        for _ in range(2):
            if cits:
                cits.pop(0)()
    while cits:
        cits.pop(0)()
    bg_run(len(bg))
    for j in range(3):
        k.allgather(G4, kT_own[j], kT_g4[j])
    for j in range(2):
        k.allgather(G4, v_own[j], v_g4[j])
    cast_layer1()
    want_hT[0] = True
    order0 = list(range(NTP, NT)) + list(range(NTP))
    nxt0 = {order0[i]: ((order0[i + 1], 0) if i + 1 < len(order0) else (0, 1)) for i in range(len(order0))}
    for t in range(NTP, NT):
        nexttile[0] = nxt0[t]
        l0_tile(t)
    for j in range(3):
        k.allgather(G2, kT_g4[j], kT_all[j])
    for j in range(2):
        k.allgather(G2, v_g4[j], v_all[j])
    build_memset(0, memkv_s.ap[0, :, 0:512], memkv_s.ap[0, :, 512:1024], memkv_s)
    for t in range(NTP):
        nexttile[0] = nxt0[t]
        bg_run(4)
        l0_tile(t)
    bg_run(len(bg))
    want_hT[0] = False
    flush_ffn()
    load_gains(1)
    for t in range(NTP):
        nexttile[0] = (t + 1, 1)
        l1_tile(t, "front")
        if t % 4 == 3:
            sg_ = t // 4
            k.allgather(G4, exU_own[sg_], exU_g4[sg_])
            if sg_ >= 1:
                k.allgather(G2, exU_g4[sg_ - 1], exU_all[sg_ - 1])
    k.dma("pool", exD_own.ap, exD_sb.ap, w=exD_own, r=exD_sb)
    k.allgather(G4, exD_own, exD_g4)
    nexttile[0] = (NTP + 1, 1)
    l1_tile(NTP, "single")
    k.allgather(G2, exU_g4[3], exU_all[3])
    k.allgather(G2, exD_g4, exD_all)
    nexttile[0] = (0, 1)
    l1_tile(NTP + 1, "single")
    k.dma("sp", exDa_sb.ap, exD_all.ap.rearrange("(r p) f -> p r f", p=128), w=exDa_sb, r=exD_all)
    E("dve", lambda e: e.memset(Sst.ap, 0.0), w=[Sst])
    S3 = Sst.ap.rearrange("p (h v) -> p h v", h=8)
    snap = Ft[0]
    build_memset(0, memkv_s.ap[1, :, 0:512], memkv_s.ap[1, :, 512:1024], memkv_s)
    for sg_ in range(4):
        E("dve", lambda e: e.memset(snap.ap, 0.0), w=[snap])
        for r in range(8):
            eb = exb[r % 2]
            k.dma("sp", eb.ap[:, 0:1024], exU_all[sg_].ap[r * 128:(r + 1) * 128, :], w=eb, r=exU_all[sg_])
            m = scm.ap[:, r:r + 1]
            E("dve", lambda e, m=m: e.scalar_tensor_tensor(out=snap.ap, in0=Sst.ap, scalar=m, in1=snap.ap, op0=ALU.mult, op1=ALU.add),
              r=[Sst, scm, snap], w=[snap])
            E("dve", lambda e, r=r, sg_=sg_: e.tensor_tensor(out=S3, in0=S3,
                                                            in1=exDa_sb.ap[:, r, sg_ * 8:(sg_ + 1) * 8].unsqueeze(2).to_broadcast([128, 8, 128]), op=ALU.mult),
              r=[Sst, exDa_sb], w=[Sst])
            E("dve", lambda e, eb=eb: e.tensor_tensor(out=Sst.ap, in0=Sst.ap, in1=eb.ap[:, 0:1024], op=ALU.add), r=[Sst, eb], w=[Sst])
        k.dma("pool", snapd[sg_].ap, snap.ap, w=snapd[sg_], r=snap)
        if sg_ == 3:
            k.dma("pool", stp_o.rearrange("h k v -> k h v"), S3, r=Sst, final=True)
        for t in range(4 * sg_, 4 * sg_ + 4):
            nexttile[0] = (t + 1, 1) if t + 1 < NTP else None
            l1_back(t)
    flush_ffn()
    k.finish()
    return nc


_ROPE_THETA = 10000.0


def _tok_base(c, t):
    return 512 * (8 * (t // 4) + c) + 128 * (t % 4)


def _tables(c):
    inv = np.power(np.float32(_ROPE_THETA), -np.arange(32, dtype=np.float32) / np.float32(32)).astype(np.float32)
    pos_p = np.concatenate([_tok_base(c, t) + np.arange(128) for t in range(NTP)]).astype(np.float32)
    pos_s = (1024 + np.arange(64)).astype(np.float32)
    pos = np.concatenate([pos_p] + [pos_s] * 4)
    ang = (pos[:, None] * inv[None, :]).astype(np.float32)
    cos = np.tile(np.cos(ang).astype(np.float32), (1, 8))
    sin = np.tile(np.sin(ang).astype(np.float32), (1, 8))
    qc = np.arange(32)[:, None, None]
    p = np.arange(128)[None, :, None]
    vb = np.arange(128)[None, None, :]
    gq = (8 * (qc // 8) + c) * 8 + qc % 8
    grp = vb // 4
    kchunk = (8 * (grp // 8) + grp % 8) * 8 + 2 * (vb % 4) + p // 64
    vis = kchunk <= gq
    maskb = np.where(vis, 0.0, NEGB).astype(np.float32)
    scanm = np.tile((np.arange(8) == c).astype(np.float32)[None, :], (128, 1))
    s = np.arange(128)[:, None]
    t = np.arange(128)[None, :]
    same = (s // 64) == (t // 64)
    tri = (same & (s <= t)).astype(np.float32)
    triref = (same & ((s % 64) <= 32)).astype(np.float32)
    Dm = tri - triref
    ind = np.zeros((128, 4), np.float32)
    sl = np.arange(128)
    ind[:, 0] = (sl < 64)
    ind[:, 1] = (sl >= 64)
    ind[:, 2] = (sl < 64) & (sl % 64 <= 32)
    ind[:, 3] = (sl >= 64) & (sl % 64 <= 32)
    hconst = np.concatenate([Dm, tri, ind], axis=1).astype(np.float32)
    return cos, sin, maskb, scanm, hconst


def kernel(x_prompt, x_sample, cache_mla_latent, cache_mla_krope, cache_hgrn_state,
           cache_mem_k, cache_mem_v, mem_prompt,
           ln_mix_pre, ln_mix_post, ln_ffn_pre, ln_ffn_post, mem_norm, w_mem_kv,
           mla_w_in, mla_q_norm, mla_kv_norm, mla_w_uq, mla_w_uk, mla_w_uv, mla_w_out,
           hgrn_w_in, hgrn_lb, hgrn_o_norm, hgrn_w_out, w_ffn_up, w_ffn_down):
    f = lambda a: np.ascontiguousarray(np.asarray(a, dtype=np.float32))
    x_prompt, x_sample = f(x_prompt), f(x_sample)
    gains = np.stack([f(ln_mix_pre)[0], f(ln_mix_pre)[1], f(ln_mix_post)[0], f(ln_mix_post)[1],
                      f(ln_ffn_pre)[0], f(ln_ffn_pre)[1], f(ln_ffn_post)[0], f(ln_ffn_post)[1],
                      f(mem_norm)[0], f(mem_norm)[1]], axis=0)
    shared = {
        "memp": f(mem_prompt)[0], "gains": f(gains), "qn_g": f(mla_q_norm)[0], "kvn_g": f(mla_kv_norm)[0],
        "on_g": f(np.tile(f(hgrn_o_norm)[0], 8)), "hlb": f(hgrn_lb), "w_memkv": f(w_mem_kv), "w_in0": f(mla_w_in)[0],
        "w_uq": f(mla_w_uq)[0], "w_ukT": f(np.transpose(f(mla_w_uk)[0], (1, 2, 0))),
        "w_uv": f(f(mla_w_uv)[0].reshape(256, 1024)), "w_out0": f(mla_w_out)[0], "w_in1": f(hgrn_w_in)[0],
        "w_out1": f(hgrn_w_out)[0], "w_up": f(w_ffn_up), "w_dn": f(w_ffn_down),
    }
    in_maps = []
    for c in range(NCORE):
        cos, sin, maskb, scanm, hconst = _tables(c)
        sl = slice(4 * c, 4 * c + 4)
        m = dict(shared)
        m["xin"] = f(np.concatenate([x_prompt[0, _tok_base(c, t):_tok_base(c, t) + 128] for t in range(NTP)]
                                    + [x_sample[sl].reshape(256, D)], axis=0))
        m["c_lat"] = f(np.asarray(cache_mla_latent)[0, sl])
        m["c_kr"] = f(np.asarray(cache_mla_krope)[0, sl])
        m["c_st"] = f(np.asarray(cache_hgrn_state)[0, sl])
        m["c_mk"] = f(np.asarray(cache_mem_k)[:, sl].reshape(2, 4, 256, 512))
        m["c_mv"] = f(np.asarray(cache_mem_v)[:, sl].reshape(2, 4, 256, 512))
        m["cos_t"], m["sin_t"], m["maskb"], m["scanm"], m["hconst"] = cos, sin, maskb, scanm, hconst
        in_maps.append(m)
    nc = build()
    res = run_bass_kernel_spmd(nc, in_maps, core_ids=list(range(NCORE)))
    R = res.results
    y_p = np.zeros((1, 16384, D), np.float32)
    lat_p = np.zeros((1, 1, 16384, 256), np.float32)
    kr_p = np.zeros((1, 1, 16384, 64), np.float32)
    for c in range(NCORE):
        for t in range(NTP):
            b0 = _tok_base(c, t)
            y_p[0, b0:b0 + 128] = R[c]["y_o"][t * 128:(t + 1) * 128]
            lat_p[0, 0, b0:b0 + 128] = R[c]["lat_o"][t * 128:(t + 1) * 128]
            kr_p[0, 0, b0:b0 + 128] = R[c]["kr_o"][t * 128:(t + 1) * 128]
    y_s = np.concatenate([R[c]["y_o"][2048:2304].reshape(4, 64, D) for c in range(NCORE)], 0)
    st_p = R[NCORE - 1]["stp_o"][None, None]
    mk_p = R[0]["mk_o"].reshape(2, 1, 256, 4, 128)
    mv_p = R[0]["mv_o"].reshape(2, 1, 256, 4, 128)
    lat_s = np.concatenate([R[c]["lat_o"][2048:2304].reshape(4, 64, 256) for c in range(NCORE)], 0)[None]
    kr_s = np.concatenate([R[c]["kr_o"][2048:2304].reshape(4, 64, 64) for c in range(NCORE)], 0)[None]
    st_s = np.concatenate([R[c]["sts_o"] for c in range(NCORE)], 0)[None]
    o = lambda a: np.ascontiguousarray(a, dtype=np.float32)
    return (o(y_p), o(y_s), o(lat_p), o(kr_p), o(st_p), o(mk_p), o(mv_p), o(lat_s), o(kr_s), o(st_s))
```

```python
import numpy as np
import concourse.bass as bass
import concourse.mybir as mybir
from concourse.bass_utils import run_bass_kernel_spmd

F32 = mybir.dt.float32
BF16 = mybir.dt.bfloat16
AF = mybir.ActivationFunctionType
ALU = mybir.AluOpType
AX = mybir.AxisListType
NCORE = 8
D = 1024
NTP = 16
NTS = 2
NT = NTP + NTS
EPS = 1e-6
NEGB = -30000.0
G4 = [[0, 1, 2, 3], [4, 5, 6, 7]]
G2 = [[0, 4], [1, 5], [2, 6], [3, 7]]


class Buf:
    def __init__(self, ap, sem=None):
        self.ap = ap
        self.sem = sem
        self.qs = {}
        self.lastw = {}
        self.reads = {}


class QSem:
    def __init__(self, sem):
        self.sem = sem
        self.tot = 0

    def __getitem__(self, key):
        return self.ap[key]


def _merge(d, evs):
    for key, ev in evs.items():
        if key not in d or d[key][1] < ev[1]:
            d[key] = ev


class KB:
    def __init__(self, nc):
        self.nc = nc
        self.E = dict(pe=nc.tensor, act=nc.scalar, dve=nc.vector, pool=nc.gpsimd, sp=nc.sync)
        self.psem = {e: nc.alloc_semaphore("prog_" + e) for e in ("pe", "act", "dve", "pool")}
        self.cnt = {e: 0 for e in self.psem}
        self.known = {e: {} for e in self.E}
        self.n = 0
        self.ccsem = nc.alloc_semaphore("ccsem")
        self.cccnt = 0
        self.outs = {}

    def _name(self, p):
        self.n += 1
        return "%s%d" % (p, self.n)

    def sb(self, shape, dt=F32, dma=False):
        t = self.nc.alloc_sbuf_tensor(self._name("sb"), list(shape), dt)
        return Buf(t.ap(), True if dma else None)

    def ps(self, shape, dt=F32):
        t = self.nc.alloc_psum_tensor(self._name("ps"), list(shape), dt)
        return Buf(t.ap())

    def dram(self, shape, dt=F32, dma=False):
        t = self.nc.dram_tensor(self._name("dr"), list(shape), dt)
        return Buf(t.ap(), True if dma else None)

    def _wait(self, eng, evs, skip=None):
        for key, ev in evs.items():
            sem, val = ev[0], ev[1]
            if skip is not None and key == skip:
                continue
            if len(ev) > 2 and ev[2] is not None:
                val = max(val, ev[2].tot)
            if self.known[eng].get(key, 0) >= val:
                continue
            self.E[eng].wait_ge(sem, val)
            self.known[eng][key] = val

    def op(self, eng, fn, r=(), w=()):
        deps = {}
        for b in r:
            _merge(deps, b.lastw)
        for b in w:
            _merge(deps, b.lastw)
            _merge(deps, b.reads)
        self._wait(eng, deps, skip=("p", "pe") if eng == "pe" else None)
        ins = fn(self.E[eng])
        self.cnt[eng] += 1
        ins.then_inc(self.psem[eng], 1)
        ev = {("p", eng): (self.psem[eng], self.cnt[eng])}
        for b in w:
            b.lastw = dict(ev)
            b.reads = {}
        for b in r:
            _merge(b.reads, ev)

    def dma(self, q, out, in_, w=None, r=None, final=False):
        deps = {}
        if r is not None:
            _merge(deps, r.lastw)
        if w is not None:
            _merge(deps, {kk: vv for kk, vv in w.lastw.items() if kk[0] != "d"})
            _merge(deps, w.reads)
        self._wait(q, deps)
        sb = w if (w is not None and w.sem is not None) else r
        assert sb is not None and sb.sem is not None
        if q not in sb.qs:
            sb.qs[q] = QSem(self.nc.alloc_semaphore(self._name("dq")))
        qs = sb.qs[q]
        self.E[q].dma_start(out=out, in_=in_).then_inc(qs.sem, 16)
        qs.tot += 16
        ev = {("d", id(qs)): (qs.sem, qs.tot, qs)}
        if w is not None:
            _merge(w.lastw, ev)
            w.reads = {}
        if r is not None:
            _merge(r.reads, ev)
        if final:
            _merge(self.outs, ev)

    def allgather(self, groups, src, dst):
        deps = {}
        _merge(deps, src.lastw)
        _merge(deps, dst.lastw)
        _merge(deps, dst.reads)
        _merge(deps, getattr(self, "lastcc", {}))
        self._wait("pool", deps)
        sem = self.nc.alloc_semaphore(self._name("cc"))
        self.nc.gpsimd.collective_compute("AllGather", ALU.bypass, replica_groups=groups,
                                          ins=[src.ap], outs=[dst.ap]).then_inc(sem)
        ev = {("cc", id(sem)): (sem, 1)}
        self.lastcc = dict(ev)
        dst.lastw = dict(ev)
        dst.reads = {}
        _merge(src.reads, ev)

    def finish(self):
        self._wait("pool", self.outs)


def build():
    nc = bass.Bass("TRN2", target_bir_lowering=False)
    k = KB(nc)

    def din(name, shape, dt=F32):
        return nc.dram_tensor(name, list(shape), dt, kind="ExternalInput").ap()

    def dout(name, shape):
        return nc.dram_tensor(name, list(shape), F32, kind="ExternalOutput").ap()

    xin = din("xin", [NT * 128, D])
    c_lat = din("c_lat", [4, 1024, 256])
    c_kr = din("c_kr", [4, 1024, 64])
    c_st = din("c_st", [4, 8, 128, 128])
    c_mk = din("c_mk", [2, 4, 256, 512])
    c_mv = din("c_mv", [2, 4, 256, 512])
    memp = din("memp", [256, D])
    gains = din("gains", [10, D])
    qn_g = din("qn_g", [384])
    kvn_g = din("kvn_g", [256])
    on_g = din("on_g", [1024])
    hlb = din("hlb", [2, 1024])
    w_memkv = din("w_memkv", [2, 1024, 1024])
    w_in0 = din("w_in0", [1024, 1216])
    w_uq = din("w_uq", [384, 1536])
    w_ukT = din("w_ukT", [8, 128, 256])
    w_uv = din("w_uv", [256, 1024])
    w_out0 = din("w_out0", [1536, 1024])
    w_in1 = din("w_in1", [1024, 4608])
    w_out1 = din("w_out1", [1536, 1024])
    w_up = din("w_up", [2, 1024, 4096])
    w_dn = din("w_dn", [2, 4096, 1024])
    cos_t = din("cos_t", [NT * 128, 256])
    sin_t = din("sin_t", [NT * 128, 256])
    maskb = din("maskb", [32, 128, 128])
    scanm = din("scanm", [128, 8])
    hconst = din("hconst", [128, 128 + 128 + 4])

    y_o = dout("y_o", [NT * 128, D])
    lat_o = dout("lat_o", [NT * 128, 256])
    kr_o = dout("kr_o", [NT * 128, 64])
    stp_o = dout("stp_o", [8, 128, 128])
    mk_o = dout("mk_o", [2, 256, 512])
    mv_o = dout("mv_o", [2, 256, 512])
    sts_o = dout("sts_o", [4, 8, 128, 128])

    def wscr(shape):
        return k.dram(shape, BF16)
    s_memkv = wscr([2, 1024, 1024]); s_in0 = wscr([1024, 1216]); s_uq = wscr([384, 1536])
    s_ukT = wscr([8, 128, 256]); s_uv = wscr([256, 1024]); s_out0 = wscr([1536, 1024])
    s_in1 = wscr([1024, 4608]); s_out1 = wscr([1536, 1024]); s_up = [wscr([1024, 4096]) for _ in range(2)]
    s_dn = [wscr([4096, 1024]) for _ in range(2)]
    kT_rows = [128, 128, 64]
    kT_own = [k.dram([r_, 2048], BF16) for r_ in kT_rows]
    kT_g4 = [k.dram([4 * r_, 2048], BF16) for r_ in kT_rows]
    kT_all = [k.dram([8 * r_, 2048], BF16) for r_ in kT_rows]
    v_own = [k.dram([1024, 256], BF16) for _ in range(2)]
    v_g4 = [k.dram([4 * 1024, 256], BF16) for _ in range(2)]
    v_all = [k.dram([8 * 1024, 256], BF16) for _ in range(2)]
    kT_s = [k.dram([320, 1088], BF16) for _ in range(4)]
    v_s = [k.dram([1088, 256], BF16, dma=True) for _ in range(4)]
    x1_s = k.dram([NT * 128, D], F32)
    hT_s = k.dram([NT * 128, D], BF16)
    memkv_s = k.dram([2, 256, 1024], F32)
    exU_own = [k.dram([128, 1024], F32) for _ in range(4)]
    exU_g4 = [k.dram([4 * 128, 1024], F32) for _ in range(4)]
    exU_all = [k.dram([8 * 128, 1024], F32) for _ in range(4)]
    exD_own = k.dram([128, 32], F32); exD_g4 = k.dram([4 * 128, 32], F32); exD_all = k.dram([8 * 128, 32], F32)
    snapd = [k.dram([128, 1024], F32) for _ in range(4)]
    st_o = k.dram([NTP * 128, 1024], F32); st_g = k.dram([NTP * 128, 1024], F32)
    st_q = k.dram([NTP * 128, 1024], BF16); st_x = k.dram([NTP * 128, 512], BF16)

    ident = k.sb([128, 128], BF16)
    epsb = k.sb([128, 1], F32)
    ones_f = k.sb([128, 128], F32)
    gb = [k.sb([128, D], F32, dma=True) for _ in range(4)]
    qn_b = k.sb([128, 384], F32, dma=True)
    kvn_b = k.sb([128, 256], F32, dma=True)
    on_b = k.sb([128, 1024], F32, dma=True)
    lbb = k.sb([128, 1024], F32, dma=True)
    oml = k.sb([128, 1024], F32, dma=True)
    hc = k.sb([128, 260], F32, dma=True)
    scm = k.sb([128, 8], F32, dma=True)
    xr = [k.sb([128, D], F32, dma=True) for _ in range(4)]
    WS = 4096
    wr = [k.sb([128, WS], BF16, dma=True) for _ in range(6)]
    wri = [0]
    ktr = [k.sb([128, 3, 512], BF16, dma=True) for _ in range(3)]
    vtr = [k.sb([128, 4, 258], BF16, dma=True) for _ in range(3)]
    kvi = [0]
    mkr = [k.sb([128, 128], F32, dma=True) for _ in range(2)]
    cosr = [k.sb([128, 256], F32, dma=True) for _ in range(2)]
    sinr = [k.sb([128, 256], F32, dma=True) for _ in range(2)]
    Ft = [k.sb([128, 1024], F32, dma=True) for _ in range(7)]
    Ht = [k.sb([128, 1024], BF16, dma=True) for _ in range(14)]
    qsb = k.sb([128, 1536], F32)
    small = [k.sb([128, 16], F32) for _ in range(8)]
    exb = [k.sb([128, 1032], F32, dma=True) for _ in range(2)]
    Sst = k.sb([128, 1024], F32, dma=True)
    Dacc = k.sb([128, 8], F32)
    exD_sb = k.sb([128, 32], F32, dma=True)
    exDa_sb = k.sb([128, 8, 32], F32, dma=True)
    stsb = k.sb([128, 32], F32)
    mkT = [k.sb([128, 4, 256], BF16) for _ in range(2)]
    mv = [k.sb([128, 2, 4, 130], BF16, dma=True) for _ in range(2)]
    mstage = k.sb([128, 2, 512], BF16, dma=True)
    P = [k.ps([128, 512], F32) for _ in range(6)]
    PT = [k.ps([128, 1024], BF16) for _ in range(2)]
    pti = [0]

    E = k.op

    E("pool", lambda e: e.memset(ident.ap, 0.0), w=[ident])
    E("pool", lambda e: e.affine_select(out=ident.ap, in_=ident.ap, pattern=[[-1, 128]], compare_op=ALU.not_equal,
                                        fill=1.0, base=0, channel_multiplier=1), r=[ident], w=[ident])
    E("pool", lambda e: e.memset(epsb.ap, EPS), w=[epsb])
    E("pool", lambda e: e.memset(ones_f.ap, 1.0), w=[ones_f])
    for vb in vtr:
        E("pool", lambda e, vb=vb: e.memset(vb.ap, 1.0), w=[vb])
    for kb_ in ktr:
        E("pool", lambda e, kb_=kb_: e.memset(kb_.ap[64:128, 2, :], 0.0), w=[kb_])
    for m in mv:
        E("pool", lambda e, m=m: e.memset(m.ap, 1.0), w=[m])
    k.dma("sp", qn_b.ap, qn_g.partition_broadcast(128), w=qn_b)
    k.dma("sp", kvn_b.ap, kvn_g.partition_broadcast(128), w=kvn_b)
    k.dma("sp", on_b.ap, on_g.partition_broadcast(128), w=on_b)
    k.dma("sp", hc.ap, hconst, w=hc)
    k.dma("sp", scm.ap, scanm, w=scm)
    k.dma("sp", lbb.ap, hlb[1].partition_broadcast(128), w=lbb)
    k.dma("sp", oml.ap, hlb[0].partition_broadcast(128), w=oml)
    E("dve", lambda e: e.tensor_tensor(out=lbb.ap, in0=lbb.ap, in1=oml.ap, op=ALU.subtract), r=[oml, lbb], w=[lbb])
    E("act", lambda e: e.activation(out=lbb.ap, in_=lbb.ap, func=AF.Sigmoid), r=[lbb], w=[lbb])
    E("dve", lambda e: e.tensor_scalar(out=oml.ap, in0=lbb.ap, scalar1=-1.0, scalar2=1.0, op0=ALU.mult, op1=ALU.add),
      r=[lbb], w=[oml])

    wsem = Buf(None, True)
    wsem0 = Buf(None, True)

    bg = []

    def bg_run(n):
        for _ in range(n):
            if bg:
                bg.pop(0)()

    def castw(dst, src2d, dst2d, rows, ws, defer=False):
        step = 128
        for r0 in range(0, rows, step):
            fn = lambda r0=r0: k.dma("pool", dst2d[r0:r0 + step], src2d[r0:r0 + step], w=dst, r=ws)
            if defer:
                bg.append(fn)
            else:
                fn()

    def cast_fin(ws, scs):
        _q = ws.qs["pool"]
        _fin = {("d", id(_q)): (_q.sem, _q.tot, _q)}
        for sc in scs:
            sc.lastw = dict(_fin)
    castw(s_memkv, w_memkv.rearrange("l k n -> (l k) n"), s_memkv.ap.rearrange("l k n -> (l k) n"), 2048, wsem0)
    castw(s_in0, w_in0, s_in0.ap, 1024, wsem0)
    cast_fin(wsem0, (s_memkv, s_in0))

    wsemB = Buf(None, True)

    def cast_rest():
        castw(s_uq, w_uq, s_uq.ap, 384, wsem, defer=True)
        castw(s_ukT, w_ukT.rearrange("h d c -> (h d) c"), s_ukT.ap.rearrange("h d c -> (h d) c"), 1024, wsem, defer=True)
        castw(s_uv, w_uv, s_uv.ap, 256, wsem, defer=True)
        castw(s_out0, w_out0, s_out0.ap, 1536, wsem, defer=True)
        castw(s_up[0], w_up[0], s_up[0].ap, 1024, wsem, defer=True)
        castw(s_dn[0], w_dn[0], s_dn[0].ap, 4096, wsem, defer=True)

    def cast_layer1():
        castw(s_in1, w_in1, s_in1.ap, 1024, wsemB, defer=True)
        castw(s_out1, w_out1, s_out1.ap, 1536, wsemB, defer=True)
        castw(s_up[1], w_up[1], s_up[1].ap, 1024, wsemB, defer=True)
        castw(s_dn[1], w_dn[1], s_dn[1].ap, 4096, wsemB, defer=True)

    def wload(src_buf, src_ap, shape):
        b = wr[wri[0] % 6]
        wri[0] += 1
        n = shape[1] * shape[2]
        assert n <= WS
        view = b.ap[:, 0:n].rearrange("p (a b) -> p a b", a=shape[1])
        k.dma("sp", view, src_ap, w=b, r=src_buf)
        return b, view

    def rstd_of(src_ap, src_buf, n, junk, out_small, col=0):
        o = out_small.ap[:, col:col + 1]
        E("act", lambda e: e.activation(out=junk.ap[:, 0:n], in_=src_ap, func=AF.Square, accum_out=o),
          r=[src_buf], w=[junk, out_small])
        E("act", lambda e: e.activation(out=o, in_=o, func=AF.Ln, bias=epsb.ap, scale=1.0 / n), r=[out_small, epsb], w=[out_small])
        E("act", lambda e: e.activation(out=o, in_=o, func=AF.Exp, scale=-0.5), r=[out_small], w=[out_small])
        return o

    def transposes(dst_buf, dst_fn, src_buf, src_fn, n, rows=128, cols=128, whole=None, ptview=None):
        pt = PT[pti[0] % 2]
        pti[0] += 1
        for i in range(n):
            E("pe", lambda e, i=i: e.transpose(out=pt.ap[0:cols, i * 128:i * 128 + rows], in_=src_fn(i), identity=ident.ap[0:rows, 0:rows]),
              r=[src_buf, ident], w=[pt])
        if whole is not None and rows == 128:
            src = pt.ap[0:cols, 0:n * 128]
            if ptview is not None:
                src = ptview(src)
            E("dve", lambda e: e.tensor_copy(out=whole, in_=src), r=[pt], w=[dst_buf])
            return
        for i in range(n):
            E("dve", lambda e, i=i: e.tensor_copy(out=dst_fn(i), in_=pt.ap[0:cols, i * 128:i * 128 + rows]), r=[pt], w=[dst_buf])

    def load_gains(l):
        for j in range(4):
            k.dma("sp", gb[j].ap, gains[2 * j + l].partition_broadcast(128), w=gb[j])

    xissued = {}
    nexttile = [None]

    xseq = [0]
    hpre = [k.sb([128, D], BF16, dma=True) for _ in range(2)]
    hseq = [0]
    hissued = {}
    want_hT = [False]

    def xprefetch(t, l):
        if (t, l) in xissued:
            return
        slot = xseq[0] % 4
        xseq[0] += 1
        xissued[(t, l)] = slot
        xb = xr[slot]
        if l == 0:
            k.dma("sp", xb.ap, xin[t * 128:(t + 1) * 128, :], w=xb)
            if want_hT[0]:
                hs = hseq[0] % 2
                hseq[0] += 1
                hissued[t] = hs
                k.dma("sp", hpre[hs].ap, hT_s.ap[t * 128:(t + 1) * 128, :], w=hpre[hs], r=hT_s)
        else:
            k.dma("sp", xb.ap, x1_s.ap[t * 128:(t + 1) * 128, :], w=xb, r=x1_s)

    def prefetch_next():
        if nexttile[0] is not None:
            xprefetch(*nexttile[0])

    def front(t, l, gi):
        xprefetch(t, l)
        return xr[xissued.pop((t, l))]

    def norm_T(xb, gi, hb, hTb, sm=None):
        sm = small[0] if sm is None else sm
        rs = rstd_of(xb.ap, xb, D, Ht[13], sm)
        E("dve", lambda e: e.scalar_tensor_tensor(out=hb.ap, in0=xb.ap, scalar=rs, in1=gb[gi].ap, op0=ALU.mult, op1=ALU.mult),
          r=[xb, sm, gb[gi]], w=[hb])
        transposes(hTb, lambda i: hTb.ap[:, i * 128:(i + 1) * 128], hb, lambda i: hb.ap[:, i * 128:(i + 1) * 128], 8, whole=hTb.ap[:, 0:8 * 128])

    def proj(hTb, nk, wsrc_buf, wsrc_fn, ncols, pbank, kstep=4):
        for k0 in range(0, nk, kstep):
            kn = min(kstep, nk - k0)
            wb, wv = wload(wsrc_buf, wsrc_fn(k0, kn), [128, kn, ncols])
            for kk in range(kn):
                kc = k0 + kk
                E("pe", lambda e, kc=kc, kk=kk, wv=wv: e.matmul(pbank.ap[:, 0:ncols], lhsT=hTb.ap[:, kc * 128:(kc + 1) * 128],
                                                           rhs=wv[:, kk, :], start=(kc == 0), stop=(kc == nk - 1)),
                  r=[hTb, wb], w=[pbank])

    def wcols(sbuf2d, c0, ncols):
        return lambda k0, kn: sbuf2d[k0 * 128:(k0 + kn) * 128, c0:c0 + ncols].rearrange("(k p) n -> p k n", p=128)

    def phase0():
        for l in range(2):
            k.dma("sp", gb[0].ap, gains[8 + l].partition_broadcast(128), w=gb[0])
            for mb in range(2):
                xb = xr[mb % 2]
                k.dma("sp", xb.ap, memp[mb * 128:(mb + 1) * 128, :], w=xb)
                norm_T(xb, 0, Ht[0], Ht[1])
                for g in range(2):
                    proj(Ht[1], 8, s_memkv, wcols(s_memkv.ap[l], g * 512, 512), 512, P[g])
                    E("dve", lambda e, g=g: e.tensor_copy(out=Ft[0].ap[:, g * 512:(g + 1) * 512], in_=P[g].ap), r=[P[g]], w=[Ft[0]])
                k.dma("pool", mk_o[l, mb * 128:(mb + 1) * 128, :], Ft[0].ap[:, 0:512], r=Ft[0], final=True)
                k.dma("pool", mv_o[l, mb * 128:(mb + 1) * 128, :], Ft[0].ap[:, 512:1024], r=Ft[0], final=True)
                k.dma("pool", memkv_s.ap[l, mb * 128:(mb + 1) * 128, :], Ft[0].ap, w=memkv_s, r=Ft[0])

    def build_memset(slot, ksrc_ap, vsrc_ap, src_buf):
        k.dma("pool", mstage.ap, ksrc_ap.rearrange("(b p) n -> p b n", p=128), w=mstage, r=src_buf)
        for blk in range(2):
            transposes(mkT[slot], lambda h, blk=blk: mkT[slot].ap[:, h, blk * 128:(blk + 1) * 128],
                       mstage, lambda h, blk=blk: mstage.ap[:, blk, h * 128:(h + 1) * 128], 4,
                       whole=mkT[slot].ap[:, :, blk * 128:(blk + 1) * 128], ptview=lambda a: a.rearrange("p (h m) -> p h m", h=4))
        for blk in range(2):
            k.dma("pool", mv[slot].ap[:, blk, :, 0:128], vsrc_ap[blk * 128:(blk + 1) * 128, :].rearrange("p (h d) -> p h d", h=4), w=mv[slot], r=src_buf)

    def cross_attn(xqT, segs, crossT):
        pTx = Ht[11]
        csb = Ht[12]
        for (t0, ntk, slot) in segs:
            for h in range(4):
                for blk in range(2):
                    bank = P[4 + (h // 2)]
                    off = ((h % 2) * 2 + blk) * 128
                    E("pe", lambda e, h=h, blk=blk, bank=bank, off=off: e.matmul(
                        bank.ap[:, off + t0:off + t0 + ntk], lhsT=mkT[slot].ap[:, h, blk * 128:(blk + 1) * 128],
                        rhs=xqT.ap[:, h * 128 + t0:h * 128 + t0 + ntk], start=True, stop=True), r=[mkT[slot], xqT], w=[bank])
        for hb in range(2):
            E("act", lambda e, hb=hb: e.activation(out=pTx.ap[:, hb * 512:(hb + 1) * 512], in_=P[4 + hb].ap, func=AF.Exp,
                                                   scale=128.0 ** -0.5), r=[P[4 + hb]], w=[pTx])
        for (t0, ntk, slot) in segs:
            for h in range(4):
                bank = P[h // 2]
                for blk in range(2):
                    off = (h * 2 + blk) * 128
                    E("pe", lambda e, h=h, blk=blk, bank=bank, off=off: e.matmul(
                        bank.ap[t0:t0 + ntk, (h % 2) * 130:(h % 2) * 130 + 129], lhsT=pTx.ap[:, off + t0:off + t0 + ntk],
                        rhs=mv[slot].ap[:, blk, h, 0:129], start=(blk == 0), stop=(blk == 1)), r=[pTx, mv[slot]], w=[bank])
        for h in range(4):
            bank = P[h // 2]
            c0 = (h % 2) * 130
            E("dve", lambda e, h=h, bank=bank, c0=c0: e.reciprocal(out=small[1].ap[:, h:h + 1], in_=bank.ap[:, c0 + 128:c0 + 129]),
              r=[bank], w=[small[1]])
            E("dve", lambda e, h=h, bank=bank, c0=c0: e.tensor_scalar(out=csb.ap[:, h * 128:(h + 1) * 128], in0=bank.ap[:, c0:c0 + 128],
                                                                   scalar1=small[1].ap[:, h:h + 1], scalar2=None, op0=ALU.mult),
              r=[bank, small[1]], w=[csb])
        transposes(crossT, lambda h: crossT.ap[:, h * 128:(h + 1) * 128], csb, lambda h: csb.ap[:, h * 128:(h + 1) * 128], 4, whole=crossT.ap[:, 0:512])

    def post_add(xb, gi, src=None):
        if src is None:
            z = Ft[6]
            for g in range(2):
                E("dve", lambda e, g=g: e.tensor_copy(out=z.ap[:, g * 512:(g + 1) * 512], in_=P[g].ap), r=[P[g]], w=[z])
        else:
            z = src
        rs = rstd_of(z.ap, z, D, Ht[13], small[2])
        E("dve", lambda e: e.scalar_tensor_tensor(out=z.ap, in0=z.ap, scalar=rs, in1=gb[gi].ap, op0=ALU.mult, op1=ALU.mult),
          r=[z, small[2], gb[gi]], w=[z])
        E("dve", lambda e: e.tensor_tensor(out=xb.ap, in0=xb.ap, in1=z.ap, op=ALU.add), r=[xb, z], w=[xb])

    pending = []

    def ffn_pair(items):
        l = items[0][0]
        hbs = [(Ht[0], Ht[1]), (Ht[2], Ht[3])]
        accs = [Ft[0], Ft[1]]
        rls = [Ft[5], Ft[4]]
        hids = [Ht[9], Ht[7]]
        hidTs = [Ht[10], Ht[8]]
        for j, (l_, t, xb, last) in enumerate(items):
            norm_T(xb, 2, hbs[j][0], hbs[j][1], sm=small[0] if j == 0 else small[1])
        for pc in range(4):
            ups = [wload(s_up[l], s_up[l].ap[:, pc * 1024 + g * 512:pc * 1024 + (g + 1) * 512].rearrange("(k p) n -> p k n", p=128), [128, 8, 512])
                   for g in range(2)]
            for j in range(len(items)):
                hTb = hbs[j][1]
                for g in range(2):
                    bank = P[2 + 2 * j + g]
                    wb, wv = ups[g]
                    for kc in range(8):
                        E("pe", lambda e, kc=kc, bank=bank, wv=wv, hTb=hTb: e.matmul(bank.ap, lhsT=hTb.ap[:, kc * 128:(kc + 1) * 128],
                                                                                rhs=wv[:, kc, :], start=(kc == 0), stop=(kc == 7)),
                          r=[hTb, wb], w=[bank])
            dns = [wload(s_dn[l], s_dn[l].ap[pc * 1024 + h_ * 512:pc * 1024 + (h_ + 1) * 512, :].rearrange("(k p) n -> p k n", p=128), [128, 4, 1024])
                   for h_ in range(2)]
            for j in range(len(items)):
                rl, hid, hidT, acc = rls[j], hids[j], hidTs[j], accs[j]
                for g in range(2):
                    bank = P[2 + 2 * j + g]
                    E("act", lambda e, g=g, bank=bank, rl=rl: e.activation(out=rl.ap[:, g * 512:(g + 1) * 512], in_=bank.ap, func=AF.Relu), r=[bank], w=[rl])
                E("pool", lambda e, rl=rl, hid=hid: e.tensor_tensor(out=hid.ap, in0=rl.ap, in1=rl.ap, op=ALU.mult), r=[rl], w=[hid])
                transposes(hidT, lambda i, hidT=hidT: hidT.ap[:, i * 128:(i + 1) * 128], hid, lambda i, hid=hid: hid.ap[:, i * 128:(i + 1) * 128], 8,
                           whole=hidT.ap[:, 0:1024])
                for kc in range(8):
                    wb2, wv2 = dns[kc // 4]
                    for g in range(2):
                        E("pe", lambda e, kc=kc, g=g, wv2=wv2, hidT=hidT: e.matmul(P[g].ap, lhsT=hidT.ap[:, kc * 128:(kc + 1) * 128],
                                                                              rhs=wv2[:, kc % 4, g * 512:(g + 1) * 512], start=(kc == 0), stop=(kc == 7)),
                          r=[hidT, wb2], w=[P[g]])
                for g in range(2):
                    if pc == 0:
                        E("dve", lambda e, g=g, acc=acc: e.tensor_copy(out=acc.ap[:, g * 512:(g + 1) * 512], in_=P[g].ap), r=[P[g]], w=[acc])
                    else:
                        E("dve", lambda e, g=g, acc=acc: e.tensor_tensor(out=acc.ap[:, g * 512:(g + 1) * 512], in0=P[g].ap,
                                                                         in1=acc.ap[:, g * 512:(g + 1) * 512], op=ALU.add), r=[P[g], acc], w=[acc])
        for j, (l_, t, xb, last) in enumerate(items):
            post_add(xb, 3, src=accs[j])
            if last:
                k.dma("pool", y_o[t * 128:(t + 1) * 128, :], xb.ap, r=xb, final=True)
            else:
                k.dma("pool", x1_s.ap[t * 128:(t + 1) * 128, :], xb.ap, w=x1_s, r=xb)

    def flush_ffn():
        if pending:
            ffn_pair(list(pending))
            del pending[:]

    def out_and_ffn(l, t, xb, mixT, crossT, s_out, last):
        for half in range(3):
            wb, wv = wload(s_out, s_out.ap[half * 512:(half + 1) * 512, :].rearrange("(k p) n -> p k n", p=128), [128, 4, 1024])
            for kk in range(4):
                kc = half * 4 + kk
                src, so = (mixT, kc * 128) if kc < 8 else (crossT, (kc - 8) * 128)
                for g in range(2):
                    E("pe", lambda e, kc=kc, kk=kk, g=g, src=src, so=so, wv=wv: e.matmul(
                        P[g].ap, lhsT=src.ap[:, so:so + 128], rhs=wv[:, kk, g * 512:(g + 1) * 512],
                        start=(kc == 0), stop=(kc == 11)), r=[src, wb], w=[P[g]])
        post_add(xb, 1)
        prefetch_next()
        pending.append((l, t, xb, last))
        if len(pending) == 2:
            flush_ffn()

    def l0_kpart(t):
        xb = front(t, 0, 0)
        od = t % 2
        hb, hTb = (Ht[0], Ht[1]) if od == 0 else (Ht[4], Ht[5])
        norm_T(xb, 0, hb, hTb, sm=small[0] if od == 0 else small[1])
        k.dma("pool", hT_s.ap[t * 128:(t + 1) * 128, :], hTb.ap, w=hT_s, r=hTb)
        pb = P[od]
        proj(hTb, 8, s_in0, wcols(s_in0.ap, 384, 320), 320, pb, kstep=8)
        ckv = Ft[0] if od == 0 else Ft[3]
        E("dve", lambda e: e.tensor_copy(out=ckv.ap[:, 0:320], in_=pb.ap[:, 0:320]), r=[pb], w=[ckv])
        sm3 = small[3] if od == 0 else small[4]
        rs = rstd_of(ckv.ap[:, 0:256], ckv, 256, Ht[13], sm3)
        lat = Ft[1] if od == 0 else Ft[4]
        E("dve", lambda e: e.scalar_tensor_tensor(out=lat.ap[:, 0:256], in0=ckv.ap[:, 0:256], scalar=rs, in1=kvn_b.ap, op0=ALU.mult, op1=ALU.mult),
          r=[ckv, sm3, kvn_b], w=[lat])
        cb, sbf = cosr[t % 2], sinr[t % 2]
        k.dma("sp", cb.ap, cos_t[t * 128:(t + 1) * 128, :], w=cb)
        k.dma("sp", sbf.ap, sin_t[t * 128:(t + 1) * 128, :], w=sbf)
        kr = lat.ap[:, 256:320]
        tmp = Ft[2] if od == 0 else Ft[5]
        x1, x2 = ckv.ap[:, 256:288], ckv.ap[:, 288:320]
        c, s = cb.ap[:, 0:32], sbf.ap[:, 0:32]
        E("dve", lambda e: e.tensor_tensor(out=tmp.ap[:, 0:32], in0=x1, in1=c, op=ALU.mult), r=[ckv, cb], w=[tmp])
        E("dve", lambda e: e.tensor_tensor(out=tmp.ap[:, 32:64], in0=x2, in1=s, op=ALU.mult), r=[ckv, sbf], w=[tmp])
        E("dve", lambda e: e.tensor_tensor(out=tmp.ap[:, 64:96], in0=x2, in1=c, op=ALU.mult), r=[ckv, cb], w=[tmp])
        E("dve", lambda e: e.tensor_tensor(out=tmp.ap[:, 96:128], in0=x1, in1=s, op=ALU.mult), r=[ckv, sbf], w=[tmp])
        E("dve", lambda e: e.tensor_tensor(out=lat.ap[:, 256:288], in0=tmp.ap[:, 0:32], in1=tmp.ap[:, 32:64], op=ALU.subtract), r=[tmp], w=[lat])
        E("dve", lambda e: e.tensor_tensor(out=lat.ap[:, 288:320], in0=tmp.ap[:, 64:96], in1=tmp.ap[:, 96:128], op=ALU.add), r=[tmp], w=[lat])
        k.dma("pool", lat_o[t * 128:(t + 1) * 128, :], lat.ap[:, 0:256], r=lat, final=True)
        k.dma("pool", kr_o[t * 128:(t + 1) * 128, :], lat.ap[:, 256:320], r=lat, final=True)
        kb = Ht[2] if od == 0 else Ht[6]
        E("dve", lambda e: e.tensor_copy(out=kb.ap[:, 0:320], in_=lat.ap[:, 0:320]), r=[lat], w=[kb])
        kTb = Ht[3] if od == 0 else Ht[7]
        transposes(kTb, lambda i: kTb.ap[:, i * 128:(i + 1) * 128], kb, lambda i: kb.ap[:, i * 128:(i + 1) * 128], 2, whole=kTb.ap[:, 0:2 * 128])
        transposes(kTb, lambda i: kTb.ap[0:64, 256:384], kb, lambda i: kb.ap[:, 256:320], 1, rows=128, cols=64, whole=kTb.ap[0:64, 256:384])
        if t < NTP:
            k.dma("pool", v_own[t // 8].ap[(t % 8) * 128:(t % 8 + 1) * 128, :], kb.ap[:, 0:256], w=v_own[t // 8], r=kb)
            k.dma("pool", kT_own[0].ap[:, t * 128:(t + 1) * 128], kTb.ap[:, 0:128], w=kT_own[0], r=kTb)
            k.dma("pool", kT_own[1].ap[:, t * 128:(t + 1) * 128], kTb.ap[:, 128:256], w=kT_own[1], r=kTb)
            k.dma("pool", kT_own[2].ap[:, t * 128:(t + 1) * 128], kTb.ap[0:64, 256:384], w=kT_own[2], r=kTb)
        else:
            for hs in range(2):
                sq = (t - NTP) * 2 + hs
                k.dma("pool", v_s[sq].ap[1024:1088, :], kb.ap[hs * 64:(hs + 1) * 64, 0:256], w=v_s[sq], r=kb)
                k.dma("pool", kT_s[sq].ap[0:256, 1024:1088].rearrange("(j p) n -> p j n", p=128),
                      kTb.ap[:, 0:256].rearrange("p (j n) -> p j n", j=2)[:, :, hs * 64:(hs + 1) * 64], w=kT_s[sq], r=kTb)
                k.dma("pool", kT_s[sq].ap[256:320, 1024:1088], kTb.ap[0:64, 256 + hs * 64:256 + (hs + 1) * 64], w=kT_s[sq], r=kTb)

    def sample_cache_prep_v():
        for sq in range(4):
            k.dma("pool", v_s[sq].ap[0:1024, :], c_lat[sq], w=v_s[sq], r=wsem0)

    def sample_cache_iters():
        its = []
        for sq in range(4):
            for tt in range(8):
                def it(sq=sq, tt=tt):
                    kb = Ht[8] if tt % 2 == 0 else Ht[10]
                    k.dma("pool", kb.ap[:, 0:256], c_lat[sq, tt * 128:(tt + 1) * 128, :], w=kb)
                    k.dma("pool", kb.ap[:, 256:320], c_kr[sq, tt * 128:(tt + 1) * 128, :], w=kb)
                    kTb = Ht[9] if tt % 2 == 0 else Ht[11]
                    transposes(kTb, lambda i: kTb.ap[:, i * 128:(i + 1) * 128], kb, lambda i: kb.ap[:, i * 128:(i + 1) * 128], 2, whole=kTb.ap[:, 0:2 * 128])
                    transposes(kTb, lambda i: kTb.ap[0:64, 256:384], kb, lambda i: kb.ap[:, 256:320], 1, rows=128, cols=64, whole=kTb.ap[0:64, 256:384])
                    k.dma("pool", kT_s[sq].ap[0:256, tt * 128:(tt + 1) * 128].rearrange("(j p) n -> p j n", p=128),
                          kTb.ap[:, 0:256].rearrange("p (j n) -> p j n", j=2), w=kT_s[sq], r=kTb)
                    k.dma("pool", kT_s[sq].ap[256:320, tt * 128:(tt + 1) * 128], kTb.ap[0:64, 256:384], w=kT_s[sq], r=kTb)
                its.append(it)
        return its

    def attend(qlT, qrT, n, kT_src, kfn, vfn, L, mask_qc, osb, odst):
        nblk = (L + 127) // 128
        if mask_qc is not None:
            mb = mkr[mask_qc % 2]
            k.dma("sp", mb.ap, maskb[mask_qc], w=mb)
        pS = [P[2], P[3], P[4], P[5]]
        pTs = [Ht[7], Ht[8], Ht[9], Ht[10]]
        acc = Ft[1]
        blocks = []

        def load_group(g0):
            key0 = g0 * 128
            nkeys = min(512, L - key0)
            kt = ktr[kvi[0] % 3]
            vt = vtr[kvi[0] % 3]
            kvi[0] += 1
            for j in range(3):
                kr_ = 128 if j < 2 else 64
                k.dma("sp", kt.ap[0:kr_, j, 0:nkeys], kfn(j, key0, nkeys), w=kt, r=kT_src[j])
            nfull = nkeys // 128
            vb_, vap = vfn(key0, nkeys)
            if nfull > 0:
                k.dma("sp", vt.ap[:, 0:nfull, 0:256], vap[0:nfull * 128, :].rearrange("(b p) c -> p b c", p=128), w=vt, r=vb_)
            if nkeys % 128:
                rem = nkeys % 128
                k.dma("sp", vt.ap[0:rem, nfull, 0:256], vap[nfull * 128:nkeys, :], w=vt, r=vb_)
            return kt, vt

        cur = [None, None]

        def emit_st(blk):
            if blk % 4 == 0:
                cur[0], cur[1] = load_group(blk)
            kt, vt = cur
            b = blk % 4
            kn = min(128, L - blk * 128)
            ps = pS[blk % 4]
            for j in range(3):
                kr_ = 128
                rhs = qlT.ap[:, (j * 2 + n) * 512:(j * 2 + n + 1) * 512] if j < 2 else qrT.ap[:, n * 512:(n + 1) * 512]
                E("pe", lambda e, j=j, kr_=kr_, rhs=rhs, ps=ps, kt=kt, b=b, kn=kn: e.matmul(
                    ps.ap[0:kn, :], lhsT=kt.ap[0:kr_, j, b * 128:b * 128 + kn], rhs=rhs, start=(j == 0), stop=(j == 2)),
                  r=[kt, qlT, qrT], w=[ps])
            return (vt, b, kn)

        def emit_exp(blk, info):
            vt, b, kn = info
            ps = pS[blk % 4]
            pt = pTs[blk % 4]
            if mask_qc is not None:
                E("act", lambda e, ps=ps, pt=pt, kn=kn, blk=blk: e.activation(
                    out=pt.ap[0:kn, 0:512], in_=ps.ap[0:kn, :], func=AF.Exp, scale=192.0 ** -0.5, bias=mb.ap[0:kn, blk:blk + 1]),
                  r=[ps, mb], w=[pt])
            else:
                E("act", lambda e, ps=ps, pt=pt, kn=kn: e.activation(
                    out=pt.ap[0:kn, 0:512], in_=ps.ap[0:kn, :], func=AF.Exp, scale=192.0 ** -0.5), r=[ps], w=[pt])

        def emit_pv(blk, info):
            vt, b, kn = info
            pt = pTs[blk % 4]
            for cc in range(2):
                E("pe", lambda e, cc=cc, pt=pt, vt=vt, b=b, kn=kn, blk=blk: e.matmul(
                    P[cc].ap, lhsT=vt.ap[0:kn, b, cc * 128:(cc + 1) * 128], rhs=pt.ap[0:kn, 0:512],
                    start=(blk == 0), stop=(blk == nblk - 1)), r=[pt, vt], w=[P[cc]])
            if blk == 0:
                E("dve", lambda e, pt=pt: e.tensor_copy(out=acc.ap[:, 0:512], in_=pt.ap[:, 0:512]), r=[pt], w=[acc])
            else:
                E("dve", lambda e, pt=pt, kn=kn: e.tensor_tensor(out=acc.ap[0:kn, 0:512], in0=acc.ap[0:kn, 0:512], in1=pt.ap[0:kn, 0:512], op=ALU.add),
                  r=[pt, acc], w=[acc])

        units = [list(range(u, min(u + 2, nblk))) for u in range(0, nblk, 2)]
        infos = {b_: emit_st(b_) for b_ in units[0]}
        for ui, un in enumerate(units):
            if ui + 1 < len(units):
                for b_ in units[ui + 1]:
                    infos[b_] = emit_st(b_)
            for b_ in un:
                emit_exp(b_, infos[b_])
            for b_ in un:
                emit_pv(b_, infos.pop(b_))
        E("pe", lambda e: e.matmul(P[2].ap, lhsT=ones_f.ap, rhs=acc.ap[:, 0:512], start=True, stop=True), r=[ones_f, acc], w=[P[2]])
        E("dve", lambda e: e.reciprocal(out=acc.ap[:, 512:1024], in_=P[2].ap), r=[P[2]], w=[acc])
        for cc in range(2):
            E("dve", lambda e, cc=cc: e.tensor_tensor(out=odst[:, (cc * 2 + n) * 512:(cc * 2 + n + 1) * 512], in0=P[cc].ap,
                                                      in1=acc.ap[:, 512:1024], op=ALU.mult), r=[P[cc], acc], w=[osb])

    def l0_tile(t):
        xb = front(t, 0, 0)
        hTb = hpre[hissued.pop(t)]
        proj(hTb, 8, s_in0, wcols(s_in0.ap, 0, 384), 384, P[0], kstep=8)
        cq = Ft[0]
        E("dve", lambda e: e.tensor_copy(out=cq.ap[:, 0:384], in_=P[0].ap[:, 0:384]), r=[P[0]], w=[cq])
        rs = rstd_of(cq.ap[:, 0:384], cq, 384, Ht[13], small[3])
        cqb, cqT = Ht[2], Ht[3]
        E("dve", lambda e: e.scalar_tensor_tensor(out=cqb.ap[:, 0:384], in0=cq.ap[:, 0:384], scalar=rs, in1=qn_b.ap, op0=ALU.mult, op1=ALU.mult),
          r=[cq, small[3], qn_b], w=[cqb])
        transposes(cqT, lambda i: cqT.ap[:, i * 128:(i + 1) * 128], cqb, lambda i: cqb.ap[:, i * 128:(i + 1) * 128], 3, whole=cqT.ap[:, 0:3 * 128])
        wb, wv = wload(s_uq, s_uq.ap.rearrange("(k p) n -> p k n", p=128)[:, :, 0:768], [128, 3, 768])
        wb2, wv2 = wload(s_uq, s_uq.ap.rearrange("(k p) n -> p k n", p=128)[:, :, 768:1536], [128, 3, 768])
        for g in range(3):
            wbb, wvv, c0 = (wb, wv, g * 512) if g == 0 else ((wb2, wv2, g * 512 - 768) if g == 2 else (None, None, None))
            for kc in range(3):
                if g == 1:
                    E("pe", lambda e, kc=kc: e.matmul(P[2].ap[:, 0:256], lhsT=cqT.ap[:, kc * 128:(kc + 1) * 128], rhs=wv[:, kc, 512:768],
                                                     start=(kc == 0), stop=(kc == 2)), r=[cqT, wb], w=[P[2]])
                    E("pe", lambda e, kc=kc: e.matmul(P[3].ap[:, 0:256], lhsT=cqT.ap[:, kc * 128:(kc + 1) * 128], rhs=wv2[:, kc, 0:256],
                                                     start=(kc == 0), stop=(kc == 2)), r=[cqT, wb2], w=[P[3]])
                else:
                    bank = P[0] if g == 0 else P[1]
                    E("pe", lambda e, kc=kc, bank=bank, wvv=wvv, c0=c0: e.matmul(bank.ap, lhsT=cqT.ap[:, kc * 128:(kc + 1) * 128],
                                                                            rhs=wvv[:, kc, c0:c0 + 512], start=(kc == 0), stop=(kc == 2)),
                      r=[cqT, wbb], w=[bank])
        E("dve", lambda e: e.tensor_copy(out=qsb.ap[:, 0:512], in_=P[0].ap), r=[P[0]], w=[qsb])
        E("dve", lambda e: e.tensor_copy(out=qsb.ap[:, 512:768], in_=P[2].ap[:, 0:256]), r=[P[2]], w=[qsb])
        E("dve", lambda e: e.tensor_copy(out=qsb.ap[:, 768:1024], in_=P[3].ap[:, 0:256]), r=[P[3]], w=[qsb])
        E("dve", lambda e: e.tensor_copy(out=qsb.ap[:, 1024:1536], in_=P[1].ap), r=[P[1]], w=[qsb])
        q3 = qsb.ap.rearrange("p (h d) -> p h d", h=8)
        qn, qr = Ht[2], Ht[6]
        E("dve", lambda e: e.tensor_copy(out=qn.ap.rearrange("p (h d) -> p h d", h=8), in_=q3[:, :, 0:128]), r=[qsb], w=[qn])
        cb, sbf = cosr[t % 2], sinr[t % 2]
        k.dma("sp", cb.ap, cos_t[t * 128:(t + 1) * 128, :], w=cb)
        k.dma("sp", sbf.ap, sin_t[t * 128:(t + 1) * 128, :], w=sbf)
        c3 = cb.ap.rearrange("p (h d) -> p h d", h=8)
        s3 = sbf.ap.rearrange("p (h d) -> p h d", h=8)
        tmp = Ft[2]
        t4 = tmp.ap.rearrange("p (a h d) -> p a h d", a=4, h=8)
        x1, x2 = q3[:, :, 128:160], q3[:, :, 160:192]
        E("dve", lambda e: e.tensor_tensor(out=t4[:, 0], in0=x1, in1=c3, op=ALU.mult), r=[qsb, cb], w=[tmp])
        E("dve", lambda e: e.tensor_tensor(out=t4[:, 1], in0=x2, in1=s3, op=ALU.mult), r=[qsb, sbf], w=[tmp])
        E("dve", lambda e: e.tensor_tensor(out=t4[:, 2], in0=x2, in1=c3, op=ALU.mult), r=[qsb, cb], w=[tmp])
        E("dve", lambda e: e.tensor_tensor(out=t4[:, 3], in0=x1, in1=s3, op=ALU.mult), r=[qsb, sbf], w=[tmp])
        qr3 = qr.ap[:, 0:512].rearrange("p (h d) -> p h d", h=8)
        E("dve", lambda e: e.tensor_tensor(out=qr3[:, :, 0:32], in0=t4[:, 0], in1=t4[:, 1], op=ALU.subtract), r=[tmp], w=[qr])
        E("dve", lambda e: e.tensor_tensor(out=qr3[:, :, 32:64], in0=t4[:, 2], in1=t4[:, 3], op=ALU.add), r=[tmp], w=[qr])
        qnT, qrT0 = Ht[3], Ht[0]
        transposes(qnT, lambda i: qnT.ap[:, i * 128:(i + 1) * 128], qn, lambda i: qn.ap[:, i * 128:(i + 1) * 128], 8, whole=qnT.ap[:, 0:8 * 128])
        transposes(qrT0, lambda i: qrT0.ap[0:64, i * 128:(i + 1) * 128], qr, lambda i: qr.ap[:, i * 64:(i + 1) * 64], 8, rows=128, cols=64, whole=qrT0.ap[0:64, 0:1024])
        qrT = Ht[6]
        E("dve", lambda e: e.tensor_copy(out=qrT.ap[0:64, :].rearrange("p (n h q) -> p n h q", n=2, h=8),
                                         in_=qrT0.ap[0:64, :].rearrange("p (h n q) -> p n h q", h=8, n=2)), r=[qrT0], w=[qrT])
        E("dve", lambda e: e.memset(qrT.ap[64:128, :], 0.0), w=[qrT])
        wbk, wvk = wload(s_ukT, s_ukT.ap.rearrange("h d c -> d h c"), [128, 8, 256])
        qlT = Ft[3]
        qlT_b = Buf(qlT.ap.bitcast(BF16), None)
        qlT_b.lastw, qlT_b.reads = qlT.lastw, qlT.reads
        for cc in range(2):
            for h in range(8):
                bank = P[cc * 2 + h // 4]
                E("pe", lambda e, cc=cc, h=h, bank=bank: e.matmul(bank.ap[:, (h % 4) * 128:(h % 4 + 1) * 128],
                                                                   lhsT=wvk[:, h, cc * 128:(cc + 1) * 128], rhs=qnT.ap[:, h * 128:(h + 1) * 128],
                                                                   start=True, stop=True), r=[wbk, qnT], w=[bank])
            for hh in range(2):
                bank = P[cc * 2 + hh]
                dst = qlT.ap.bitcast(BF16)[:, cc * 1024:(cc + 1) * 1024].rearrange("p (n h q) -> p n h q", n=2, h=8)[:, :, hh * 4:(hh + 1) * 4, :]
                src = bank.ap.rearrange("p (h n q) -> p n h q", h=4, n=2)
                E("dve", lambda e, dst=dst, src=src: e.tensor_copy(out=dst, in_=src), r=[bank], w=[qlT])
        proj(hTb, 8, s_in0, wcols(s_in0.ap, 704, 512), 512, P[1], kstep=8)
        xqb, xqT = Ht[4], Ht[5]
        E("dve", lambda e: e.tensor_copy(out=xqb.ap[:, 0:512], in_=P[1].ap), r=[P[1]], w=[xqb])
        transposes(xqT, lambda i: xqT.ap[:, i * 128:(i + 1) * 128], xqb, lambda i: xqb.ap[:, i * 128:(i + 1) * 128], 4, whole=xqT.ap[:, 0:4 * 128])
        olT = Ft[4]
        olT_bf = olT.ap.bitcast(BF16)
        qlT_v = Buf(qlT.ap.bitcast(BF16))
        for n in range(2):
            osb = olT
            qlT_v.lastw, qlT_v.reads = qlT.lastw, qlT.reads
            if t < NTP:
                def kfn_p(j, k0, nk):
                    r_ = kT_rows[j]
                    gidx = k0 // 512
                    ip, rk = gidx // 8, gidx % 8
                    off = ip * 512 + (k0 % 512)
                    return kT_all[j].ap[rk * r_:(rk + 1) * r_, off:off + nk]

                def vfn_p(k0, nk):
                    gidx = k0 // 512
                    ip, rk = gidx // 8, gidx % 8
                    loc = ip * 512 + (k0 % 512)
                    p_, i_ = loc // 1024, loc % 1024
                    return v_all[p_], v_all[p_].ap[rk * 1024 + i_:rk * 1024 + i_ + nk, :]
                attend(qlT_v, qrT, n, kT_all, kfn_p, vfn_p, 4096 * (t // 4 + 1), t * 2 + n, osb, olT_bf)
            else:
                sq = (t - NTP) * 2 + n
                attend(qlT_v, qrT, n, [kT_s[sq]] * 3,
                       lambda j, k0, nk, sq=sq: kT_s[sq].ap[j * 128:j * 128 + (128 if j < 2 else 64), k0:k0 + nk],
                       lambda k0, nk, sq=sq: (v_s[sq], v_s[sq].ap[k0:k0 + nk, :]), 1088, None, osb, olT_bf)
            qlT.reads = qlT_v.reads
        wbv, wvv = wload(s_uv, s_uv.ap.rearrange("(k p) n -> p k n", p=128), [128, 2, 1024])
        mixT = Ht[2]
        for h in range(8):
            bank = P[h // 4]
            for cc in range(2):
                rhs = olT_bf[:, cc * 1024:(cc + 1) * 1024].rearrange("p (n h q) -> p n h q", n=2, h=8)[:, :, h, :]
                E("pe", lambda e, h=h, cc=cc, bank=bank, rhs=rhs: e.matmul(
                    bank.ap[:, (h % 4) * 128:(h % 4 + 1) * 128].rearrange("p (n q) -> p n q", n=2),
                    lhsT=wvv[:, cc, h * 128:(h + 1) * 128], rhs=rhs, start=(cc == 0), stop=(cc == 1)), r=[wbv, olT], w=[bank])
        for hh in range(2):
            E("dve", lambda e, hh=hh: e.tensor_copy(out=mixT.ap[:, hh * 512:(hh + 1) * 512], in_=P[hh].ap), r=[P[hh]], w=[mixT])
        crossT = Ht[3]
        if t < NTP:
            segs = [(0, 128, 0)]
        else:
            for hs in range(2):
                sq = (t - NTP) * 2 + hs
                build_memset(hs, c_mk[0, sq], c_mv[0, sq], wsem)
            segs = [(0, 64, 0), (64, 64, 1)]
        cross_attn(xqT, segs, crossT)
        out_and_ffn(0, t, xb, mixT, crossT, s_out0, last=False)

    Dm = hc.ap[:, 0:128]
    maskT = hc.ap[:, 128:256]
    Ind = hc.ap[:, 256:260]

    def l1_tail(t, xb, osb_, gate, xqb):
        osq = Ft[1]
        E("dve", lambda e: e.tensor_tensor(out=osq.ap, in0=osb_.ap, in1=osb_.ap, op=ALU.mult), r=[osb_], w=[osq])
        r8 = small[3]
        E("dve", lambda e: e.tensor_reduce(out=r8.ap[:, 0:8], in_=osq.ap.rearrange("p (h v) -> p h v", h=8), axis=AX.X, op=ALU.add), r=[osq], w=[r8])
        E("act", lambda e: e.activation(out=r8.ap[:, 0:8], in_=r8.ap[:, 0:8], func=AF.Ln, bias=epsb.ap, scale=1.0 / 128), r=[r8, epsb], w=[r8])
        E("act", lambda e: e.activation(out=r8.ap[:, 0:8], in_=r8.ap[:, 0:8], func=AF.Exp, scale=-0.5), r=[r8], w=[r8])
        E("dve", lambda e: e.tensor_tensor(out=osb_.ap.rearrange("p (h v) -> p h v", h=8), in0=osb_.ap.rearrange("p (h v) -> p h v", h=8),
                                           in1=r8.ap[:, 0:8].unsqueeze(2).to_broadcast([128, 8, 128]), op=ALU.mult), r=[osb_, r8], w=[osb_])
        E("dve", lambda e: e.tensor_tensor(out=osb_.ap, in0=osb_.ap, in1=on_b.ap, op=ALU.mult), r=[osb_, on_b], w=[osb_])
        mixb, mixT = Ht[5], Ht[2]
        E("dve", lambda e: e.tensor_tensor(out=mixb.ap, in0=osb_.ap, in1=gate.ap, op=ALU.mult), r=[osb_, gate], w=[mixb])
        transposes(mixT, lambda i: mixT.ap[:, i * 128:(i + 1) * 128], mixb, lambda i: mixb.ap[:, i * 128:(i + 1) * 128], 8, whole=mixT.ap[:, 0:8 * 128])
        xqT, crossT = Ht[6], Ht[3]
        transposes(xqT, lambda i: xqT.ap[:, i * 128:(i + 1) * 128], xqb, lambda i: xqb.ap[:, i * 128:(i + 1) * 128], 4, whole=xqT.ap[:, 0:4 * 128])
        if t < NTP:
            segs = [(0, 128, 0)]
        else:
            for hs in range(2):
                sq = (t - NTP) * 2 + hs
                build_memset(hs, c_mk[1, sq], c_mv[1, sq], wsem)
            segs = [(0, 64, 0), (64, 64, 1)]
        cross_attn(xqT, segs, crossT)
        out_and_ffn(1, t, xb, mixT, crossT, s_out1, last=True)

    def l1_back(t):
        xb = front(t, 1, 0)
        osb_, gate, qcT, xqb, Sr = Ft[0], Ft[2], Ht[7], Ht[4], Ht[8]
        rr = slice(t * 128, (t + 1) * 128)
        k.dma("sp", osb_.ap, st_o.ap[rr, :], w=osb_, r=st_o)
        k.dma("sp", gate.ap, st_g.ap[rr, :], w=gate, r=st_g)
        k.dma("sp", qcT.ap, st_q.ap[rr, :], w=qcT, r=st_q)
        k.dma("sp", xqb.ap[:, 0:512], st_x.ap[rr, :], w=xqb, r=st_x)
        Sin = Ft[3]
        if t % 4 == 0:
            k.dma("sp", Sin.ap, snapd[t // 4].ap, w=Sin, r=snapd[t // 4])
        E("dve", lambda e: e.tensor_copy(out=Sr.ap, in_=Sin.ap), r=[Sin], w=[Sr])
        for h in range(8):
            bank = P[h // 4]
            E("pe", lambda e, h=h, bank=bank: e.matmul(bank.ap[:, (h % 4) * 128:(h % 4 + 1) * 128], lhsT=qcT.ap[:, h * 128:(h + 1) * 128],
                                                       rhs=Sr.ap[:, h * 128:(h + 1) * 128], start=True, stop=True), r=[qcT, Sr], w=[bank])
        for hh in range(2):
            E("dve", lambda e, hh=hh: e.tensor_tensor(out=osb_.ap[:, hh * 512:(hh + 1) * 512], in0=P[hh].ap, in1=osb_.ap[:, hh * 512:(hh + 1) * 512],
                                                      op=ALU.add), r=[P[hh], osb_], w=[osb_])
        l1_tail(t, xb, osb_, gate, xqb)

    def l1_tile(t, mode):
        passA = False
        frontm = (mode == "front")
        xb = front(t, 1, 0)
        if frontm:
            prefetch_next()
        hb, hTb = Ht[0], Ht[1]
        norm_T(xb, 0, hb, hTb)
        qs, sg, gate, logf, kk = Ft[0], Ft[1], Ft[2], Ft[3], Ft[4]
        vb, xqb = Ht[2], Ht[4]
        def do_groups(groups):
          for gi in groups:
              bank = P[gi % 4]
              proj(hTb, 8, s_in1, wcols(s_in1.ap, gi * 512, 512), 512, bank, kstep=8)
              c0 = (gi % 2) * 512
              if gi < 2:
                  E("act", lambda e, bank=bank, c0=c0: e.activation(out=qs.ap[:, c0:c0 + 512], in_=bank.ap, func=AF.Silu), r=[bank], w=[qs])
              elif gi < 4:
                  E("act", lambda e, bank=bank, c0=c0: e.activation(out=sg.ap[:, c0:c0 + 512], in_=bank.ap, func=AF.Sigmoid), r=[bank], w=[sg])
              elif gi < 6:
                  E("dve", lambda e, bank=bank, c0=c0: e.tensor_copy(out=vb.ap[:, c0:c0 + 512], in_=bank.ap), r=[bank], w=[vb])
              elif gi < 8:
                  E("act", lambda e, bank=bank, c0=c0: e.activation(out=gate.ap[:, c0:c0 + 512], in_=bank.ap, func=AF.Silu), r=[bank], w=[gate])
              else:
                  E("dve", lambda e, bank=bank: e.tensor_copy(out=xqb.ap[:, 0:512], in_=bank.ap), r=[bank], w=[xqb])
        do_groups([2, 3, 4, 5, 0, 1])
        E("dve", lambda e: e.tensor_tensor(out=sg.ap, in0=sg.ap, in1=oml.ap, op=ALU.mult), r=[sg, oml], w=[sg])
        E("dve", lambda e: e.tensor_tensor(out=logf.ap, in0=sg.ap, in1=lbb.ap, op=ALU.add), r=[sg, lbb], w=[logf])
        E("act", lambda e: e.activation(out=logf.ap, in_=logf.ap, func=AF.Ln), r=[logf], w=[logf])
        E("dve", lambda e: e.tensor_tensor(out=kk.ap, in0=oml.ap, in1=sg.ap, op=ALU.subtract), r=[sg, oml], w=[kk])
        E1 = Ft[5]
        ke, qe = Ht[5], Ht[6]
        for g in range(2):
            E("pe", lambda e, g=g: e.matmul(P[4 + g].ap, lhsT=Dm, rhs=logf.ap[:, g * 512:(g + 1) * 512], start=True, stop=True),
              r=[hc, logf], w=[P[4 + g]])
        for g in range(2):
            E("act", lambda e, g=g: e.activation(out=E1.ap[:, g * 512:(g + 1) * 512], in_=P[4 + g].ap, func=AF.Exp, scale=-1.0), r=[P[4 + g]], w=[E1])
        E("dve", lambda e: e.tensor_tensor(out=ke.ap, in0=kk.ap, in1=E1.ap, op=ALU.mult), r=[kk, E1], w=[ke])
        if not passA:
            for g in range(2):
                E("act", lambda e, g=g: e.activation(out=E1.ap[:, g * 512:(g + 1) * 512], in_=P[4 + g].ap, func=AF.Exp, scale=1.0), r=[P[4 + g]], w=[E1])
            E("dve", lambda e: e.tensor_tensor(out=qe.ap, in0=qs.ap, in1=E1.ap, op=ALU.mult), r=[qs, E1], w=[qe])
        st = P[4]
        for h in range(8):
            E("pe", lambda e, h=h: e.matmul(st.ap[:, h * 4:(h + 1) * 4], lhsT=logf.ap[:, h * 128:(h + 1) * 128], rhs=Ind, start=True, stop=True),
              r=[logf, hc], w=[st])
        cs = small[5]
        er = small[6]
        el = small[7]
        E("dve", lambda e: e.tensor_copy(out=stsb.ap, in_=st.ap[:, 0:32]), r=[st], w=[stsb])
        st3 = stsb.ap.rearrange("p (h f) -> p h f", h=8)
        E("act", lambda e: e.activation(out=cs.ap.rearrange("p (h n) -> p h n", h=8), in_=st3[:, :, 0:2], func=AF.Exp), r=[stsb], w=[cs])
        E("act", lambda e: e.activation(out=er.ap.rearrange("p (h n) -> p h n", h=8), in_=st3[:, :, 2:4], func=AF.Exp), r=[stsb], w=[er])
        E("dve", lambda e: e.tensor_tensor(out=el.ap.rearrange("p (h n) -> p h n", h=8), in0=st3[:, :, 0:2], in1=st3[:, :, 2:4], op=ALU.subtract),
          r=[stsb], w=[el])
        E("act", lambda e: e.activation(out=el.ap, in_=el.ap, func=AF.Exp), r=[el], w=[el])
        do_groups([6, 7, 8])
        if not passA:
            qeT, keT = Ht[7], Ht[8]
            transposes(qeT, lambda i: qeT.ap[:, i * 128:(i + 1) * 128], qe, lambda i: qe.ap[:, i * 128:(i + 1) * 128], 8, whole=qeT.ap[:, 0:8 * 128])
            transposes(keT, lambda i: keT.ap[:, i * 128:(i + 1) * 128], ke, lambda i: ke.ap[:, i * 128:(i + 1) * 128], 8, whole=keT.ap[:, 0:8 * 128])
            aTm = Ht[9]
            for h in range(8):
                bank = P[4 + h // 4]
                E("pe", lambda e, h=h, bank=bank: e.matmul(bank.ap[:, (h % 4) * 128:(h % 4 + 1) * 128], lhsT=keT.ap[:, h * 128:(h + 1) * 128],
                                                           rhs=qeT.ap[:, h * 128:(h + 1) * 128], start=True, stop=True), r=[keT, qeT], w=[bank])
            for hh in range(2):
                bank = P[4 + hh]
                E("dve", lambda e, hh=hh, bank=bank: e.tensor_tensor(out=aTm.ap[:, hh * 512:(hh + 1) * 512].rearrange("p (h t) -> p h t", h=4),
                                                                     in0=bank.ap.rearrange("p (h t) -> p h t", h=4),
                                                                     in1=maskT.unsqueeze(1).to_broadcast([128, 4, 128]), op=ALU.mult), r=[bank, hc], w=[aTm])
        S3 = Sst.ap.rearrange("p (h v) -> p h v", h=8)
        coef = small[4]
        if frontm and t % 4 == 0:
            E("dve", lambda e: e.memset(Sst.ap, 0.0), w=[Sst])
            E("dve", lambda e: e.memset(Dacc.ap, 1.0), w=[Dacc])
        for n in range(2):
            sq = (t - NTP) * 2 + n
            if t >= NTP:
                k.dma("sp", S3, c_st[sq].rearrange("h k v -> k h v"), w=Sst)
            r0 = n * 64
            for h in range(8):
                bank = P[2 + h // 4]
                E("pe", lambda e, h=h, bank=bank, r0=r0: e.matmul(bank.ap[:, (h % 4) * 128:(h % 4 + 1) * 128], lhsT=ke.ap[r0:r0 + 64, h * 128:(h + 1) * 128],
                                                                  rhs=vb.ap[r0:r0 + 64, h * 128:(h + 1) * 128], start=True, stop=True), r=[ke, vb], w=[bank])
            if not passA:
                Sr = Ht[10 + n]
                E("dve", lambda e, n=n, Sr=Sr: e.tensor_tensor(out=Sr.ap.rearrange("p (h v) -> p h v", h=8), in0=S3,
                                                                in1=er.ap.rearrange("p (h n) -> p h n", h=8)[:, :, n:n + 1].to_broadcast([128, 8, 128]),
                                                                op=ALU.mult), r=[Sst, er], w=[Sr])
                for h in range(8):
                    bank = P[h // 4]
                    oo = bank.ap[r0:r0 + 64, (h % 4) * 128:(h % 4 + 1) * 128]
                    E("pe", lambda e, h=h, oo=oo, r0=r0: e.matmul(oo, lhsT=aTm.ap[:, h * 128 + r0:h * 128 + r0 + 64], rhs=vb.ap[:, h * 128:(h + 1) * 128],
                                                                  start=True, stop=False), r=[aTm, vb], w=[bank])
                    E("pe", lambda e, h=h, oo=oo, r0=r0, Sr=Sr: e.matmul(oo, lhsT=qeT.ap[:, h * 128 + r0:h * 128 + r0 + 64], rhs=Sr.ap[:, h * 128:(h + 1) * 128],
                                                                         start=False, stop=True), r=[qeT, Sr], w=[bank])
            if frontm:
                E("dve", lambda e, n=n: e.tensor_tensor(out=coef.ap.rearrange("p (h n) -> p h n", h=8)[:, :, n],
                                                        in0=er.ap.rearrange("p (h n) -> p h n", h=8)[:, :, n], in1=Dacc.ap, op=ALU.mult),
                  r=[er, Dacc], w=[coef])
            t1 = Ft[5]
            for hh in range(2):
                E("dve", lambda e, hh=hh, n=n: e.tensor_tensor(
                    out=t1.ap[:, hh * 512:(hh + 1) * 512].rearrange("p (h v) -> p h v", h=4), in0=P[2 + hh].ap.rearrange("p (h v) -> p h v", h=4),
                    in1=el.ap.rearrange("p (h n) -> p h n", h=8)[:, hh * 4:(hh + 1) * 4, n:n + 1].to_broadcast([128, 4, 128]), op=ALU.mult),
                  r=[P[2 + hh], el], w=[t1])
            E("dve", lambda e, n=n: e.tensor_tensor(out=S3, in0=S3, in1=cs.ap.rearrange("p (h n) -> p h n", h=8)[:, :, n:n + 1].to_broadcast([128, 8, 128]),
                                                    op=ALU.mult), r=[Sst, cs], w=[Sst])
            E("dve", lambda e: e.tensor_tensor(out=Sst.ap, in0=Sst.ap, in1=t1.ap, op=ALU.add), r=[Sst, t1], w=[Sst])
            if frontm:
                E("dve", lambda e, n=n: e.tensor_tensor(out=Dacc.ap, in0=Dacc.ap, in1=cs.ap.rearrange("p (h n) -> p h n", h=8)[:, :, n], op=ALU.mult),
                  r=[Dacc, cs], w=[Dacc])
            if t >= NTP:
                k.dma("pool", sts_o[sq].rearrange("h k v -> k h v"), S3, r=Sst, final=True)
        osb_ = Ft[0]
        for hh in range(2):
            E("dve", lambda e, hh=hh: e.tensor_copy(out=osb_.ap[:, hh * 512:(hh + 1) * 512], in_=P[hh].ap), r=[P[hh]], w=[osb_])
        if frontm:
            if t % 4 == 3:
                sg_ = t // 4
                E("dve", lambda e: e.tensor_copy(out=exD_sb.ap[:, sg_ * 8:(sg_ + 1) * 8], in_=Dacc.ap), r=[Dacc], w=[exD_sb])
                k.dma("pool", exU_own[sg_].ap, Sst.ap, w=exU_own[sg_], r=Sst)
            qcT = Ht[3]
            E("dve", lambda e: e.tensor_tensor(out=qcT.ap.rearrange("p (h n q) -> p h n q", h=8, n=2),
                                               in0=qeT.ap.rearrange("p (h n q) -> p h n q", h=8, n=2),
                                               in1=coef.ap.rearrange("p (h n) -> p h n", h=8).unsqueeze(3).to_broadcast([128, 8, 2, 64]), op=ALU.mult),
              r=[qeT, coef], w=[qcT])
            rr = slice(t * 128, (t + 1) * 128)
            k.dma("pool", st_q.ap[rr, :], qcT.ap, w=st_q, r=qcT)
            k.dma("pool", st_o.ap[rr, :], osb_.ap, w=st_o, r=osb_)
            k.dma("pool", st_g.ap[rr, :], gate.ap, w=st_g, r=gate)
            k.dma("pool", st_x.ap[rr, :], xqb.ap[:, 0:512], w=st_x, r=xqb)
            return
        l1_tail(t, xb, osb_, gate, xqb)

    sample_cache_prep_v()
    cast_rest()
    phase0()
    load_gains(0)
    cits = sample_cache_iters()
    g1 = [lambda j=j: k.allgather(G4, kT_own[j], kT_g4[j]) for j in range(3)] + \
         [lambda j=j: k.allgather(G4, v_own[j], v_g4[j]) for j in range(2)]
    g2 = [lambda j=j: k.allgather(G2, kT_g4[j], kT_all[j]) for j in range(3)] + \
         [lambda j=j: k.allgather(G2, v_g4[j], v_all[j]) for j in range(2)]
    for t in range(NT):
        bg_run(5)
        l0_kpart(t)
        for _ in range(2):
            if cits:
                cits.pop(0)()
        if t >= NTP - 1 and g1:
            g1.pop(0)()
    ci = 0
    while cits:
        cits.pop(0)()
        ci += 1
        if ci % 6 == 0 and g1:
            g1.pop(0)()
    bg_run(len(bg))
    while g1:
        g1.pop(0)()
    cast_layer1()
    want_hT[0] = True
    order0 = list(range(NTP, NT)) + list(range(NTP))
    nxt0 = {order0[i]: ((order0[i + 1], 0) if i + 1 < len(order0) else (0, 1)) for i in range(len(order0))}
    nexttile[0] = nxt0[NTP]
    l0_tile(NTP)
    g2.pop(0)()
    g2.pop(0)()
    nexttile[0] = nxt0[NTP + 1]
    l0_tile(NTP + 1)
    while g2:
        g2.pop(0)()
    build_memset(0, memkv_s.ap[0, :, 0:512], memkv_s.ap[0, :, 512:1024], memkv_s)
    for t in range(NTP):
        nexttile[0] = nxt0[t]
        bg_run(4)
        l0_tile(t)
    bg_run(len(bg))
    want_hT[0] = False
    flush_ffn()
    load_gains(1)
    for t in range(NTP):
        nexttile[0] = (t + 1, 1)
        l1_tile(t, "front")
        if t % 4 == 3:
            sg_ = t // 4
            k.allgather(G4, exU_own[sg_], exU_g4[sg_])
            if sg_ >= 1:
                k.allgather(G2, exU_g4[sg_ - 1], exU_all[sg_ - 1])
    k.dma("pool", exD_own.ap, exD_sb.ap, w=exD_own, r=exD_sb)
    k.allgather(G4, exD_own, exD_g4)
    nexttile[0] = (NTP + 1, 1)
    l1_tile(NTP, "single")
    k.allgather(G2, exU_g4[3], exU_all[3])
    k.allgather(G2, exD_g4, exD_all)
    nexttile[0] = (0, 1)
    l1_tile(NTP + 1, "single")
    k.dma("sp", exDa_sb.ap, exD_all.ap.rearrange("(r p) f -> p r f", p=128), w=exDa_sb, r=exD_all)
    E("dve", lambda e: e.memset(Sst.ap, 0.0), w=[Sst])
    S3 = Sst.ap.rearrange("p (h v) -> p h v", h=8)
    snap = Ft[0]
    build_memset(0, memkv_s.ap[1, :, 0:512], memkv_s.ap[1, :, 512:1024], memkv_s)
    for sg_ in range(4):
        E("dve", lambda e: e.memset(snap.ap, 0.0), w=[snap])
        for r in range(8):
            eb = exb[r % 2]
            k.dma("sp", eb.ap[:, 0:1024], exU_all[sg_].ap[r * 128:(r + 1) * 128, :], w=eb, r=exU_all[sg_])
            m = scm.ap[:, r:r + 1]
            E("dve", lambda e, m=m: e.scalar_tensor_tensor(out=snap.ap, in0=Sst.ap, scalar=m, in1=snap.ap, op0=ALU.mult, op1=ALU.add),
              r=[Sst, scm, snap], w=[snap])
            E("dve", lambda e, r=r, sg_=sg_: e.tensor_tensor(out=S3, in0=S3,
                                                            in1=exDa_sb.ap[:, r, sg_ * 8:(sg_ + 1) * 8].unsqueeze(2).to_broadcast([128, 8, 128]), op=ALU.mult),
              r=[Sst, exDa_sb], w=[Sst])
            E("dve", lambda e, eb=eb: e.tensor_tensor(out=Sst.ap, in0=Sst.ap, in1=eb.ap[:, 0:1024], op=ALU.add), r=[Sst, eb], w=[Sst])
        k.dma("pool", snapd[sg_].ap, snap.ap, w=snapd[sg_], r=snap)
        if sg_ == 3:
            k.dma("pool", stp_o.rearrange("h k v -> k h v"), S3, r=Sst, final=True)
        for t in range(4 * sg_, 4 * sg_ + 4):
            nexttile[0] = (t + 1, 1) if t + 1 < NTP else None
            l1_back(t)
    flush_ffn()
    k.finish()
    return nc


_ROPE_THETA = 10000.0


def _tok_base(c, t):
    return 512 * (8 * (t // 4) + c) + 128 * (t % 4)


def _tables(c):
    inv = np.power(np.float32(_ROPE_THETA), -np.arange(32, dtype=np.float32) / np.float32(32)).astype(np.float32)
    pos_p = np.concatenate([_tok_base(c, t) + np.arange(128) for t in range(NTP)]).astype(np.float32)
    pos_s = (1024 + np.arange(64)).astype(np.float32)
    pos = np.concatenate([pos_p] + [pos_s] * 4)
    ang = (pos[:, None] * inv[None, :]).astype(np.float32)
    cos = np.tile(np.cos(ang).astype(np.float32), (1, 8))
    sin = np.tile(np.sin(ang).astype(np.float32), (1, 8))
    qc = np.arange(32)[:, None, None]
    p = np.arange(128)[None, :, None]
    vb = np.arange(128)[None, None, :]
    gq = (8 * (qc // 8) + c) * 8 + qc % 8
    grp = vb // 4
    kchunk = (8 * (grp // 8) + grp % 8) * 8 + 2 * (vb % 4) + p // 64
    vis = kchunk <= gq
    maskb = np.where(vis, 0.0, NEGB).astype(np.float32)
    scanm = np.tile((np.arange(8) == c).astype(np.float32)[None, :], (128, 1))
    s = np.arange(128)[:, None]
    t = np.arange(128)[None, :]
    same = (s // 64) == (t // 64)
    tri = (same & (s <= t)).astype(np.float32)
    triref = (same & ((s % 64) <= 32)).astype(np.float32)
    Dm = tri - triref
    ind = np.zeros((128, 4), np.float32)
    sl = np.arange(128)
    ind[:, 0] = (sl < 64)
    ind[:, 1] = (sl >= 64)
    ind[:, 2] = (sl < 64) & (sl % 64 <= 32)
    ind[:, 3] = (sl >= 64) & (sl % 64 <= 32)
    hconst = np.concatenate([Dm, tri, ind], axis=1).astype(np.float32)
    return cos, sin, maskb, scanm, hconst


def kernel(x_prompt, x_sample, cache_mla_latent, cache_mla_krope, cache_hgrn_state,
           cache_mem_k, cache_mem_v, mem_prompt,
           ln_mix_pre, ln_mix_post, ln_ffn_pre, ln_ffn_post, mem_norm, w_mem_kv,
           mla_w_in, mla_q_norm, mla_kv_norm, mla_w_uq, mla_w_uk, mla_w_uv, mla_w_out,
           hgrn_w_in, hgrn_lb, hgrn_o_norm, hgrn_w_out, w_ffn_up, w_ffn_down):
    f = lambda a: np.ascontiguousarray(np.asarray(a, dtype=np.float32))
    x_prompt, x_sample = f(x_prompt), f(x_sample)
    gains = np.stack([f(ln_mix_pre)[0], f(ln_mix_pre)[1], f(ln_mix_post)[0], f(ln_mix_post)[1],
                      f(ln_ffn_pre)[0], f(ln_ffn_pre)[1], f(ln_ffn_post)[0], f(ln_ffn_post)[1],
                      f(mem_norm)[0], f(mem_norm)[1]], axis=0)
    shared = {
        "memp": f(mem_prompt)[0], "gains": f(gains), "qn_g": f(mla_q_norm)[0], "kvn_g": f(mla_kv_norm)[0],
        "on_g": f(np.tile(f(hgrn_o_norm)[0], 8)), "hlb": f(hgrn_lb), "w_memkv": f(w_mem_kv), "w_in0": f(mla_w_in)[0],
        "w_uq": f(mla_w_uq)[0], "w_ukT": f(np.transpose(f(mla_w_uk)[0], (1, 2, 0))),
        "w_uv": f(f(mla_w_uv)[0].reshape(256, 1024)), "w_out0": f(mla_w_out)[0], "w_in1": f(hgrn_w_in)[0],
        "w_out1": f(hgrn_w_out)[0], "w_up": f(w_ffn_up), "w_dn": f(w_ffn_down),
    }
    in_maps = []
    for c in range(NCORE):
        cos, sin, maskb, scanm, hconst = _tables(c)
        sl = slice(4 * c, 4 * c + 4)
        m = dict(shared)
        m["xin"] = f(np.concatenate([x_prompt[0, _tok_base(c, t):_tok_base(c, t) + 128] for t in range(NTP)]
                                    + [x_sample[sl].reshape(256, D)], axis=0))
        m["c_lat"] = f(np.asarray(cache_mla_latent)[0, sl])
        m["c_kr"] = f(np.asarray(cache_mla_krope)[0, sl])
        m["c_st"] = f(np.asarray(cache_hgrn_state)[0, sl])
        m["c_mk"] = f(np.asarray(cache_mem_k)[:, sl].reshape(2, 4, 256, 512))
        m["c_mv"] = f(np.asarray(cache_mem_v)[:, sl].reshape(2, 4, 256, 512))
        m["cos_t"], m["sin_t"], m["maskb"], m["scanm"], m["hconst"] = cos, sin, maskb, scanm, hconst
        in_maps.append(m)
    nc = build()
    res = run_bass_kernel_spmd(nc, in_maps, core_ids=list(range(NCORE)))
    R = res.results
    y_p = np.zeros((1, 16384, D), np.float32)
    lat_p = np.zeros((1, 1, 16384, 256), np.float32)
    kr_p = np.zeros((1, 1, 16384, 64), np.float32)
    for c in range(NCORE):
        for t in range(NTP):
            b0 = _tok_base(c, t)
            y_p[0, b0:b0 + 128] = R[c]["y_o"][t * 128:(t + 1) * 128]
            lat_p[0, 0, b0:b0 + 128] = R[c]["lat_o"][t * 128:(t + 1) * 128]
            kr_p[0, 0, b0:b0 + 128] = R[c]["kr_o"][t * 128:(t + 1) * 128]
    y_s = np.concatenate([R[c]["y_o"][2048:2304].reshape(4, 64, D) for c in range(NCORE)], 0)
    st_p = R[NCORE - 1]["stp_o"][None, None]
    mk_p = R[0]["mk_o"].reshape(2, 1, 256, 4, 128)
    mv_p = R[0]["mv_o"].reshape(2, 1, 256, 4, 128)
    lat_s = np.concatenate([R[c]["lat_o"][2048:2304].reshape(4, 64, 256) for c in range(NCORE)], 0)[None]
    kr_s = np.concatenate([R[c]["kr_o"][2048:2304].reshape(4, 64, 64) for c in range(NCORE)], 0)[None]
    st_s = np.concatenate([R[c]["sts_o"] for c in range(NCORE)], 0)[None]
    o = lambda a: np.ascontiguousarray(a, dtype=np.float32)
    return (o(y_p), o(y_s), o(lat_p), o(kr_p), o(st_p), o(mk_p), o(mv_p), o(lat_s), o(kr_s), o(st_s))
```

```python
import numpy as np
import concourse.bass as bass
import concourse.mybir as mybir
from concourse.bass_utils import run_bass_kernel_spmd

F32 = mybir.dt.float32
BF16 = mybir.dt.bfloat16
AF = mybir.ActivationFunctionType
ALU = mybir.AluOpType
AX = mybir.AxisListType
NCORE = 8
D = 1024
NTP = 16
NTS = 2
NT = NTP + NTS
EPS = 1e-6
NEGB = -30000.0
G4 = [[0, 1, 2, 3], [4, 5, 6, 7]]
G2 = [[0, 4], [1, 5], [2, 6], [3, 7]]


class Buf:
    def __init__(self, ap, sem=None):
        self.ap = ap
        self.sem = sem
        self.qs = {}
        self.lastw = {}
        self.reads = {}


class QSem:
    def __init__(self, sem):
        self.sem = sem
        self.tot = 0

    def __getitem__(self, key):
        return self.ap[key]


def _merge(d, evs):
    for key, ev in evs.items():
        if key not in d or d[key][1] < ev[1]:
            d[key] = ev


class KB:
    def __init__(self, nc):
        self.nc = nc
        self.E = dict(pe=nc.tensor, act=nc.scalar, dve=nc.vector, pool=nc.gpsimd, sp=nc.sync)
        self.psem = {e: nc.alloc_semaphore("prog_" + e) for e in ("pe", "act", "dve", "pool")}
        self.cnt = {e: 0 for e in self.psem}
        self.known = {e: {} for e in self.E}
        self.n = 0
        self.ccsem = nc.alloc_semaphore("ccsem")
        self.cccnt = 0
        self.outs = {}

    def _name(self, p):
        self.n += 1
        return "%s%d" % (p, self.n)

    def sb(self, shape, dt=F32, dma=False):
        t = self.nc.alloc_sbuf_tensor(self._name("sb"), list(shape), dt)
        return Buf(t.ap(), True if dma else None)

    def ps(self, shape, dt=F32):
        t = self.nc.alloc_psum_tensor(self._name("ps"), list(shape), dt)
        return Buf(t.ap())

    def dram(self, shape, dt=F32, dma=False):
        t = self.nc.dram_tensor(self._name("dr"), list(shape), dt)
        return Buf(t.ap(), True if dma else None)

    def _wait(self, eng, evs, skip=None):
        for key, ev in evs.items():
            sem, val = ev[0], ev[1]
            if skip is not None and key == skip:
                continue
            if len(ev) > 2 and ev[2] is not None:
                val = max(val, ev[2].tot)
            if self.known[eng].get(key, 0) >= val:
                continue
            self.E[eng].wait_ge(sem, val)
            self.known[eng][key] = val

    def op(self, eng, fn, r=(), w=()):
        deps = {}
        for b in r:
            _merge(deps, b.lastw)
        for b in w:
            _merge(deps, b.lastw)
            _merge(deps, b.reads)
        self._wait(eng, deps, skip=("p", "pe") if eng == "pe" else None)
        ins = fn(self.E[eng])
        self.cnt[eng] += 1
        ins.then_inc(self.psem[eng], 1)
        ev = {("p", eng): (self.psem[eng], self.cnt[eng])}
        for b in w:
            b.lastw = dict(ev)
            b.reads = {}
        for b in r:
            _merge(b.reads, ev)

    def dma(self, q, out, in_, w=None, r=None, final=False):
        deps = {}
        if r is not None:
            _merge(deps, r.lastw)
        if w is not None:
            _merge(deps, {kk: vv for kk, vv in w.lastw.items() if kk[0] != "d"})
            _merge(deps, w.reads)
        self._wait(q, deps)
        sb = w if (w is not None and w.sem is not None) else r
        assert sb is not None and sb.sem is not None
        if q not in sb.qs:
            sb.qs[q] = QSem(self.nc.alloc_semaphore(self._name("dq")))
        qs = sb.qs[q]
        self.E[q].dma_start(out=out, in_=in_).then_inc(qs.sem, 16)
        qs.tot += 16
        ev = {("d", id(qs)): (qs.sem, qs.tot, qs)}
        if w is not None:
            _merge(w.lastw, ev)
            w.reads = {}
        if r is not None:
            _merge(r.reads, ev)
        if final:
            _merge(self.outs, ev)

    def allgather(self, groups, src, dst):
        deps = {}
        _merge(deps, src.lastw)
        _merge(deps, dst.lastw)
        _merge(deps, dst.reads)
        _merge(deps, getattr(self, "lastcc", {}))
        self._wait("pool", deps)
        sem = self.nc.alloc_semaphore(self._name("cc"))
        self.nc.gpsimd.collective_compute("AllGather", ALU.bypass, replica_groups=groups,
                                          ins=[src.ap], outs=[dst.ap]).then_inc(sem)
        ev = {("cc", id(sem)): (sem, 1)}
        self.lastcc = dict(ev)
        dst.lastw = dict(ev)
        dst.reads = {}
        _merge(src.reads, ev)

    def finish(self):
        self._wait("pool", self.outs)


def build():
    nc = bass.Bass("TRN2", target_bir_lowering=False)
    k = KB(nc)

    def din(name, shape, dt=F32):
        return nc.dram_tensor(name, list(shape), dt, kind="ExternalInput").ap()

    def dout(name, shape):
        return nc.dram_tensor(name, list(shape), F32, kind="ExternalOutput").ap()

    xin = din("xin", [NT * 128, D])
    c_lat = din("c_lat", [4, 1024, 256])
    c_kr = din("c_kr", [4, 1024, 64])
    c_st = din("c_st", [4, 8, 128, 128])
    c_mk = din("c_mk", [2, 4, 256, 512])
    c_mv = din("c_mv", [2, 4, 256, 512])
    memp = din("memp", [256, D])
    gains = din("gains", [10, D])
    qn_g = din("qn_g", [384])
    kvn_g = din("kvn_g", [256])
    on_g = din("on_g", [1024])
    hlb = din("hlb", [2, 1024])
    w_memkv = din("w_memkv", [2, 1024, 1024])
    w_in0 = din("w_in0", [1024, 1216])
    w_uq = din("w_uq", [384, 1536])
    w_ukT = din("w_ukT", [8, 128, 256])
    w_uv = din("w_uv", [256, 1024])
    w_out0 = din("w_out0", [1536, 1024])
    w_in1 = din("w_in1", [1024, 4608])
    w_out1 = din("w_out1", [1536, 1024])
    w_up = din("w_up", [2, 1024, 4096])
    w_dn = din("w_dn", [2, 4096, 1024])
    cos_t = din("cos_t", [NT * 128, 256])
    sin_t = din("sin_t", [NT * 128, 256])
    maskb = din("maskb", [32, 128, 128])
    scanm = din("scanm", [128, 8])
    hconst = din("hconst", [128, 128 + 128 + 4])

    y_o = dout("y_o", [NT * 128, D])
    lat_o = dout("lat_o", [NT * 128, 256])
    kr_o = dout("kr_o", [NT * 128, 64])
    stp_o = dout("stp_o", [8, 128, 128])
    mk_o = dout("mk_o", [2, 256, 512])
    mv_o = dout("mv_o", [2, 256, 512])
    sts_o = dout("sts_o", [4, 8, 128, 128])

    def wscr(shape):
        return k.dram(shape, BF16)
    s_memkv = wscr([2, 1024, 1024]); s_in0 = wscr([1024, 1216]); s_uq = wscr([384, 1536])
    s_ukT = wscr([8, 128, 256]); s_uv = wscr([256, 1024]); s_out0 = wscr([1536, 1024])
    s_in1 = wscr([1024, 4608]); s_out1 = wscr([1536, 1024]); s_up = [wscr([1024, 4096]) for _ in range(2)]
    s_dn = [wscr([4096, 1024]) for _ in range(2)]
    kT_rows = [128, 128, 64]
    kT_own = [k.dram([r_, 2048], BF16) for r_ in kT_rows]
    kT_g4 = [k.dram([4 * r_, 2048], BF16) for r_ in kT_rows]
    kT_all = [k.dram([8 * r_, 2048], BF16) for r_ in kT_rows]
    v_own = [k.dram([1024, 256], BF16) for _ in range(2)]
    v_g4 = [k.dram([4 * 1024, 256], BF16) for _ in range(2)]
    v_all = [k.dram([8 * 1024, 256], BF16) for _ in range(2)]
    kT_s = [k.dram([320, 1088], BF16) for _ in range(4)]
    v_s = [k.dram([1088, 256], BF16, dma=True) for _ in range(4)]
    x1_s = k.dram([NT * 128, D], F32)
    hT_s = k.dram([NT * 128, D], BF16)
    memkv_s = k.dram([2, 256, 1024], F32)
    exU_own = [k.dram([128, 1024], F32) for _ in range(4)]
    exU_g4 = [k.dram([4 * 128, 1024], F32) for _ in range(4)]
    exU_all = [k.dram([8 * 128, 1024], F32) for _ in range(4)]
    exD_own = k.dram([128, 32], F32); exD_g4 = k.dram([4 * 128, 32], F32); exD_all = k.dram([8 * 128, 32], F32)
    snapd = [k.dram([128, 1024], F32) for _ in range(4)]
    st_o = k.dram([NTP * 128, 1024], F32); st_g = k.dram([NTP * 128, 1024], F32)
    st_q = k.dram([NTP * 128, 1024], BF16); st_x = k.dram([NTP * 128, 512], BF16)

    ident = k.sb([128, 128], BF16)
    epsb = k.sb([128, 1], F32)
    ones_f = k.sb([128, 128], F32)
    gb = [k.sb([128, D], F32, dma=True) for _ in range(4)]
    qn_b = k.sb([128, 384], F32, dma=True)
    kvn_b = k.sb([128, 256], F32, dma=True)
    on_b = k.sb([128, 1024], F32, dma=True)
    lbb = k.sb([128, 1024], F32, dma=True)
    oml = k.sb([128, 1024], F32, dma=True)
    hc = k.sb([128, 260], F32, dma=True)
    scm = k.sb([128, 8], F32, dma=True)
    xr = [k.sb([128, D], F32, dma=True) for _ in range(4)]
    WS = 4096
    wr = [k.sb([128, WS], BF16, dma=True) for _ in range(6)]
    wri = [0]
    ktr = [k.sb([128, 3, 512], BF16, dma=True) for _ in range(3)]
    vtr = [k.sb([128, 4, 258], BF16, dma=True) for _ in range(3)]
    kvi = [0]
    mkr = [k.sb([128, 128], F32, dma=True) for _ in range(2)]
    cosr = [k.sb([128, 256], F32, dma=True) for _ in range(2)]
    sinr = [k.sb([128, 256], F32, dma=True) for _ in range(2)]
    Ft = [k.sb([128, 1024], F32, dma=True) for _ in range(7)]
    Ht = [k.sb([128, 1024], BF16, dma=True) for _ in range(14)]
    qsb = k.sb([128, 1536], F32)
    small = [k.sb([128, 16], F32) for _ in range(8)]
    exb = [k.sb([128, 1032], F32, dma=True) for _ in range(2)]
    Sst = k.sb([128, 1024], F32, dma=True)
    Dacc = k.sb([128, 8], F32)
    exD_sb = k.sb([128, 32], F32, dma=True)
    exDa_sb = k.sb([128, 8, 32], F32, dma=True)
    stsb = k.sb([128, 32], F32)
    mkT = [k.sb([128, 4, 256], BF16) for _ in range(2)]
    mv = [k.sb([128, 2, 4, 130], BF16, dma=True) for _ in range(2)]
    mstage = k.sb([128, 2, 512], BF16, dma=True)
    P = [k.ps([128, 512], F32) for _ in range(6)]
    PT = [k.ps([128, 1024], BF16) for _ in range(2)]
    pti = [0]

    E = k.op

    E("pool", lambda e: e.memset(ident.ap, 0.0), w=[ident])
    E("pool", lambda e: e.affine_select(out=ident.ap, in_=ident.ap, pattern=[[-1, 128]], compare_op=ALU.not_equal,
                                        fill=1.0, base=0, channel_multiplier=1), r=[ident], w=[ident])
    E("pool", lambda e: e.memset(epsb.ap, EPS), w=[epsb])
    E("pool", lambda e: e.memset(ones_f.ap, 1.0), w=[ones_f])
    for vb in vtr:
        E("pool", lambda e, vb=vb: e.memset(vb.ap, 1.0), w=[vb])
    for kb_ in ktr:
        E("pool", lambda e, kb_=kb_: e.memset(kb_.ap[64:128, 2, :], 0.0), w=[kb_])
    for m in mv:
        E("pool", lambda e, m=m: e.memset(m.ap, 1.0), w=[m])
    k.dma("sp", qn_b.ap, qn_g.partition_broadcast(128), w=qn_b)
    k.dma("sp", kvn_b.ap, kvn_g.partition_broadcast(128), w=kvn_b)
    k.dma("sp", on_b.ap, on_g.partition_broadcast(128), w=on_b)
    k.dma("sp", hc.ap, hconst, w=hc)
    k.dma("sp", scm.ap, scanm, w=scm)
    k.dma("sp", lbb.ap, hlb[1].partition_broadcast(128), w=lbb)
    k.dma("sp", oml.ap, hlb[0].partition_broadcast(128), w=oml)
    E("dve", lambda e: e.tensor_tensor(out=lbb.ap, in0=lbb.ap, in1=oml.ap, op=ALU.subtract), r=[oml, lbb], w=[lbb])
    E("act", lambda e: e.activation(out=lbb.ap, in_=lbb.ap, func=AF.Sigmoid), r=[lbb], w=[lbb])
    E("dve", lambda e: e.tensor_scalar(out=oml.ap, in0=lbb.ap, scalar1=-1.0, scalar2=1.0, op0=ALU.mult, op1=ALU.add),
      r=[lbb], w=[oml])

    wsem = Buf(None, True)
    wsem0 = Buf(None, True)

    bg = []

    def bg_run(n):
        for _ in range(n):
            if bg:
                bg.pop(0)()

    def castw(dst, src2d, dst2d, rows, ws, defer=False):
        step = 128
        for r0 in range(0, rows, step):
            fn = lambda r0=r0: k.dma("pool", dst2d[r0:r0 + step], src2d[r0:r0 + step], w=dst, r=ws)
            if defer:
                bg.append(fn)
            else:
                fn()

    def cast_fin(ws, scs):
        _q = ws.qs["pool"]
        _fin = {("d", id(_q)): (_q.sem, _q.tot, _q)}
        for sc in scs:
            sc.lastw = dict(_fin)
    castw(s_memkv, w_memkv.rearrange("l k n -> (l k) n"), s_memkv.ap.rearrange("l k n -> (l k) n"), 2048, wsem0)
    castw(s_in0, w_in0, s_in0.ap, 1024, wsem0)
    cast_fin(wsem0, (s_memkv, s_in0))

    wsemB = Buf(None, True)

    def cast_rest():
        castw(s_uq, w_uq, s_uq.ap, 384, wsem, defer=True)
        castw(s_ukT, w_ukT.rearrange("h d c -> (h d) c"), s_ukT.ap.rearrange("h d c -> (h d) c"), 1024, wsem, defer=True)
        castw(s_uv, w_uv, s_uv.ap, 256, wsem, defer=True)
        castw(s_out0, w_out0, s_out0.ap, 1536, wsem, defer=True)
        castw(s_up[0], w_up[0], s_up[0].ap, 1024, wsem, defer=True)
        castw(s_dn[0], w_dn[0], s_dn[0].ap, 4096, wsem, defer=True)

    def cast_layer1():
        castw(s_in1, w_in1, s_in1.ap, 1024, wsemB, defer=True)
        castw(s_out1, w_out1, s_out1.ap, 1536, wsemB, defer=True)
        castw(s_up[1], w_up[1], s_up[1].ap, 1024, wsemB, defer=True)
        castw(s_dn[1], w_dn[1], s_dn[1].ap, 4096, wsemB, defer=True)

    def wload(src_buf, src_ap, shape):
        b = wr[wri[0] % 6]
        wri[0] += 1
        n = shape[1] * shape[2]
        assert n <= WS
        view = b.ap[:, 0:n].rearrange("p (a b) -> p a b", a=shape[1])
        k.dma("sp", view, src_ap, w=b, r=src_buf)
        return b, view

    def rstd_of(src_ap, src_buf, n, junk, out_small, col=0):
        o = out_small.ap[:, col:col + 1]
        E("act", lambda e: e.activation(out=junk.ap[:, 0:n], in_=src_ap, func=AF.Square, accum_out=o),
          r=[src_buf], w=[junk, out_small])
        E("act", lambda e: e.activation(out=o, in_=o, func=AF.Ln, bias=epsb.ap, scale=1.0 / n), r=[out_small, epsb], w=[out_small])
        E("act", lambda e: e.activation(out=o, in_=o, func=AF.Exp, scale=-0.5), r=[out_small], w=[out_small])
        return o

    def transposes(dst_buf, dst_fn, src_buf, src_fn, n, rows=128, cols=128, whole=None, ptview=None):
        pt = PT[pti[0] % 2]
        pti[0] += 1
        for i in range(n):
            E("pe", lambda e, i=i: e.transpose(out=pt.ap[0:cols, i * 128:i * 128 + rows], in_=src_fn(i), identity=ident.ap[0:rows, 0:rows]),
              r=[src_buf, ident], w=[pt])
        if whole is not None and rows == 128:
            src = pt.ap[0:cols, 0:n * 128]
            if ptview is not None:
                src = ptview(src)
            E("dve", lambda e: e.tensor_copy(out=whole, in_=src), r=[pt], w=[dst_buf])
            return
        for i in range(n):
            E("dve", lambda e, i=i: e.tensor_copy(out=dst_fn(i), in_=pt.ap[0:cols, i * 128:i * 128 + rows]), r=[pt], w=[dst_buf])

    def load_gains(l):
        for j in range(4):
            k.dma("sp", gb[j].ap, gains[2 * j + l].partition_broadcast(128), w=gb[j])

    xissued = {}
    nexttile = [None]

    xseq = [0]
    hpre = [k.sb([128, D], BF16, dma=True) for _ in range(2)]
    hseq = [0]
    hissued = {}
    want_hT = [False]

    def xprefetch(t, l):
        if (t, l) in xissued:
            return
        slot = xseq[0] % 4
        xseq[0] += 1
        xissued[(t, l)] = slot
        xb = xr[slot]
        if l == 0:
            k.dma("sp", xb.ap, xin[t * 128:(t + 1) * 128, :], w=xb)
            if want_hT[0]:
                hs = hseq[0] % 2
                hseq[0] += 1
                hissued[t] = hs
                k.dma("sp", hpre[hs].ap, hT_s.ap[t * 128:(t + 1) * 128, :], w=hpre[hs], r=hT_s)
        else:
            k.dma("sp", xb.ap, x1_s.ap[t * 128:(t + 1) * 128, :], w=xb, r=x1_s)

    def prefetch_next():
        if nexttile[0] is not None:
            xprefetch(*nexttile[0])

    def front(t, l, gi):
        xprefetch(t, l)
        return xr[xissued.pop((t, l))]

    def norm_T(xb, gi, hb, hTb, sm=None):
        sm = small[0] if sm is None else sm
        rs = rstd_of(xb.ap, xb, D, Ht[13], sm)
        E("dve", lambda e: e.scalar_tensor_tensor(out=hb.ap, in0=xb.ap, scalar=rs, in1=gb[gi].ap, op0=ALU.mult, op1=ALU.mult),
          r=[xb, sm, gb[gi]], w=[hb])
        transposes(hTb, lambda i: hTb.ap[:, i * 128:(i + 1) * 128], hb, lambda i: hb.ap[:, i * 128:(i + 1) * 128], 8, whole=hTb.ap[:, 0:8 * 128])

    def proj(hTb, nk, wsrc_buf, wsrc_fn, ncols, pbank, kstep=4):
        for k0 in range(0, nk, kstep):
            kn = min(kstep, nk - k0)
            wb, wv = wload(wsrc_buf, wsrc_fn(k0, kn), [128, kn, ncols])
            for kk in range(kn):
                kc = k0 + kk
                E("pe", lambda e, kc=kc, kk=kk, wv=wv: e.matmul(pbank.ap[:, 0:ncols], lhsT=hTb.ap[:, kc * 128:(kc + 1) * 128],
                                                           rhs=wv[:, kk, :], start=(kc == 0), stop=(kc == nk - 1)),
                  r=[hTb, wb], w=[pbank])

    def wcols(sbuf2d, c0, ncols):
        return lambda k0, kn: sbuf2d[k0 * 128:(k0 + kn) * 128, c0:c0 + ncols].rearrange("(k p) n -> p k n", p=128)

    def phase0():
        for l in range(2):
            k.dma("sp", gb[0].ap, gains[8 + l].partition_broadcast(128), w=gb[0])
            for mb in range(2):
                xb = xr[mb % 2]
                k.dma("sp", xb.ap, memp[mb * 128:(mb + 1) * 128, :], w=xb)
                norm_T(xb, 0, Ht[0], Ht[1])
                for g in range(2):
                    proj(Ht[1], 8, s_memkv, wcols(s_memkv.ap[l], g * 512, 512), 512, P[g])
                    E("dve", lambda e, g=g: e.tensor_copy(out=Ft[0].ap[:, g * 512:(g + 1) * 512], in_=P[g].ap), r=[P[g]], w=[Ft[0]])
                k.dma("pool", mk_o[l, mb * 128:(mb + 1) * 128, :], Ft[0].ap[:, 0:512], r=Ft[0], final=True)
                k.dma("pool", mv_o[l, mb * 128:(mb + 1) * 128, :], Ft[0].ap[:, 512:1024], r=Ft[0], final=True)
                k.dma("pool", memkv_s.ap[l, mb * 128:(mb + 1) * 128, :], Ft[0].ap, w=memkv_s, r=Ft[0])

    def build_memset(slot, ksrc_ap, vsrc_ap, src_buf):
        k.dma("pool", mstage.ap, ksrc_ap.rearrange("(b p) n -> p b n", p=128), w=mstage, r=src_buf)
        for blk in range(2):
            transposes(mkT[slot], lambda h, blk=blk: mkT[slot].ap[:, h, blk * 128:(blk + 1) * 128],
                       mstage, lambda h, blk=blk: mstage.ap[:, blk, h * 128:(h + 1) * 128], 4,
                       whole=mkT[slot].ap[:, :, blk * 128:(blk + 1) * 128], ptview=lambda a: a.rearrange("p (h m) -> p h m", h=4))
        for blk in range(2):
            k.dma("pool", mv[slot].ap[:, blk, :, 0:128], vsrc_ap[blk * 128:(blk + 1) * 128, :].rearrange("p (h d) -> p h d", h=4), w=mv[slot], r=src_buf)

    def cross_attn(xqT, segs, crossT):
        pTx = Ht[11]
        csb = Ht[12]
        for (t0, ntk, slot) in segs:
            for h in range(4):
                for blk in range(2):
                    bank = P[4 + (h // 2)]
                    off = ((h % 2) * 2 + blk) * 128
                    E("pe", lambda e, h=h, blk=blk, bank=bank, off=off: e.matmul(
                        bank.ap[:, off + t0:off + t0 + ntk], lhsT=mkT[slot].ap[:, h, blk * 128:(blk + 1) * 128],
                        rhs=xqT.ap[:, h * 128 + t0:h * 128 + t0 + ntk], start=True, stop=True), r=[mkT[slot], xqT], w=[bank])
        for hb in range(2):
            E("act", lambda e, hb=hb: e.activation(out=pTx.ap[:, hb * 512:(hb + 1) * 512], in_=P[4 + hb].ap, func=AF.Exp,
                                                   scale=128.0 ** -0.5), r=[P[4 + hb]], w=[pTx])
        for (t0, ntk, slot) in segs:
            for h in range(4):
                bank = P[h // 2]
                for blk in range(2):
                    off = (h * 2 + blk) * 128
                    E("pe", lambda e, h=h, blk=blk, bank=bank, off=off: e.matmul(
                        bank.ap[t0:t0 + ntk, (h % 2) * 130:(h % 2) * 130 + 129], lhsT=pTx.ap[:, off + t0:off + t0 + ntk],
                        rhs=mv[slot].ap[:, blk, h, 0:129], start=(blk == 0), stop=(blk == 1)), r=[pTx, mv[slot]], w=[bank])
        for h in range(4):
            bank = P[h // 2]
            c0 = (h % 2) * 130
            E("dve", lambda e, h=h, bank=bank, c0=c0: e.reciprocal(out=small[1].ap[:, h:h + 1], in_=bank.ap[:, c0 + 128:c0 + 129]),
              r=[bank], w=[small[1]])
            E("dve", lambda e, h=h, bank=bank, c0=c0: e.tensor_scalar(out=csb.ap[:, h * 128:(h + 1) * 128], in0=bank.ap[:, c0:c0 + 128],
                                                                   scalar1=small[1].ap[:, h:h + 1], scalar2=None, op0=ALU.mult),
              r=[bank, small[1]], w=[csb])
        transposes(crossT, lambda h: crossT.ap[:, h * 128:(h + 1) * 128], csb, lambda h: csb.ap[:, h * 128:(h + 1) * 128], 4, whole=crossT.ap[:, 0:512])

    def post_add(xb, gi, src=None):
        if src is None:
            z = Ft[6]
            for g in range(2):
                E("dve", lambda e, g=g: e.tensor_copy(out=z.ap[:, g * 512:(g + 1) * 512], in_=P[g].ap), r=[P[g]], w=[z])
        else:
            z = src
        rs = rstd_of(z.ap, z, D, Ht[13], small[2])
        E("dve", lambda e: e.scalar_tensor_tensor(out=z.ap, in0=z.ap, scalar=rs, in1=gb[gi].ap, op0=ALU.mult, op1=ALU.mult),
          r=[z, small[2], gb[gi]], w=[z])
        E("dve", lambda e: e.tensor_tensor(out=xb.ap, in0=xb.ap, in1=z.ap, op=ALU.add), r=[xb, z], w=[xb])

    pending = []

    def ffn_pair(items):
        l = items[0][0]
        hbs = [(Ht[0], Ht[1]), (Ht[2], Ht[3])]
        accs = [Ft[0], Ft[1]]
        rls = [Ft[5], Ft[4]]
        hids = [Ht[9], Ht[7]]
        hidTs = [Ht[10], Ht[8]]
        for j, (l_, t, xb, last) in enumerate(items):
            norm_T(xb, 2, hbs[j][0], hbs[j][1], sm=small[0] if j == 0 else small[1])
        for pc in range(4):
            ups = [wload(s_up[l], s_up[l].ap[:, pc * 1024 + g * 512:pc * 1024 + (g + 1) * 512].rearrange("(k p) n -> p k n", p=128), [128, 8, 512])
                   for g in range(2)]
            for j in range(len(items)):
                hTb = hbs[j][1]
                for g in range(2):
                    bank = P[2 + 2 * j + g]
                    wb, wv = ups[g]
                    for kc in range(8):
                        E("pe", lambda e, kc=kc, bank=bank, wv=wv, hTb=hTb: e.matmul(bank.ap, lhsT=hTb.ap[:, kc * 128:(kc + 1) * 128],
                                                                                rhs=wv[:, kc, :], start=(kc == 0), stop=(kc == 7)),
                          r=[hTb, wb], w=[bank])
            dns = [wload(s_dn[l], s_dn[l].ap[pc * 1024 + h_ * 512:pc * 1024 + (h_ + 1) * 512, :].rearrange("(k p) n -> p k n", p=128), [128, 4, 1024])
                   for h_ in range(2)]
            for j in range(len(items)):
                rl, hid, hidT, acc = rls[j], hids[j], hidTs[j], accs[j]
                for g in range(2):
                    bank = P[2 + 2 * j + g]
                    E("act", lambda e, g=g, bank=bank, rl=rl: e.activation(out=rl.ap[:, g * 512:(g + 1) * 512], in_=bank.ap, func=AF.Relu), r=[bank], w=[rl])
                E("pool", lambda e, rl=rl, hid=hid: e.tensor_tensor(out=hid.ap, in0=rl.ap, in1=rl.ap, op=ALU.mult), r=[rl], w=[hid])
                transposes(hidT, lambda i, hidT=hidT: hidT.ap[:, i * 128:(i + 1) * 128], hid, lambda i, hid=hid: hid.ap[:, i * 128:(i + 1) * 128], 8,
                           whole=hidT.ap[:, 0:1024])
                for kc in range(8):
                    wb2, wv2 = dns[kc // 4]
                    for g in range(2):
                        E("pe", lambda e, kc=kc, g=g, wv2=wv2, hidT=hidT: e.matmul(P[g].ap, lhsT=hidT.ap[:, kc * 128:(kc + 1) * 128],
                                                                              rhs=wv2[:, kc % 4, g * 512:(g + 1) * 512], start=(kc == 0), stop=(kc == 7)),
                          r=[hidT, wb2], w=[P[g]])
                for g in range(2):
                    if pc == 0:
                        E("dve", lambda e, g=g, acc=acc: e.tensor_copy(out=acc.ap[:, g * 512:(g + 1) * 512], in_=P[g].ap), r=[P[g]], w=[acc])
                    else:
                        E("dve", lambda e, g=g, acc=acc: e.tensor_tensor(out=acc.ap[:, g * 512:(g + 1) * 512], in0=P[g].ap,
                                                                         in1=acc.ap[:, g * 512:(g + 1) * 512], op=ALU.add), r=[P[g], acc], w=[acc])
        for j, (l_, t, xb, last) in enumerate(items):
            post_add(xb, 3, src=accs[j])
            if last:
                k.dma("pool", y_o[t * 128:(t + 1) * 128, :], xb.ap, r=xb, final=True)
            else:
                k.dma("pool", x1_s.ap[t * 128:(t + 1) * 128, :], xb.ap, w=x1_s, r=xb)

    def flush_ffn():
        if pending:
            ffn_pair(list(pending))
            del pending[:]

    def out_and_ffn(l, t, xb, mixT, crossT, s_out, last):
        for half in range(3):
            wb, wv = wload(s_out, s_out.ap[half * 512:(half + 1) * 512, :].rearrange("(k p) n -> p k n", p=128), [128, 4, 1024])
            for kk in range(4):
                kc = half * 4 + kk
                src, so = (mixT, kc * 128) if kc < 8 else (crossT, (kc - 8) * 128)
                for g in range(2):
                    E("pe", lambda e, kc=kc, kk=kk, g=g, src=src, so=so, wv=wv: e.matmul(
                        P[g].ap, lhsT=src.ap[:, so:so + 128], rhs=wv[:, kk, g * 512:(g + 1) * 512],
                        start=(kc == 0), stop=(kc == 11)), r=[src, wb], w=[P[g]])
        post_add(xb, 1)
        prefetch_next()
        pending.append((l, t, xb, last))
        if len(pending) == 2:
            flush_ffn()

    def l0_kpart(t):
        xb = front(t, 0, 0)
        od = t % 2
        hb, hTb = (Ht[0], Ht[1]) if od == 0 else (Ht[4], Ht[5])
        norm_T(xb, 0, hb, hTb, sm=small[0] if od == 0 else small[1])
        k.dma("pool", hT_s.ap[t * 128:(t + 1) * 128, :], hTb.ap, w=hT_s, r=hTb)
        pb = P[od]
        proj(hTb, 8, s_in0, wcols(s_in0.ap, 384, 320), 320, pb, kstep=8)
        ckv = Ft[0] if od == 0 else Ft[3]
        E("dve", lambda e: e.tensor_copy(out=ckv.ap[:, 0:320], in_=pb.ap[:, 0:320]), r=[pb], w=[ckv])
        sm3 = small[3] if od == 0 else small[4]
        rs = rstd_of(ckv.ap[:, 0:256], ckv, 256, Ht[13], sm3)
        lat = Ft[1] if od == 0 else Ft[4]
        E("dve", lambda e: e.scalar_tensor_tensor(out=lat.ap[:, 0:256], in0=ckv.ap[:, 0:256], scalar=rs, in1=kvn_b.ap, op0=ALU.mult, op1=ALU.mult),
          r=[ckv, sm3, kvn_b], w=[lat])
        cb, sbf = cosr[t % 2], sinr[t % 2]
        k.dma("sp", cb.ap, cos_t[t * 128:(t + 1) * 128, :], w=cb)
        k.dma("sp", sbf.ap, sin_t[t * 128:(t + 1) * 128, :], w=sbf)
        kr = lat.ap[:, 256:320]
        tmp = Ft[2] if od == 0 else Ft[5]
        x1, x2 = ckv.ap[:, 256:288], ckv.ap[:, 288:320]
        c, s = cb.ap[:, 0:32], sbf.ap[:, 0:32]
        E("dve", lambda e: e.tensor_tensor(out=tmp.ap[:, 0:32], in0=x1, in1=c, op=ALU.mult), r=[ckv, cb], w=[tmp])
        E("dve", lambda e: e.tensor_tensor(out=tmp.ap[:, 32:64], in0=x2, in1=s, op=ALU.mult), r=[ckv, sbf], w=[tmp])
        E("dve", lambda e: e.tensor_tensor(out=tmp.ap[:, 64:96], in0=x2, in1=c, op=ALU.mult), r=[ckv, cb], w=[tmp])
        E("dve", lambda e: e.tensor_tensor(out=tmp.ap[:, 96:128], in0=x1, in1=s, op=ALU.mult), r=[ckv, sbf], w=[tmp])
        E("dve", lambda e: e.tensor_tensor(out=lat.ap[:, 256:288], in0=tmp.ap[:, 0:32], in1=tmp.ap[:, 32:64], op=ALU.subtract), r=[tmp], w=[lat])
        E("dve", lambda e: e.tensor_tensor(out=lat.ap[:, 288:320], in0=tmp.ap[:, 64:96], in1=tmp.ap[:, 96:128], op=ALU.add), r=[tmp], w=[lat])
        k.dma("pool", lat_o[t * 128:(t + 1) * 128, :], lat.ap[:, 0:256], r=lat, final=True)
        k.dma("pool", kr_o[t * 128:(t + 1) * 128, :], lat.ap[:, 256:320], r=lat, final=True)
        kb = Ht[2] if od == 0 else Ht[6]
        E("dve", lambda e: e.tensor_copy(out=kb.ap[:, 0:320], in_=lat.ap[:, 0:320]), r=[lat], w=[kb])
        kTb = Ht[3] if od == 0 else Ht[7]
        transposes(kTb, lambda i: kTb.ap[:, i * 128:(i + 1) * 128], kb, lambda i: kb.ap[:, i * 128:(i + 1) * 128], 2, whole=kTb.ap[:, 0:2 * 128])
        transposes(kTb, lambda i: kTb.ap[0:64, 256:384], kb, lambda i: kb.ap[:, 256:320], 1, rows=128, cols=64, whole=kTb.ap[0:64, 256:384])
        if t < NTP:
            k.dma("pool", v_own[t // 8].ap[(t % 8) * 128:(t % 8 + 1) * 128, :], kb.ap[:, 0:256], w=v_own[t // 8], r=kb)
            k.dma("pool", kT_own[0].ap[:, t * 128:(t + 1) * 128], kTb.ap[:, 0:128], w=kT_own[0], r=kTb)
            k.dma("pool", kT_own[1].ap[:, t * 128:(t + 1) * 128], kTb.ap[:, 128:256], w=kT_own[1], r=kTb)
            k.dma("pool", kT_own[2].ap[:, t * 128:(t + 1) * 128], kTb.ap[0:64, 256:384], w=kT_own[2], r=kTb)
        else:
            for hs in range(2):
                sq = (t - NTP) * 2 + hs
                k.dma("pool", v_s[sq].ap[1024:1088, :], kb.ap[hs * 64:(hs + 1) * 64, 0:256], w=v_s[sq], r=kb)
                k.dma("pool", kT_s[sq].ap[0:256, 1024:1088].rearrange("(j p) n -> p j n", p=128),
                      kTb.ap[:, 0:256].rearrange("p (j n) -> p j n", j=2)[:, :, hs * 64:(hs + 1) * 64], w=kT_s[sq], r=kTb)
                k.dma("pool", kT_s[sq].ap[256:320, 1024:1088], kTb.ap[0:64, 256 + hs * 64:256 + (hs + 1) * 64], w=kT_s[sq], r=kTb)

    def sample_cache_prep_v():
        for sq in range(4):
            k.dma("pool", v_s[sq].ap[0:1024, :], c_lat[sq], w=v_s[sq], r=wsem0)

    def sample_cache_iters():
        its = []
        for sq in range(4):
            for tt in range(8):
                def it(sq=sq, tt=tt):
                    kb = Ht[8] if tt % 2 == 0 else Ht[10]
                    k.dma("pool", kb.ap[:, 0:256], c_lat[sq, tt * 128:(tt + 1) * 128, :], w=kb)
                    k.dma("pool", kb.ap[:, 256:320], c_kr[sq, tt * 128:(tt + 1) * 128, :], w=kb)
                    kTb = Ht[9] if tt % 2 == 0 else Ht[11]
                    transposes(kTb, lambda i: kTb.ap[:, i * 128:(i + 1) * 128], kb, lambda i: kb.ap[:, i * 128:(i + 1) * 128], 2, whole=kTb.ap[:, 0:2 * 128])
                    transposes(kTb, lambda i: kTb.ap[0:64, 256:384], kb, lambda i: kb.ap[:, 256:320], 1, rows=128, cols=64, whole=kTb.ap[0:64, 256:384])
                    k.dma("pool", kT_s[sq].ap[0:256, tt * 128:(tt + 1) * 128].rearrange("(j p) n -> p j n", p=128),
                          kTb.ap[:, 0:256].rearrange("p (j n) -> p j n", j=2), w=kT_s[sq], r=kTb)
                    k.dma("pool", kT_s[sq].ap[256:320, tt * 128:(tt + 1) * 128], kTb.ap[0:64, 256:384], w=kT_s[sq], r=kTb)
                its.append(it)
        return its

    def attend(qlT, qrT, n, kT_src, kfn, vfn, L, mask_qc, osb, odst):
        nblk = (L + 127) // 128
        if mask_qc is not None:
            mb = mkr[mask_qc % 2]
            k.dma("sp", mb.ap, maskb[mask_qc], w=mb)
        pS = [P[2], P[3], P[4], P[5]]
        pTs = [Ht[7], Ht[8], Ht[9], Ht[10]]
        acc = Ft[1]
        blocks = []

        def load_group(g0):
            key0 = g0 * 128
            nkeys = min(512, L - key0)
            kt = ktr[kvi[0] % 3]
            vt = vtr[kvi[0] % 3]
            kvi[0] += 1
            for j in range(3):
                kr_ = 128 if j < 2 else 64
                k.dma("sp", kt.ap[0:kr_, j, 0:nkeys], kfn(j, key0, nkeys), w=kt, r=kT_src[j])
            nfull = nkeys // 128
            vb_, vap = vfn(key0, nkeys)
            if nfull > 0:
                k.dma("sp", vt.ap[:, 0:nfull, 0:256], vap[0:nfull * 128, :].rearrange("(b p) c -> p b c", p=128), w=vt, r=vb_)
            if nkeys % 128:
                rem = nkeys % 128
                k.dma("sp", vt.ap[0:rem, nfull, 0:256], vap[nfull * 128:nkeys, :], w=vt, r=vb_)
            return kt, vt

        cur = [None, None]

        def emit_st(blk):
            if blk % 4 == 0:
                cur[0], cur[1] = load_group(blk)
            kt, vt = cur
            b = blk % 4
            kn = min(128, L - blk * 128)
            ps = pS[blk % 4]
            for j in range(3):
                kr_ = 128
                rhs = qlT.ap[:, (j * 2 + n) * 512:(j * 2 + n + 1) * 512] if j < 2 else qrT.ap[:, n * 512:(n + 1) * 512]
                E("pe", lambda e, j=j, kr_=kr_, rhs=rhs, ps=ps, kt=kt, b=b, kn=kn: e.matmul(
                    ps.ap[0:kn, :], lhsT=kt.ap[0:kr_, j, b * 128:b * 128 + kn], rhs=rhs, start=(j == 0), stop=(j == 2)),
                  r=[kt, qlT, qrT], w=[ps])
            return (vt, b, kn)

        def emit_exp(blk, info):
            vt, b, kn = info
            ps = pS[blk % 4]
            pt = pTs[blk % 4]
            if mask_qc is not None:
                E("act", lambda e, ps=ps, pt=pt, kn=kn, blk=blk: e.activation(
                    out=pt.ap[0:kn, 0:512], in_=ps.ap[0:kn, :], func=AF.Exp, scale=192.0 ** -0.5, bias=mb.ap[0:kn, blk:blk + 1]),
                  r=[ps, mb], w=[pt])
            else:
                E("act", lambda e, ps=ps, pt=pt, kn=kn: e.activation(
                    out=pt.ap[0:kn, 0:512], in_=ps.ap[0:kn, :], func=AF.Exp, scale=192.0 ** -0.5), r=[ps], w=[pt])

        def emit_pv(blk, info):
            vt, b, kn = info
            pt = pTs[blk % 4]
            for cc in range(2):
                E("pe", lambda e, cc=cc, pt=pt, vt=vt, b=b, kn=kn, blk=blk: e.matmul(
                    P[cc].ap, lhsT=vt.ap[0:kn, b, cc * 128:(cc + 1) * 128], rhs=pt.ap[0:kn, 0:512],
                    start=(blk == 0), stop=(blk == nblk - 1)), r=[pt, vt], w=[P[cc]])
            if blk == 0:
                E("dve", lambda e, pt=pt: e.tensor_copy(out=acc.ap[:, 0:512], in_=pt.ap[:, 0:512]), r=[pt], w=[acc])
            else:
                E("dve", lambda e, pt=pt, kn=kn: e.tensor_tensor(out=acc.ap[0:kn, 0:512], in0=acc.ap[0:kn, 0:512], in1=pt.ap[0:kn, 0:512], op=ALU.add),
                  r=[pt, acc], w=[acc])

        units = [list(range(u, min(u + 2, nblk))) for u in range(0, nblk, 2)]
        infos = {b_: emit_st(b_) for b_ in units[0]}
        for ui, un in enumerate(units):
            if ui + 1 < len(units):
                for b_ in units[ui + 1]:
                    infos[b_] = emit_st(b_)
            for b_ in un:
                emit_exp(b_, infos[b_])
            for b_ in un:
                emit_pv(b_, infos.pop(b_))
        E("pe", lambda e: e.matmul(P[2].ap, lhsT=ones_f.ap, rhs=acc.ap[:, 0:512], start=True, stop=True), r=[ones_f, acc], w=[P[2]])
        E("dve", lambda e: e.reciprocal(out=acc.ap[:, 512:1024], in_=P[2].ap), r=[P[2]], w=[acc])
        for cc in range(2):
            E("dve", lambda e, cc=cc: e.tensor_tensor(out=odst[:, (cc * 2 + n) * 512:(cc * 2 + n + 1) * 512], in0=P[cc].ap,
                                                      in1=acc.ap[:, 512:1024], op=ALU.mult), r=[P[cc], acc], w=[osb])

    def l0_tile(t):
        xb = front(t, 0, 0)
        hTb = hpre[hissued.pop(t)]
        proj(hTb, 8, s_in0, wcols(s_in0.ap, 0, 384), 384, P[0], kstep=8)
        cq = Ft[0]
        E("dve", lambda e: e.tensor_copy(out=cq.ap[:, 0:384], in_=P[0].ap[:, 0:384]), r=[P[0]], w=[cq])
        rs = rstd_of(cq.ap[:, 0:384], cq, 384, Ht[13], small[3])
        cqb, cqT = Ht[2], Ht[3]
        E("dve", lambda e: e.scalar_tensor_tensor(out=cqb.ap[:, 0:384], in0=cq.ap[:, 0:384], scalar=rs, in1=qn_b.ap, op0=ALU.mult, op1=ALU.mult),
          r=[cq, small[3], qn_b], w=[cqb])
        transposes(cqT, lambda i: cqT.ap[:, i * 128:(i + 1) * 128], cqb, lambda i: cqb.ap[:, i * 128:(i + 1) * 128], 3, whole=cqT.ap[:, 0:3 * 128])
        wb, wv = wload(s_uq, s_uq.ap.rearrange("(k p) n -> p k n", p=128)[:, :, 0:768], [128, 3, 768])
        wb2, wv2 = wload(s_uq, s_uq.ap.rearrange("(k p) n -> p k n", p=128)[:, :, 768:1536], [128, 3, 768])
        for g in range(3):
            wbb, wvv, c0 = (wb, wv, g * 512) if g == 0 else ((wb2, wv2, g * 512 - 768) if g == 2 else (None, None, None))
            for kc in range(3):
                if g == 1:
                    E("pe", lambda e, kc=kc: e.matmul(P[2].ap[:, 0:256], lhsT=cqT.ap[:, kc * 128:(kc + 1) * 128], rhs=wv[:, kc, 512:768],
                                                     start=(kc == 0), stop=(kc == 2)), r=[cqT, wb], w=[P[2]])
                    E("pe", lambda e, kc=kc: e.matmul(P[3].ap[:, 0:256], lhsT=cqT.ap[:, kc * 128:(kc + 1) * 128], rhs=wv2[:, kc, 0:256],
                                                     start=(kc == 0), stop=(kc == 2)), r=[cqT, wb2], w=[P[3]])
                else:
                    bank = P[0] if g == 0 else P[1]
                    E("pe", lambda e, kc=kc, bank=bank, wvv=wvv, c0=c0: e.matmul(bank.ap, lhsT=cqT.ap[:, kc * 128:(kc + 1) * 128],
                                                                            rhs=wvv[:, kc, c0:c0 + 512], start=(kc == 0), stop=(kc == 2)),
                      r=[cqT, wbb], w=[bank])
        E("dve", lambda e: e.tensor_copy(out=qsb.ap[:, 0:512], in_=P[0].ap), r=[P[0]], w=[qsb])
        E("dve", lambda e: e.tensor_copy(out=qsb.ap[:, 512:768], in_=P[2].ap[:, 0:256]), r=[P[2]], w=[qsb])
        E("dve", lambda e: e.tensor_copy(out=qsb.ap[:, 768:1024], in_=P[3].ap[:, 0:256]), r=[P[3]], w=[qsb])
        E("dve", lambda e: e.tensor_copy(out=qsb.ap[:, 1024:1536], in_=P[1].ap), r=[P[1]], w=[qsb])
        q3 = qsb.ap.rearrange("p (h d) -> p h d", h=8)
        qn, qr = Ht[2], Ht[6]
        E("dve", lambda e: e.tensor_copy(out=qn.ap.rearrange("p (h d) -> p h d", h=8), in_=q3[:, :, 0:128]), r=[qsb], w=[qn])
        cb, sbf = cosr[t % 2], sinr[t % 2]
        k.dma("sp", cb.ap, cos_t[t * 128:(t + 1) * 128, :], w=cb)
        k.dma("sp", sbf.ap, sin_t[t * 128:(t + 1) * 128, :], w=sbf)
        c3 = cb.ap.rearrange("p (h d) -> p h d", h=8)
        s3 = sbf.ap.rearrange("p (h d) -> p h d", h=8)
        tmp = Ft[2]
        t4 = tmp.ap.rearrange("p (a h d) -> p a h d", a=4, h=8)
        x1, x2 = q3[:, :, 128:160], q3[:, :, 160:192]
        E("dve", lambda e: e.tensor_tensor(out=t4[:, 0], in0=x1, in1=c3, op=ALU.mult), r=[qsb, cb], w=[tmp])
        E("dve", lambda e: e.tensor_tensor(out=t4[:, 1], in0=x2, in1=s3, op=ALU.mult), r=[qsb, sbf], w=[tmp])
        E("dve", lambda e: e.tensor_tensor(out=t4[:, 2], in0=x2, in1=c3, op=ALU.mult), r=[qsb, cb], w=[tmp])
        E("dve", lambda e: e.tensor_tensor(out=t4[:, 3], in0=x1, in1=s3, op=ALU.mult), r=[qsb, sbf], w=[tmp])
        qr3 = qr.ap[:, 0:512].rearrange("p (h d) -> p h d", h=8)
        E("dve", lambda e: e.tensor_tensor(out=qr3[:, :, 0:32], in0=t4[:, 0], in1=t4[:, 1], op=ALU.subtract), r=[tmp], w=[qr])
        E("dve", lambda e: e.tensor_tensor(out=qr3[:, :, 32:64], in0=t4[:, 2], in1=t4[:, 3], op=ALU.add), r=[tmp], w=[qr])
        qnT, qrT0 = Ht[3], Ht[0]
        transposes(qnT, lambda i: qnT.ap[:, i * 128:(i + 1) * 128], qn, lambda i: qn.ap[:, i * 128:(i + 1) * 128], 8, whole=qnT.ap[:, 0:8 * 128])
        transposes(qrT0, lambda i: qrT0.ap[0:64, i * 128:(i + 1) * 128], qr, lambda i: qr.ap[:, i * 64:(i + 1) * 64], 8, rows=128, cols=64, whole=qrT0.ap[0:64, 0:1024])
        qrT = Ht[6]
        E("dve", lambda e: e.tensor_copy(out=qrT.ap[0:64, :].rearrange("p (n h q) -> p n h q", n=2, h=8),
                                         in_=qrT0.ap[0:64, :].rearrange("p (h n q) -> p n h q", h=8, n=2)), r=[qrT0], w=[qrT])
        E("dve", lambda e: e.memset(qrT.ap[64:128, :], 0.0), w=[qrT])
        wbk, wvk = wload(s_ukT, s_ukT.ap.rearrange("h d c -> d h c"), [128, 8, 256])
        qlT = Ft[3]
        qlT_b = Buf(qlT.ap.bitcast(BF16), None)
        qlT_b.lastw, qlT_b.reads = qlT.lastw, qlT.reads
        for cc in range(2):
            for h in range(8):
                bank = P[cc * 2 + h // 4]
                E("pe", lambda e, cc=cc, h=h, bank=bank: e.matmul(bank.ap[:, (h % 4) * 128:(h % 4 + 1) * 128],
                                                                   lhsT=wvk[:, h, cc * 128:(cc + 1) * 128], rhs=qnT.ap[:, h * 128:(h + 1) * 128],
                                                                   start=True, stop=True), r=[wbk, qnT], w=[bank])
            for hh in range(2):
                bank = P[cc * 2 + hh]
                dst = qlT.ap.bitcast(BF16)[:, cc * 1024:(cc + 1) * 1024].rearrange("p (n h q) -> p n h q", n=2, h=8)[:, :, hh * 4:(hh + 1) * 4, :]
                src = bank.ap.rearrange("p (h n q) -> p n h q", h=4, n=2)
                E("dve", lambda e, dst=dst, src=src: e.tensor_copy(out=dst, in_=src), r=[bank], w=[qlT])
        proj(hTb, 8, s_in0, wcols(s_in0.ap, 704, 512), 512, P[1], kstep=8)
        xqb, xqT = Ht[4], Ht[5]
        E("dve", lambda e: e.tensor_copy(out=xqb.ap[:, 0:512], in_=P[1].ap), r=[P[1]], w=[xqb])
        transposes(xqT, lambda i: xqT.ap[:, i * 128:(i + 1) * 128], xqb, lambda i: xqb.ap[:, i * 128:(i + 1) * 128], 4, whole=xqT.ap[:, 0:4 * 128])
        olT = Ft[4]
        olT_bf = olT.ap.bitcast(BF16)
        qlT_v = Buf(qlT.ap.bitcast(BF16))
        for n in range(2):
            osb = olT
            qlT_v.lastw, qlT_v.reads = qlT.lastw, qlT.reads
            if t < NTP:
                def kfn_p(j, k0, nk):
                    r_ = kT_rows[j]
                    gidx = k0 // 512
                    ip, rk = gidx // 8, gidx % 8
                    off = ip * 512 + (k0 % 512)
                    return kT_all[j].ap[rk * r_:(rk + 1) * r_, off:off + nk]

                def vfn_p(k0, nk):
                    gidx = k0 // 512
                    ip, rk = gidx // 8, gidx % 8
                    loc = ip * 512 + (k0 % 512)
                    p_, i_ = loc // 1024, loc % 1024
                    return v_all[p_], v_all[p_].ap[rk * 1024 + i_:rk * 1024 + i_ + nk, :]
                attend(qlT_v, qrT, n, kT_all, kfn_p, vfn_p, 4096 * (t // 4 + 1), t * 2 + n, osb, olT_bf)
            else:
                sq = (t - NTP) * 2 + n
                attend(qlT_v, qrT, n, [kT_s[sq]] * 3,
                       lambda j, k0, nk, sq=sq: kT_s[sq].ap[j * 128:j * 128 + (128 if j < 2 else 64), k0:k0 + nk],
                       lambda k0, nk, sq=sq: (v_s[sq], v_s[sq].ap[k0:k0 + nk, :]), 1088, None, osb, olT_bf)
            qlT.reads = qlT_v.reads
        wbv, wvv = wload(s_uv, s_uv.ap.rearrange("(k p) n -> p k n", p=128), [128, 2, 1024])
        mixT = Ht[2]
        for h in range(8):
            bank = P[h // 4]
            for cc in range(2):
                rhs = olT_bf[:, cc * 1024:(cc + 1) * 1024].rearrange("p (n h q) -> p n h q", n=2, h=8)[:, :, h, :]
                E("pe", lambda e, h=h, cc=cc, bank=bank, rhs=rhs: e.matmul(
                    bank.ap[:, (h % 4) * 128:(h % 4 + 1) * 128].rearrange("p (n q) -> p n q", n=2),
                    lhsT=wvv[:, cc, h * 128:(h + 1) * 128], rhs=rhs, start=(cc == 0), stop=(cc == 1)), r=[wbv, olT], w=[bank])
        for hh in range(2):
            E("dve", lambda e, hh=hh: e.tensor_copy(out=mixT.ap[:, hh * 512:(hh + 1) * 512], in_=P[hh].ap), r=[P[hh]], w=[mixT])
        crossT = Ht[3]
        if t < NTP:
            segs = [(0, 128, 0)]
        else:
            for hs in range(2):
                sq = (t - NTP) * 2 + hs
                build_memset(hs, c_mk[0, sq], c_mv[0, sq], wsem)
            segs = [(0, 64, 0), (64, 64, 1)]
        cross_attn(xqT, segs, crossT)
        out_and_ffn(0, t, xb, mixT, crossT, s_out0, last=False)

    Dm = hc.ap[:, 0:128]
    maskT = hc.ap[:, 128:256]
    Ind = hc.ap[:, 256:260]

    def l1_tail(t, xb, osb_, gate, xqb):
        osq = Ft[1]
        E("dve", lambda e: e.tensor_tensor(out=osq.ap, in0=osb_.ap, in1=osb_.ap, op=ALU.mult), r=[osb_], w=[osq])
        r8 = small[3]
        E("dve", lambda e: e.tensor_reduce(out=r8.ap[:, 0:8], in_=osq.ap.rearrange("p (h v) -> p h v", h=8), axis=AX.X, op=ALU.add), r=[osq], w=[r8])
        E("act", lambda e: e.activation(out=r8.ap[:, 0:8], in_=r8.ap[:, 0:8], func=AF.Ln, bias=epsb.ap, scale=1.0 / 128), r=[r8, epsb], w=[r8])
        E("act", lambda e: e.activation(out=r8.ap[:, 0:8], in_=r8.ap[:, 0:8], func=AF.Exp, scale=-0.5), r=[r8], w=[r8])
        E("dve", lambda e: e.tensor_tensor(out=osb_.ap.rearrange("p (h v) -> p h v", h=8), in0=osb_.ap.rearrange("p (h v) -> p h v", h=8),
                                           in1=r8.ap[:, 0:8].unsqueeze(2).to_broadcast([128, 8, 128]), op=ALU.mult), r=[osb_, r8], w=[osb_])
        E("dve", lambda e: e.tensor_tensor(out=osb_.ap, in0=osb_.ap, in1=on_b.ap, op=ALU.mult), r=[osb_, on_b], w=[osb_])
        mixb, mixT = Ht[5], Ht[2]
        E("dve", lambda e: e.tensor_tensor(out=mixb.ap, in0=osb_.ap, in1=gate.ap, op=ALU.mult), r=[osb_, gate], w=[mixb])
        transposes(mixT, lambda i: mixT.ap[:, i * 128:(i + 1) * 128], mixb, lambda i: mixb.ap[:, i * 128:(i + 1) * 128], 8, whole=mixT.ap[:, 0:8 * 128])
        xqT, crossT = Ht[6], Ht[3]
        transposes(xqT, lambda i: xqT.ap[:, i * 128:(i + 1) * 128], xqb, lambda i: xqb.ap[:, i * 128:(i + 1) * 128], 4, whole=xqT.ap[:, 0:4 * 128])
        if t < NTP:
            segs = [(0, 128, 0)]
        else:
            for hs in range(2):
                sq = (t - NTP) * 2 + hs
                build_memset(hs, c_mk[1, sq], c_mv[1, sq], wsem)
            segs = [(0, 64, 0), (64, 64, 1)]
        cross_attn(xqT, segs, crossT)
        out_and_ffn(1, t, xb, mixT, crossT, s_out1, last=True)

    def l1_back(t):
        xb = front(t, 1, 0)
        osb_, gate, qcT, xqb, Sr = Ft[0], Ft[2], Ht[7], Ht[4], Ht[8]
        rr = slice(t * 128, (t + 1) * 128)
        k.dma("sp", osb_.ap, st_o.ap[rr, :], w=osb_, r=st_o)
        k.dma("sp", gate.ap, st_g.ap[rr, :], w=gate, r=st_g)
        k.dma("sp", qcT.ap, st_q.ap[rr, :], w=qcT, r=st_q)
        k.dma("sp", xqb.ap[:, 0:512], st_x.ap[rr, :], w=xqb, r=st_x)
        Sin = Ft[3]
        if t % 4 == 0:
            k.dma("sp", Sin.ap, snapd[t // 4].ap, w=Sin, r=snapd[t // 4])
        E("dve", lambda e: e.tensor_copy(out=Sr.ap, in_=Sin.ap), r=[Sin], w=[Sr])
        for h in range(8):
            bank = P[h // 4]
            E("pe", lambda e, h=h, bank=bank: e.matmul(bank.ap[:, (h % 4) * 128:(h % 4 + 1) * 128], lhsT=qcT.ap[:, h * 128:(h + 1) * 128],
                                                       rhs=Sr.ap[:, h * 128:(h + 1) * 128], start=True, stop=True), r=[qcT, Sr], w=[bank])
        for hh in range(2):
            E("dve", lambda e, hh=hh: e.tensor_tensor(out=osb_.ap[:, hh * 512:(hh + 1) * 512], in0=P[hh].ap, in1=osb_.ap[:, hh * 512:(hh + 1) * 512],
                                                      op=ALU.add), r=[P[hh], osb_], w=[osb_])
        l1_tail(t, xb, osb_, gate, xqb)

    def l1_tile(t, mode):
        passA = False
        frontm = (mode == "front")
        xb = front(t, 1, 0)
        if frontm:
            prefetch_next()
        hb, hTb = Ht[0], Ht[1]
        norm_T(xb, 0, hb, hTb)
        qs, sg, gate, logf, kk = Ft[0], Ft[1], Ft[2], Ft[3], Ft[4]
        vb, xqb = Ht[2], Ht[4]
        def do_groups(groups):
          for gi in groups:
              bank = P[gi % 4]
              proj(hTb, 8, s_in1, wcols(s_in1.ap, gi * 512, 512), 512, bank, kstep=8)
              c0 = (gi % 2) * 512
              if gi < 2:
                  E("act", lambda e, bank=bank, c0=c0: e.activation(out=qs.ap[:, c0:c0 + 512], in_=bank.ap, func=AF.Silu), r=[bank], w=[qs])
              elif gi < 4:
                  E("act", lambda e, bank=bank, c0=c0: e.activation(out=sg.ap[:, c0:c0 + 512], in_=bank.ap, func=AF.Sigmoid), r=[bank], w=[sg])
              elif gi < 6:
                  E("dve", lambda e, bank=bank, c0=c0: e.tensor_copy(out=vb.ap[:, c0:c0 + 512], in_=bank.ap), r=[bank], w=[vb])
              elif gi < 8:
                  E("act", lambda e, bank=bank, c0=c0: e.activation(out=gate.ap[:, c0:c0 + 512], in_=bank.ap, func=AF.Silu), r=[bank], w=[gate])
              else:
                  E("dve", lambda e, bank=bank: e.tensor_copy(out=xqb.ap[:, 0:512], in_=bank.ap), r=[bank], w=[xqb])
        do_groups([2, 3, 4, 5, 0, 1])
        E("dve", lambda e: e.tensor_tensor(out=sg.ap, in0=sg.ap, in1=oml.ap, op=ALU.mult), r=[sg, oml], w=[sg])
        E("dve", lambda e: e.tensor_tensor(out=logf.ap, in0=sg.ap, in1=lbb.ap, op=ALU.add), r=[sg, lbb], w=[logf])
        E("act", lambda e: e.activation(out=logf.ap, in_=logf.ap, func=AF.Ln), r=[logf], w=[logf])
        E("dve", lambda e: e.tensor_tensor(out=kk.ap, in0=oml.ap, in1=sg.ap, op=ALU.subtract), r=[sg, oml], w=[kk])
        E1 = Ft[5]
        ke, qe = Ht[5], Ht[6]
        for g in range(2):
            E("pe", lambda e, g=g: e.matmul(P[4 + g].ap, lhsT=Dm, rhs=logf.ap[:, g * 512:(g + 1) * 512], start=True, stop=True),
              r=[hc, logf], w=[P[4 + g]])
        for g in range(2):
            E("act", lambda e, g=g: e.activation(out=E1.ap[:, g * 512:(g + 1) * 512], in_=P[4 + g].ap, func=AF.Exp, scale=-1.0), r=[P[4 + g]], w=[E1])
        E("dve", lambda e: e.tensor_tensor(out=ke.ap, in0=kk.ap, in1=E1.ap, op=ALU.mult), r=[kk, E1], w=[ke])
        if not passA:
            for g in range(2):
                E("act", lambda e, g=g: e.activation(out=E1.ap[:, g * 512:(g + 1) * 512], in_=P[4 + g].ap, func=AF.Exp, scale=1.0), r=[P[4 + g]], w=[E1])
            E("dve", lambda e: e.tensor_tensor(out=qe.ap, in0=qs.ap, in1=E1.ap, op=ALU.mult), r=[qs, E1], w=[qe])
        st = P[4]
        for h in range(8):
            E("pe", lambda e, h=h: e.matmul(st.ap[:, h * 4:(h + 1) * 4], lhsT=logf.ap[:, h * 128:(h + 1) * 128], rhs=Ind, start=True, stop=True),
              r=[logf, hc], w=[st])
        cs = small[5]
        er = small[6]
        el = small[7]
        E("dve", lambda e: e.tensor_copy(out=stsb.ap, in_=st.ap[:, 0:32]), r=[st], w=[stsb])
        st3 = stsb.ap.rearrange("p (h f) -> p h f", h=8)
        E("act", lambda e: e.activation(out=cs.ap.rearrange("p (h n) -> p h n", h=8), in_=st3[:, :, 0:2], func=AF.Exp), r=[stsb], w=[cs])
        E("act", lambda e: e.activation(out=er.ap.rearrange("p (h n) -> p h n", h=8), in_=st3[:, :, 2:4], func=AF.Exp), r=[stsb], w=[er])
        E("dve", lambda e: e.tensor_tensor(out=el.ap.rearrange("p (h n) -> p h n", h=8), in0=st3[:, :, 0:2], in1=st3[:, :, 2:4], op=ALU.subtract),
          r=[stsb], w=[el])
        E("act", lambda e: e.activation(out=el.ap, in_=el.ap, func=AF.Exp), r=[el], w=[el])
        do_groups([6, 7, 8])
        if not passA:
            qeT, keT = Ht[7], Ht[8]
            transposes(qeT, lambda i: qeT.ap[:, i * 128:(i + 1) * 128], qe, lambda i: qe.ap[:, i * 128:(i + 1) * 128], 8, whole=qeT.ap[:, 0:8 * 128])
            transposes(keT, lambda i: keT.ap[:, i * 128:(i + 1) * 128], ke, lambda i: ke.ap[:, i * 128:(i + 1) * 128], 8, whole=keT.ap[:, 0:8 * 128])
            aTm = Ht[9]
            for h in range(8):
                bank = P[4 + h // 4]
                E("pe", lambda e, h=h, bank=bank: e.matmul(bank.ap[:, (h % 4) * 128:(h % 4 + 1) * 128], lhsT=keT.ap[:, h * 128:(h + 1) * 128],
                                                           rhs=qeT.ap[:, h * 128:(h + 1) * 128], start=True, stop=True), r=[keT, qeT], w=[bank])
            for hh in range(2):
                bank = P[4 + hh]
                E("dve", lambda e, hh=hh, bank=bank: e.tensor_tensor(out=aTm.ap[:, hh * 512:(hh + 1) * 512].rearrange("p (h t) -> p h t", h=4),
                                                                     in0=bank.ap.rearrange("p (h t) -> p h t", h=4),
                                                                     in1=maskT.unsqueeze(1).to_broadcast([128, 4, 128]), op=ALU.mult), r=[bank, hc], w=[aTm])
        S3 = Sst.ap.rearrange("p (h v) -> p h v", h=8)
        coef = small[4]
        if frontm and t % 4 == 0:
            E("dve", lambda e: e.memset(Sst.ap, 0.0), w=[Sst])
            E("dve", lambda e: e.memset(Dacc.ap, 1.0), w=[Dacc])
        for n in range(2):
            sq = (t - NTP) * 2 + n
            if t >= NTP:
                k.dma("sp", S3, c_st[sq].rearrange("h k v -> k h v"), w=Sst)
            r0 = n * 64
            for h in range(8):
                bank = P[2 + h // 4]
                E("pe", lambda e, h=h, bank=bank, r0=r0: e.matmul(bank.ap[:, (h % 4) * 128:(h % 4 + 1) * 128], lhsT=ke.ap[r0:r0 + 64, h * 128:(h + 1) * 128],
                                                                  rhs=vb.ap[r0:r0 + 64, h * 128:(h + 1) * 128], start=True, stop=True), r=[ke, vb], w=[bank])
            if not passA:
                Sr = Ht[10 + n]
                E("dve", lambda e, n=n, Sr=Sr: e.tensor_tensor(out=Sr.ap.rearrange("p (h v) -> p h v", h=8), in0=S3,
                                                                in1=er.ap.rearrange("p (h n) -> p h n", h=8)[:, :, n:n + 1].to_broadcast([128, 8, 128]),
                                                                op=ALU.mult), r=[Sst, er], w=[Sr])
                for h in range(8):
                    bank = P[h // 4]
                    oo = bank.ap[r0:r0 + 64, (h % 4) * 128:(h % 4 + 1) * 128]
                    E("pe", lambda e, h=h, oo=oo, r0=r0: e.matmul(oo, lhsT=aTm.ap[:, h * 128 + r0:h * 128 + r0 + 64], rhs=vb.ap[:, h * 128:(h + 1) * 128],
                                                                  start=True, stop=False), r=[aTm, vb], w=[bank])
                    E("pe", lambda e, h=h, oo=oo, r0=r0, Sr=Sr: e.matmul(oo, lhsT=qeT.ap[:, h * 128 + r0:h * 128 + r0 + 64], rhs=Sr.ap[:, h * 128:(h + 1) * 128],
                                                                         start=False, stop=True), r=[qeT, Sr], w=[bank])
            if frontm:
                E("dve", lambda e, n=n: e.tensor_tensor(out=coef.ap.rearrange("p (h n) -> p h n", h=8)[:, :, n],
                                                        in0=er.ap.rearrange("p (h n) -> p h n", h=8)[:, :, n], in1=Dacc.ap, op=ALU.mult),
                  r=[er, Dacc], w=[coef])
            t1 = Ft[5]
            for hh in range(2):
                E("dve", lambda e, hh=hh, n=n: e.tensor_tensor(
                    out=t1.ap[:, hh * 512:(hh + 1) * 512].rearrange("p (h v) -> p h v", h=4), in0=P[2 + hh].ap.rearrange("p (h v) -> p h v", h=4),
                    in1=el.ap.rearrange("p (h n) -> p h n", h=8)[:, hh * 4:(hh + 1) * 4, n:n + 1].to_broadcast([128, 4, 128]), op=ALU.mult),
                  r=[P[2 + hh], el], w=[t1])
            E("dve", lambda e, n=n: e.tensor_tensor(out=S3, in0=S3, in1=cs.ap.rearrange("p (h n) -> p h n", h=8)[:, :, n:n + 1].to_broadcast([128, 8, 128]),
                                                    op=ALU.mult), r=[Sst, cs], w=[Sst])
            E("dve", lambda e: e.tensor_tensor(out=Sst.ap, in0=Sst.ap, in1=t1.ap, op=ALU.add), r=[Sst, t1], w=[Sst])
            if frontm:
                E("dve", lambda e, n=n: e.tensor_tensor(out=Dacc.ap, in0=Dacc.ap, in1=cs.ap.rearrange("p (h n) -> p h n", h=8)[:, :, n], op=ALU.mult),
                  r=[Dacc, cs], w=[Dacc])
            if t >= NTP:
                k.dma("pool", sts_o[sq].rearrange("h k v -> k h v"), S3, r=Sst, final=True)
        osb_ = Ft[0]
        for hh in range(2):
            E("dve", lambda e, hh=hh: e.tensor_copy(out=osb_.ap[:, hh * 512:(hh + 1) * 512], in_=P[hh].ap), r=[P[hh]], w=[osb_])
        if frontm:
            if t % 4 == 3:
                sg_ = t // 4
                E("dve", lambda e: e.tensor_copy(out=exD_sb.ap[:, sg_ * 8:(sg_ + 1) * 8], in_=Dacc.ap), r=[Dacc], w=[exD_sb])
                k.dma("pool", exU_own[sg_].ap, Sst.ap, w=exU_own[sg_], r=Sst)
            qcT = Ht[3]
            E("dve", lambda e: e.tensor_tensor(out=qcT.ap.rearrange("p (h n q) -> p h n q", h=8, n=2),
                                               in0=qeT.ap.rearrange("p (h n q) -> p h n q", h=8, n=2),
                                               in1=coef.ap.rearrange("p (h n) -> p h n", h=8).unsqueeze(3).to_broadcast([128, 8, 2, 64]), op=ALU.mult),
              r=[qeT, coef], w=[qcT])
            rr = slice(t * 128, (t + 1) * 128)
            k.dma("pool", st_q.ap[rr, :], qcT.ap, w=st_q, r=qcT)
            k.dma("pool", st_o.ap[rr, :], osb_.ap, w=st_o, r=osb_)
            k.dma("pool", st_g.ap[rr, :], gate.ap, w=st_g, r=gate)
            k.dma("pool", st_x.ap[rr, :], xqb.ap[:, 0:512], w=st_x, r=xqb)
            return
        l1_tail(t, xb, osb_, gate, xqb)

    sample_cache_prep_v()
    cast_rest()
    phase0()
    load_gains(0)
    cits = sample_cache_iters()
    g1 = [lambda j=j: k.allgather(G4, kT_own[j], kT_g4[j]) for j in range(3)] + \
         [lambda j=j: k.allgather(G4, v_own[j], v_g4[j]) for j in range(2)]
    g2 = [lambda j=j: k.allgather(G2, kT_g4[j], kT_all[j]) for j in range(3)] + \
         [lambda j=j: k.allgather(G2, v_g4[j], v_all[j]) for j in range(2)]
    for t in range(NT):
        bg_run(5)
        l0_kpart(t)
        for _ in range(2):
            if cits:
                cits.pop(0)()
        if t >= NTP - 1 and g1:
            g1.pop(0)()
    ci = 0
    while cits:
        cits.pop(0)()
        ci += 1
        if ci % 6 == 0 and g1:
            g1.pop(0)()
    bg_run(len(bg))
    while g1:
        g1.pop(0)()
    cast_layer1()
    want_hT[0] = True
    order0 = list(range(NTP, NT)) + list(range(NTP))
    nxt0 = {order0[i]: ((order0[i + 1], 0) if i + 1 < len(order0) else (0, 1)) for i in range(len(order0))}
    nexttile[0] = nxt0[NTP]
    l0_tile(NTP)
    g2.pop(0)()
    g2.pop(0)()
    nexttile[0] = nxt0[NTP + 1]
    l0_tile(NTP + 1)
    while g2:
        g2.pop(0)()
    build_memset(0, memkv_s.ap[0, :, 0:512], memkv_s.ap[0, :, 512:1024], memkv_s)
    for t in range(NTP):
        nexttile[0] = nxt0[t]
        bg_run(4)
        l0_tile(t)
    bg_run(len(bg))
    want_hT[0] = False
    flush_ffn()
    load_gains(1)
    for t in range(NTP):
        nexttile[0] = (t + 1, 1)
        l1_tile(t, "front")
        if t % 4 == 3:
            k.allgather(G4, exU_own[t // 4], exU_g4[t // 4])
        if t % 4 == 1 and t >= 5:
            k.allgather(G2, exU_g4[t // 4 - 1], exU_all[t // 4 - 1])
    k.dma("pool", exD_own.ap, exD_sb.ap, w=exD_own, r=exD_sb)
    k.allgather(G4, exD_own, exD_g4)
    nexttile[0] = (NTP + 1, 1)
    l1_tile(NTP, "single")
    k.allgather(G2, exU_g4[3], exU_all[3])
    k.allgather(G2, exD_g4, exD_all)
    nexttile[0] = (0, 1)
    l1_tile(NTP + 1, "single")
    k.dma("sp", exDa_sb.ap, exD_all.ap.rearrange("(r p) f -> p r f", p=128), w=exDa_sb, r=exD_all)
    E("dve", lambda e: e.memset(Sst.ap, 0.0), w=[Sst])
    S3 = Sst.ap.rearrange("p (h v) -> p h v", h=8)
    snap = Ft[0]
    build_memset(0, memkv_s.ap[1, :, 0:512], memkv_s.ap[1, :, 512:1024], memkv_s)
    for sg_ in range(4):
        E("dve", lambda e: e.memset(snap.ap, 0.0), w=[snap])
        for r in range(8):
            eb = exb[r % 2]
            k.dma("sp", eb.ap[:, 0:1024], exU_all[sg_].ap[r * 128:(r + 1) * 128, :], w=eb, r=exU_all[sg_])
            m = scm.ap[:, r:r + 1]
            E("dve", lambda e, m=m: e.scalar_tensor_tensor(out=snap.ap, in0=Sst.ap, scalar=m, in1=snap.ap, op0=ALU.mult, op1=ALU.add),
              r=[Sst, scm, snap], w=[snap])
            E("dve", lambda e, r=r, sg_=sg_: e.tensor_tensor(out=S3, in0=S3,
                                                            in1=exDa_sb.ap[:, r, sg_ * 8:(sg_ + 1) * 8].unsqueeze(2).to_broadcast([128, 8, 128]), op=ALU.mult),
              r=[Sst, exDa_sb], w=[Sst])
            E("dve", lambda e, eb=eb: e.tensor_tensor(out=Sst.ap, in0=Sst.ap, in1=eb.ap[:, 0:1024], op=ALU.add), r=[Sst, eb], w=[Sst])
        k.dma("pool", snapd[sg_].ap, snap.ap, w=snapd[sg_], r=snap)
        if sg_ == 3:
            k.dma("pool", stp_o.rearrange("h k v -> k h v"), S3, r=Sst, final=True)
        for t in range(4 * sg_, 4 * sg_ + 4):
            nexttile[0] = (t + 1, 1) if t + 1 < NTP else None
            l1_back(t)
    flush_ffn()
    k.finish()
    return nc


_ROPE_THETA = 10000.0


def _tok_base(c, t):
    return 512 * (8 * (t // 4) + c) + 128 * (t % 4)


def _tables(c):
    inv = np.power(np.float32(_ROPE_THETA), -np.arange(32, dtype=np.float32) / np.float32(32)).astype(np.float32)
    pos_p = np.concatenate([_tok_base(c, t) + np.arange(128) for t in range(NTP)]).astype(np.float32)
    pos_s = (1024 + np.arange(64)).astype(np.float32)
    pos = np.concatenate([pos_p] + [pos_s] * 4)
    ang = (pos[:, None] * inv[None, :]).astype(np.float32)
    cos = np.tile(np.cos(ang).astype(np.float32), (1, 8))
    sin = np.tile(np.sin(ang).astype(np.float32), (1, 8))
    qc = np.arange(32)[:, None, None]
    p = np.arange(128)[None, :, None]
    vb = np.arange(128)[None, None, :]
    gq = (8 * (qc // 8) + c) * 8 + qc % 8
    grp = vb // 4
    kchunk = (8 * (grp // 8) + grp % 8) * 8 + 2 * (vb % 4) + p // 64
    vis = kchunk <= gq
    maskb = np.where(vis, 0.0, NEGB).astype(np.float32)
    scanm = np.tile((np.arange(8) == c).astype(np.float32)[None, :], (128, 1))
    s = np.arange(128)[:, None]
    t = np.arange(128)[None, :]
    same = (s // 64) == (t // 64)
    tri = (same & (s <= t)).astype(np.float32)
    triref = (same & ((s % 64) <= 32)).astype(np.float32)
    Dm = tri - triref
    ind = np.zeros((128, 4), np.float32)
    sl = np.arange(128)
    ind[:, 0] = (sl < 64)
    ind[:, 1] = (sl >= 64)
    ind[:, 2] = (sl < 64) & (sl % 64 <= 32)
    ind[:, 3] = (sl >= 64) & (sl % 64 <= 32)
    hconst = np.concatenate([Dm, tri, ind], axis=1).astype(np.float32)
    return cos, sin, maskb, scanm, hconst


def kernel(x_prompt, x_sample, cache_mla_latent, cache_mla_krope, cache_hgrn_state,
           cache_mem_k, cache_mem_v, mem_prompt,
           ln_mix_pre, ln_mix_post, ln_ffn_pre, ln_ffn_post, mem_norm, w_mem_kv,
           mla_w_in, mla_q_norm, mla_kv_norm, mla_w_uq, mla_w_uk, mla_w_uv, mla_w_out,
           hgrn_w_in, hgrn_lb, hgrn_o_norm, hgrn_w_out, w_ffn_up, w_ffn_down):
    f = lambda a: np.ascontiguousarray(np.asarray(a, dtype=np.float32))
    x_prompt, x_sample = f(x_prompt), f(x_sample)
    gains = np.stack([f(ln_mix_pre)[0], f(ln_mix_pre)[1], f(ln_mix_post)[0], f(ln_mix_post)[1],
                      f(ln_ffn_pre)[0], f(ln_ffn_pre)[1], f(ln_ffn_post)[0], f(ln_ffn_post)[1],
                      f(mem_norm)[0], f(mem_norm)[1]], axis=0)
    shared = {
        "memp": f(mem_prompt)[0], "gains": f(gains), "qn_g": f(mla_q_norm)[0], "kvn_g": f(mla_kv_norm)[0],
        "on_g": f(np.tile(f(hgrn_o_norm)[0], 8)), "hlb": f(hgrn_lb), "w_memkv": f(w_mem_kv), "w_in0": f(mla_w_in)[0],
        "w_uq": f(mla_w_uq)[0], "w_ukT": f(np.transpose(f(mla_w_uk)[0], (1, 2, 0))),
        "w_uv": f(f(mla_w_uv)[0].reshape(256, 1024)), "w_out0": f(mla_w_out)[0], "w_in1": f(hgrn_w_in)[0],
        "w_out1": f(hgrn_w_out)[0], "w_up": f(w_ffn_up), "w_dn": f(w_ffn_down),
    }
    in_maps = []
    for c in range(NCORE):
        cos, sin, maskb, scanm, hconst = _tables(c)
        sl = slice(4 * c, 4 * c + 4)
        m = dict(shared)
        m["xin"] = f(np.concatenate([x_prompt[0, _tok_base(c, t):_tok_base(c, t) + 128] for t in range(NTP)]
                                    + [x_sample[sl].reshape(256, D)], axis=0))
        m["c_lat"] = f(np.asarray(cache_mla_latent)[0, sl])
        m["c_kr"] = f(np.asarray(cache_mla_krope)[0, sl])
        m["c_st"] = f(np.asarray(cache_hgrn_state)[0, sl])
        m["c_mk"] = f(np.asarray(cache_mem_k)[:, sl].reshape(2, 4, 256, 512))
        m["c_mv"] = f(np.asarray(cache_mem_v)[:, sl].reshape(2, 4, 256, 512))
        m["cos_t"], m["sin_t"], m["maskb"], m["scanm"], m["hconst"] = cos, sin, maskb, scanm, hconst
        in_maps.append(m)
    nc = build()
    res = run_bass_kernel_spmd(nc, in_maps, core_ids=list(range(NCORE)))
    R = res.results
    y_p = np.zeros((1, 16384, D), np.float32)
    lat_p = np.zeros((1, 1, 16384, 256), np.float32)
    kr_p = np.zeros((1, 1, 16384, 64), np.float32)
    for c in range(NCORE):
        for t in range(NTP):
            b0 = _tok_base(c, t)
            y_p[0, b0:b0 + 128] = R[c]["y_o"][t * 128:(t + 1) * 128]
            lat_p[0, 0, b0:b0 + 128] = R[c]["lat_o"][t * 128:(t + 1) * 128]
            kr_p[0, 0, b0:b0 + 128] = R[c]["kr_o"][t * 128:(t + 1) * 128]
    y_s = np.concatenate([R[c]["y_o"][2048:2304].reshape(4, 64, D) for c in range(NCORE)], 0)
    st_p = R[NCORE - 1]["stp_o"][None, None]
    mk_p = R[0]["mk_o"].reshape(2, 1, 256, 4, 128)
    mv_p = R[0]["mv_o"].reshape(2, 1, 256, 4, 128)
    lat_s = np.concatenate([R[c]["lat_o"][2048:2304].reshape(4, 64, 256) for c in range(NCORE)], 0)[None]
    kr_s = np.concatenate([R[c]["kr_o"][2048:2304].reshape(4, 64, 64) for c in range(NCORE)], 0)[None]
    st_s = np.concatenate([R[c]["sts_o"] for c in range(NCORE)], 0)[None]
    o = lambda a: np.ascontiguousarray(a, dtype=np.float32)
    return (o(y_p), o(y_s), o(lat_p), o(kr_p), o(st_p), o(mk_p), o(mv_p), o(lat_s), o(kr_s), o(st_s))
```
